# Optimizing a Trainium2 kernel written in Bass

```python
import math
import jax, jax.numpy as jnp
from jax import lax
import numpy as np

D_MODEL = 1024
BATCH = 8
SEQ = 4096
DEPTH = 1

PLE_DIM = 256
MLA_HEADS = 8
QK_NOPE_DIM = 64
QK_ROPE_DIM = 32
V_HEAD_DIM = 64
Q_LORA_RANK = 384
KV_LORA_RANK = 256
ROPE_THETA = 10000.0
Q_BLOCK = 128
SSM_WIDTH = D_MODEL // 2
SSM_GROUP = 16
SSM_GROUPS = SSM_WIDTH // SSM_GROUP
SSM_STATE = 64
DT_MIN = 1e-3
DT_MAX = 1e-1
D_FF = 4 * D_MODEL
LN_EPS = 1e-5
RMS_EPS = 1e-6
DEEPNORM_ALPHA = (2.0 * DEPTH) ** 0.25
DEEPNORM_BETA = (8.0 * DEPTH) ** -0.25
IN_SPLITS = (Q_LORA_RANK, KV_LORA_RANK, QK_ROPE_DIM, SSM_WIDTH, D_MODEL, D_MODEL)
IN_WIDTH = sum(IN_SPLITS)

kernel_name = "hybrid_mla_s5_gated_deepnorm_block"


def layer_norm(x, g, b):
    xf = x.astype(jnp.float32)
    mu = jnp.mean(xf, axis=-1, keepdims=True)
    xc = xf - mu
    var = jnp.mean(xc * xc, axis=-1, keepdims=True)
    y = xc * lax.rsqrt(var + LN_EPS) * g.astype(jnp.float32) + b.astype(jnp.float32)
    return y.astype(x.dtype)


def rms_norm(x, g):
    xf = x.astype(jnp.float32)
    y = xf * lax.rsqrt(jnp.mean(xf * xf, axis=-1, keepdims=True) + RMS_EPS) * g.astype(jnp.float32)
    return y.astype(x.dtype)


def rope_tables(positions, dim):
    inv_freq = ROPE_THETA ** (-jnp.arange(0, dim, 2, dtype=jnp.float32) / dim)
    ang = positions.astype(jnp.float32)[..., None] * inv_freq
    return jnp.cos(ang), jnp.sin(ang)


def apply_rope(x, cos, sin):
    xf = x.astype(jnp.float32)
    half = xf.shape[-1] // 2
    x1, x2 = xf[..., :half], xf[..., half:]
    out = jnp.concatenate([x1 * cos - x2 * sin, x2 * cos + x1 * sin], axis=-1)
    return out.astype(x.dtype)


def split_cols(t, sizes):
    outs, start = [], 0
    for s in sizes:
        outs.append(t[..., start:start + s])
        start += s
    return outs


def mla_attention(cq_raw, ckv_raw, k_rope_raw, positions, q_norm_g, w_uq, kv_norm_g, w_ukv):
    bsz, seq, _ = cq_raw.shape
    cq = rms_norm(cq_raw, q_norm_g)
    q = (cq @ w_uq).reshape(bsz, seq, MLA_HEADS, QK_NOPE_DIM + QK_ROPE_DIM)
    q_nope, q_rope = q[..., :QK_NOPE_DIM], q[..., QK_NOPE_DIM:]
    ckv = rms_norm(ckv_raw, kv_norm_g)
    kv = (ckv @ w_ukv).reshape(bsz, seq, MLA_HEADS, QK_NOPE_DIM + V_HEAD_DIM)
    k_nope, v = kv[..., :QK_NOPE_DIM], kv[..., QK_NOPE_DIM:]
    cos, sin = rope_tables(positions, QK_ROPE_DIM)
    q_rope = apply_rope(q_rope, cos[:, :, None, :], sin[:, :, None, :])
    k_rope = apply_rope(k_rope_raw, cos, sin)
    scale = (QK_NOPE_DIM + QK_ROPE_DIM) ** -0.5
    nblk = seq // Q_BLOCK
    qn_blocks = q_nope.reshape(bsz, nblk, Q_BLOCK, MLA_HEADS, QK_NOPE_DIM).transpose(1, 0, 2, 3, 4)
    qr_blocks = q_rope.reshape(bsz, nblk, Q_BLOCK, MLA_HEADS, QK_ROPE_DIM).transpose(1, 0, 2, 3, 4)
    key_idx = jnp.arange(seq)

    def one_block(args):
        blk, qn, qr = args
        s = (jnp.einsum('bqhd,bkhd->bhqk', qn, k_nope)
             + jnp.einsum('bqhr,bkr->bhqk', qr, k_rope)).astype(jnp.float32) * scale
        q_idx = blk * Q_BLOCK + jnp.arange(Q_BLOCK)
        causal = key_idx[None, :] <= q_idx[:, None]
        s = jnp.where(causal[None, None], s, -1e30)
        probs = jax.nn.softmax(s, axis=-1).astype(v.dtype)
        return jnp.einsum('bhqk,bkhd->bqhd', probs, v)

    out = lax.map(one_block, (jnp.arange(nblk), qn_blocks, qr_blocks))
    return out.transpose(1, 0, 2, 3, 4).reshape(bsz, seq, MLA_HEADS * V_HEAD_DIM)


def s5_ssm(u, a_re, a_im, log_dt, b_re, b_im, c_re, c_im, d_skip):
    bsz, seq, _ = u.shape
    f32 = jnp.float32
    uf = u.astype(f32).reshape(bsz, seq, SSM_GROUPS, SSM_GROUP)
    dt = jnp.exp(log_dt.astype(f32))[:, None]
    lam_re = jnp.minimum(a_re.astype(f32), -1e-4)
    lam_im = a_im.astype(f32)
    mag = jnp.exp(lam_re * dt)
    ang = lam_im * dt
    abar_re = mag * jnp.cos(ang)
    abar_im = mag * jnp.sin(ang)
    den = lam_re * lam_re + lam_im * lam_im
    nr = abar_re - 1.0
    ni = abar_im
    f_re = (nr * lam_re + ni * lam_im) / den
    f_im = (ni * lam_re - nr * lam_im) / den
    br = b_re.astype(f32)
    bi = b_im.astype(f32)
    bbar_re = f_re[..., None] * br - f_im[..., None] * bi
    bbar_im = f_re[..., None] * bi + f_im[..., None] * br
    bu_re = jnp.einsum('bsgn,gpn->bsgp', uf, bbar_re)
    bu_im = jnp.einsum('bsgn,gpn->bsgp', uf, bbar_im)
    full = (bsz, seq, SSM_GROUPS, SSM_STATE)
    ar_seq = jnp.broadcast_to(abar_re[None, None], full)
    ai_seq = jnp.broadcast_to(abar_im[None, None], full)

    def combine(left, right):
        ar1, ai1, xr1, xi1 = left
        ar2, ai2, xr2, xi2 = right
        return (ar2 * ar1 - ai2 * ai1,
                ar2 * ai1 + ai2 * ar1,
                ar2 * xr1 - ai2 * xi1 + xr2,
                ar2 * xi1 + ai2 * xr1 + xi2)

    _, _, x_re, x_im = lax.associative_scan(combine, (ar_seq, ai_seq, bu_re, bu_im), axis=1)
    y = (jnp.einsum('bsgp,gnp->bsgn', x_re, c_re.astype(f32))
         - jnp.einsum('bsgp,gnp->bsgn', x_im, c_im.astype(f32)))
    y = y + d_skip.astype(f32).reshape(SSM_GROUPS, SSM_GROUP) * uf
    return y.reshape(bsz, seq, SSM_WIDTH).astype(u.dtype)


def setup_inputs(seed: int = 0) -> dict:
    key = jax.random.key(seed)
    ks = iter(jax.random.split(key, 40))
    f32 = jnp.float32
    L = DEPTH

    def w(shape, fan_in, scale=1.0):
        return jax.random.normal(next(ks), shape, f32) * (fan_in ** -0.5) * scale

    def gain(shape):
        return 1.0 + 0.02 * jax.random.normal(next(ks), shape, f32)

    def bias(shape):
        return 0.02 * jax.random.normal(next(ks), shape, f32)

    x = jax.random.normal(next(ks), (BATCH, SEQ, D_MODEL), f32)
    p = jax.random.normal(next(ks), (DEPTH, BATCH, SEQ, PLE_DIM), f32)
    start = jax.random.randint(next(ks), (BATCH, 1), 0, SEQ, dtype=jnp.int32)
    positions = start + jnp.arange(SEQ, dtype=jnp.int32)[None, :]

    n_idx = jnp.arange(SSM_STATE, dtype=f32)
    a_re = -0.5 + 0.01 * jax.random.normal(next(ks), (L, SSM_GROUPS, SSM_STATE), f32)
    a_im = math.pi * n_idx[None, None, :] + 0.01 * jax.random.normal(next(ks), (L, SSM_GROUPS, SSM_STATE), f32)
    log_dt = jax.random.uniform(next(ks), (L, SSM_GROUPS), f32, math.log(DT_MIN), math.log(DT_MAX))

    return {
        "x": x,
        "p": p,
        "positions": positions,
        "ln_in_g": gain((D_MODEL,)),
        "ln_in_b": bias((D_MODEL,)),
        "w_in": w((L, D_MODEL, IN_WIDTH), D_MODEL),
        "b_gate": bias((L, 2 * D_MODEL)),
        "q_norm_g": gain((L, Q_LORA_RANK)),
        "w_uq": w((L, Q_LORA_RANK, MLA_HEADS * (QK_NOPE_DIM + QK_ROPE_DIM)), Q_LORA_RANK),
        "kv_norm_g": gain((L, KV_LORA_RANK)),
        "w_ukv": w((L, KV_LORA_RANK, MLA_HEADS * (QK_NOPE_DIM + V_HEAD_DIM)), KV_LORA_RANK),
        "w_attn_br": w((L, MLA_HEADS * V_HEAD_DIM, D_MODEL), MLA_HEADS * V_HEAD_DIM),
        "a_re": a_re,
        "a_im": a_im,
        "log_dt": log_dt,
        "b_re": w((L, SSM_GROUPS, SSM_STATE, SSM_GROUP), 2 * SSM_GROUP),
        "b_im": w((L, SSM_GROUPS, SSM_STATE, SSM_GROUP), 2 * SSM_GROUP),
        "c_re": w((L, SSM_GROUPS, SSM_GROUP, SSM_STATE), 2 * SSM_STATE),
        "c_im": w((L, SSM_GROUPS, SSM_GROUP, SSM_STATE), 2 * SSM_STATE),
        "d_skip": jax.random.normal(next(ks), (L, SSM_WIDTH), f32),
        "w_glu": w((L, SSM_WIDTH, SSM_WIDTH), SSM_WIDTH),
        "b_glu": bias((L, SSM_WIDTH)),
        "w_ssm_br": w((L, SSM_WIDTH, D_MODEL), SSM_WIDTH),
        "w_o": w((L, D_MODEL, D_MODEL), D_MODEL, DEEPNORM_BETA),
        "ln1_g": gain((L, D_MODEL)),
        "ln1_b": bias((L, D_MODEL)),
        "w_up": w((L, D_MODEL, D_FF), D_MODEL),
        "w_down": w((L, D_FF, D_MODEL), D_FF, DEEPNORM_BETA),
        "ln2_g": gain((L, D_MODEL)),
        "ln2_b": bias((L, D_MODEL)),
        "w_ple_gate": w((L, D_MODEL, D_MODEL), D_MODEL),
        "b_ple_gate": bias((L, D_MODEL)),
        "w_ple": w((L, PLE_DIM, D_MODEL), PLE_DIM, DEEPNORM_BETA),
        "ln3_g": gain((L, D_MODEL)),
        "ln3_b": bias((L, D_MODEL)),
    }


def reference(x, p, positions, ln_in_g, ln_in_b, w_in, b_gate, q_norm_g, w_uq, kv_norm_g, w_ukv,
              w_attn_br, a_re, a_im, log_dt, b_re, b_im, c_re, c_im, d_skip, w_glu, b_glu,
              w_ssm_br, w_o, ln1_g, ln1_b, w_up, w_down, ln2_g, ln2_b, w_ple_gate, b_ple_gate,
              w_ple, ln3_g, ln3_b):
    h = layer_norm(x, ln_in_g, ln_in_b)
    for l in range(DEPTH):
        proj = h @ w_in[l]
        cq, ckv, kr, u, g_a, g_b = split_cols(proj, IN_SPLITS)
        attn = mla_attention(cq, ckv, kr, positions, q_norm_g[l], w_uq[l], kv_norm_g[l], w_ukv[l])
        branch_a = attn @ w_attn_br[l]
        y = s5_ssm(u, a_re[l], a_im[l], log_dt[l], b_re[l], b_im[l], c_re[l], c_im[l], d_skip[l])
        y = jax.nn.gelu(y)
        y = y * jax.nn.sigmoid(y @ w_glu[l] + b_glu[l])
        branch_b = y @ w_ssm_br[l]
        bg = b_gate[l]
        merged = (jax.nn.sigmoid(g_a + bg[:D_MODEL]) * branch_a
                  + jax.nn.sigmoid(g_b + bg[D_MODEL:]) * branch_b)
        h = layer_norm(DEEPNORM_ALPHA * h + merged @ w_o[l], ln1_g[l], ln1_b[l])
        ff = jnp.square(jax.nn.relu(h @ w_up[l])) @ w_down[l]
        h = layer_norm(DEEPNORM_ALPHA * h + ff, ln2_g[l], ln2_b[l])
        ple = jax.nn.sigmoid(h @ w_ple_gate[l] + b_ple_gate[l]) * (p[l] @ w_ple[l])
        h = layer_norm(DEEPNORM_ALPHA * h + ple, ln3_g[l], ln3_b[l])
    return h
```

```python
import contextlib
import math

import numpy as np

import concourse.bass as bass
import concourse.mybir as mybir
from concourse.bass_utils import run_bass_kernel_spmd

F32 = mybir.dt.float32
BF16 = mybir.dt.bfloat16
I32 = mybir.dt.int32
AF = mybir.ActivationFunctionType
ALU = mybir.AluOpType

S = 4096
D = 1024
NCH = 8
CH = 512
TC = 256
ALPHA = 2.0 ** 0.25
TWO_PI = 2.0 * math.pi
CW1 = 6.28125
CW2 = float(TWO_PI - 6.28125)
DUP_S = 1


class Tok:
    __slots__ = ("eng", "inst", "ms")

    def __init__(self, eng, inst):
        self.eng = eng
        self.inst = inst
        self.ms = None


class Buf:
    __slots__ = ("name", "w", "r")

    def __init__(self, name=""):
        self.name = name
        self.w = None
        self.r = {}


class DSem:
    def __init__(self, sem, name):
        self.sem = sem
        self.name = name
        self.val = 0


class KB:
    def __init__(self, nc, es):
        self.nc = nc
        self.es = es
        self.E = {"pe": nc.tensor, "act": nc.scalar, "dve": nc.vector, "pool": nc.gpsimd, "sp": nc.sync}
        self.sem = {e: es.enter_context(nc.semaphore("sem_" + e)) for e in ("pe", "act", "dve", "pool")}
        self.cnt = {e: 0 for e in self.sem}
        self.unflushed = {e: [] for e in self.sem}
        self.lasttok = {e: None for e in self.sem}
        self.seen = {e: {} for e in self.E}
        self.semname = {}
        self.dsems = []
        for e, s in self.sem.items():
            self.semname[id(s)] = "sem_" + e
        self.n_inst = 0
        self.n_wait = 0

    def dsem(self, name):
        s = self.es.enter_context(self.nc.semaphore(name))
        d = DSem(s, name)
        self.semname[id(s)] = name
        self.dsems.append(d)
        return d

    def _resolve(self, tok):
        if isinstance(tok, tuple):
            return tok
        if tok.ms is None:
            e = tok.eng
            self.cnt[e] += 1
            tok.inst.then_inc(self.sem[e], 1)
            lst = self.unflushed[e]
            i = lst.index(tok)
            for t in lst[: i + 1]:
                t.ms = self.cnt[e]
            self.unflushed[e] = lst[i + 1:]
        return (self.sem[tok.eng], tok.ms)

    def _need(self, eng, tok, waits, raw):
        if tok is None:
            return
        if not isinstance(tok, tuple) and tok.eng == eng and not raw:
            return
        if not isinstance(tok, tuple) and tok.eng == eng and eng == "pe":
            return
        sem, val = self._resolve(tok)
        key = self.semname[id(sem)]
        if self.seen[eng].get(key, 0) >= val:
            return
        self.seen[eng][key] = val
        waits.append((sem, val))

    def _deps(self, eng, r, w):
        waits = []
        for b in r:
            self._need(eng, b.w, waits, True)
        for b in w:
            self._need(eng, b.w, waits, False)
            for t in b.r.values():
                self._need(eng, t, waits, False)
        for (s, v) in waits:
            self.E[eng].wait_ge(s, v)
            self.n_wait += 1

    def op(self, eng, fn, r=(), w=()):
        self._deps(eng, r, w)
        inst = fn(self.E[eng])
        self.n_inst += 1
        tok = Tok(eng, inst)
        self.unflushed[eng].append(tok)
        self.lasttok[eng] = tok
        for b in r:
            if b not in w:
                b.r[eng] = tok
        for b in w:
            b.w = tok
            b.r = {}
        return tok

    def dma(self, q, out_ap, in_ap, dsem, r=(), w=(), **kw):
        self._deps(q, r, w)
        inst = self.E[q].dma_start(out=out_ap, in_=in_ap, **kw)
        dsem.val += 16
        inst.then_inc(dsem.sem, 16)
        tok = (dsem.sem, dsem.val)
        for b in r:
            b.r["dma:" + dsem.name] = tok
        for b in w:
            b.w = tok
            b.r = {}
        return tok

    def barrier(self):
        toks = []
        for e in self.sem:
            t = self.lasttok[e]
            if t is not None:
                toks.append(self._resolve(t))
        for d in self.dsems:
            if d.val > 0 and not d.name.startswith("d_cast"):
                toks.append((d.sem, d.val))
        for e in self.E:
            for (s, v) in toks:
                key = self.semname[id(s)]
                if self.seen[e].get(key, 0) >= v:
                    continue
                self.seen[e][key] = v
                self.E[e].wait_ge(s, v)


W_SHAPES = {
    "ln_in_g": [1024], "ln_in_b": [1024], "w_in": [1024, 3232], "b_gate": [2048], "q_norm_g": [384],
    "w_uq": [384, 768], "kv_norm_g": [256], "w_ukv": [256, 1024], "w_attn_br": [512, 1024],
    "a_re": [32, 64], "a_im": [32, 64], "log_dt": [32], "b_re": [32, 64, 16], "b_im": [32, 64, 16],
    "c_re": [32, 16, 64], "c_im": [32, 16, 64], "d_skip": [512], "w_glu": [512, 512], "b_glu": [512],
    "w_ssm_br": [512, 1024], "w_o": [1024, 1024], "ln1_g": [1024], "ln1_b": [1024],
    "w_up": [1024, 4096], "w_down": [4096, 1024], "ln2_g": [1024], "ln2_b": [1024],
    "w_ple_gate": [1024, 1024], "b_ple_gate": [1024], "w_ple": [256, 1024], "ln3_g": [1024], "ln3_b": [1024],
}


PCOL_LAYOUT = [("ln_in_g", 8), ("ln_in_b", 8), ("ln1_g", 8), ("ln1_b", 8), ("ln2_g", 8), ("ln2_b", 8),
               ("ln3_g", 8), ("ln3_b", 8), ("b_ple_gate", 8), ("b_glu", 4), ("d_skip", 4),
               ("q_norm_g", 3), ("kv_norm_g", 2), ("b_gate_a", 8), ("b_gate_b", 8)]


def pack_pcols(m):
    cols = []
    for name, n in PCOL_LAYOUT:
        if name == "b_gate_a":
            v = m["b_gate"][:1024]
        elif name == "b_gate_b":
            v = m["b_gate"][1024:]
        else:
            v = m[name]
        cols.append(np.asarray(v, np.float32).reshape(n, 128).T)
    return np.ascontiguousarray(np.concatenate(cols, axis=1))


def host_consts():
    ident = np.eye(128, dtype=np.float32)
    kk = np.arange(128)[:, None]
    qq = np.arange(128)[None, :]
    maskb = np.where(kk > qq, -30000.0, 0.0).astype(np.float32)
    iota = np.broadcast_to(np.arange(512, dtype=np.float32)[None, :], (128, 512)).copy()
    inv = (10000.0 ** (-np.arange(0, 32, 2, dtype=np.float32) / 32)).astype(np.float32)
    invf = np.zeros((128, 1), np.float32)
    for r in range(64, 96):
        invf[r, 0] = inv[(r - 64) % 16]
    return {"c_ident": ident, "c_maskb": maskb, "c_iota": iota, "c_invf": invf}


def build_program(stop_after=None, dbg=()):
    nc = bass.Bass("TRN2", target_bir_lowering=False)
    dbg = set(dbg)
    din = {}
    din["x"] = nc.dram_tensor("x", [S, D], F32, kind="ExternalInput").ap()
    din["p"] = nc.dram_tensor("p", [S, 256], F32, kind="ExternalInput").ap()
    din["positions"] = nc.dram_tensor("positions", [1, S], I32, kind="ExternalInput").ap()
    for k, shp in W_SHAPES.items():
        din[k] = nc.dram_tensor(k, shp, F32, kind="ExternalInput").ap()
    for k, v in host_consts().items():
        din[k] = nc.dram_tensor(k, list(v.shape), F32, kind="ExternalInput").ap()
    din["c_pcols"] = nc.dram_tensor("c_pcols", [128, 101], F32, kind="ExternalInput").ap()
    out_d = nc.dram_tensor("out", [S, D], F32, kind="ExternalOutput").ap()
    dbg_outs = {}

    mix_d = nc.dram_tensor("mix_d", [128, 8, 24, 128], BF16, kind="Internal").ap()
    wo_d = nc.dram_tensor("wo_d", [128, 8, 1024], BF16, kind="Internal").ap()
    wup_d = nc.dram_tensor("wup_d", [128, 4, 8, 1024], BF16, kind="Internal").ap()
    wdn_d = nc.dram_tensor("wdn_d", [128, 4, 8, 1024], BF16, kind="Internal").ap()
    wpg_d = nc.dram_tensor("wpg_d", [128, 8, 1024], BF16, kind="Internal").ap()
    wple_d = nc.dram_tensor("wple_d", [128, 2, 1024], BF16, kind="Internal").ap()

    with contextlib.ExitStack() as es:
        kb = KB(nc, es)
        op, dma = kb.op, kb.dma

        sb_ctr = [0]

        def sb(stack, name, shape, dt=F32):
            sb_ctr[0] += 1
            return stack.enter_context(nc.sbuf_tensor("%s_%d" % (name, sb_ctr[0]), shape, dt))

        ps_t = es.enter_context(nc.psum_tensor("ps", [128, 8, 512], F32))
        PS = [Buf("ps%d" % i) for i in range(8)]

        def psf(i):
            return ps_t[:, i, :]

        def psb(i):
            return ps_t[:, i, :].bitcast(BF16)

        ident_f = sb(es, "ident_f", [128, 128])
        ident_b = sb(es, "ident_b", [128, 128], BF16)
        maskb_f = sb(es, "maskb_f", [128, 128])
        maskb_b = sb(es, "maskb_b", [128, 128], BF16)
        ones_f = sb(es, "ones_f", [128, 128])
        ones_b = sb(es, "ones_b", [128, 128], BF16)
        onesd_b = sb(es, "onesd_b", [128, 128], BF16)
        iota_f = sb(es, "iota_f", [128, 512])
        invf = sb(es, "invf", [128, 1])
        NPC = 101
        pc = sb(es, "pcols", [128, NPC])
        pcs = sb(es, "pcols_s", [128, NPC])
        CONSTS = Buf("consts")
        s_setup = kb.dsem("d_setup")
        col = {}
        off = 0
        with nc.allow_non_contiguous_dma(reason="tiny parameter vectors"):
            dma("sp", ident_f[:], din["c_ident"][:, :], s_setup, w=[CONSTS])
            dma("sp", maskb_f[:], din["c_maskb"][:, :], s_setup, w=[CONSTS])
            dma("sp", iota_f[:], din["c_iota"][:, :], s_setup, w=[CONSTS])
            dma("sp", invf[:], din["c_invf"][:, :], s_setup, w=[CONSTS])
            dma("sp", pc[:], din["c_pcols"][:, :], s_setup, w=[CONSTS])
            for name, n in PCOL_LAYOUT:
                col[name] = off
                off += n
        assert off == NPC, (off, NPC)
        op("dve", lambda e: e.tensor_copy(out=ident_b[:], in_=ident_f[:]), r=[CONSTS], w=[CONSTS])
        op("dve", lambda e: e.tensor_copy(out=maskb_b[:], in_=maskb_f[:]), r=[CONSTS], w=[CONSTS])
        op("dve", lambda e: e.memset(ones_f[:], 1.0), w=[CONSTS])
        op("dve", lambda e: e.memset(ones_b[:], 1.0), w=[CONSTS])
        op("dve", lambda e: e.memset(onesd_b[:], 1.0 / 1024.0), w=[CONSTS])
        op("dve", lambda e: e.tensor_scalar(out=pcs[:], in0=pc[:], scalar1=ALPHA, scalar2=None, op0=ALU.mult),
           r=[CONSTS], w=[CONSTS])
        qg = col["q_norm_g"]
        op("dve", lambda e: e.tensor_scalar(out=pc[:, qg:qg + 3], in0=pc[:, qg:qg + 3], scalar1=96.0 ** -0.5,
                                            scalar2=None, op0=ALU.mult), r=[CONSTS], w=[CONSTS])

        def pcol(name, m=0, scaled=False):
            t = pcs if scaled else pc
            c = col[name] + m
            return t[:, c:c + 1]

        s_cast = kb.dsem("d_cast")
        CAST = Buf("cast")
        win_v = din["w_in"].rearrange("(kt p) c -> p kt c", p=128)

        def emit_casts():
            wat_v = din["w_attn_br"].rearrange("(kt p) c -> p kt c", p=128)
            wss_v = din["w_ssm_br"].rearrange("(kt p) c -> p kt c", p=128)
            for m in range(8):
                for half in range(2):
                    base = 1184 + half * 1024 + m * 128
                    dma("pool", mix_d[:, m, 8 * half:8 * half + 8, :], win_v[:, :, base:base + 128], s_cast, w=[CAST])
                dma("pool", mix_d[:, m, 16:20, :], wat_v[:, :, m * 128:(m + 1) * 128], s_cast, w=[CAST])
                dma("pool", mix_d[:, m, 20:24, :], wss_v[:, :, m * 128:(m + 1) * 128], s_cast, w=[CAST])
            dma("pool", wo_d[:, :, :], din["w_o"].rearrange("(kt p) c -> p kt c", p=128), s_cast, w=[CAST])
            for g in range(4):
                dma("pool", wup_d[:, g, :, :],
                    din["w_up"].rearrange("(kt p) c -> p kt c", p=128)[:, :, g * 1024:(g + 1) * 1024], s_cast, w=[CAST])
                dma("pool", wdn_d[:, g, :, :],
                    din["w_down"][g * 1024:(g + 1) * 1024, :].rearrange("(kk p) c -> p kk c", p=128), s_cast, w=[CAST])
            dma("pool", wpg_d[:, :, :], din["w_ple_gate"].rearrange("(kt p) c -> p kt c", p=128), s_cast, w=[CAST])
            dma("pool", wple_d[:, :, :], din["w_ple"].rearrange("(kt p) c -> p kt c", p=128), s_cast, w=[CAST])

        def sincos(stack_name, ang_ap_fn, shape, bufs, out_sin, out_cos, tmp):
            ANG, TMP, OUTB = bufs
            for outap, shift in ((out_sin, 0.0), (out_cos, math.pi / 2)):
                if outap is None:
                    continue
                a, q, qi = tmp["a"], tmp["q"], tmp["qi"]
                op("dve", lambda e: e.tensor_scalar(out=a, in0=ang_ap_fn(), scalar1=float(shift), scalar2=None, op0=ALU.add),
                   r=[ANG], w=[TMP])
                op("dve", lambda e: e.tensor_scalar(out=q, in0=a, scalar1=1.0 / TWO_PI, scalar2=None, op0=ALU.mult),
                   r=[TMP], w=[TMP])
                op("dve", lambda e: e.tensor_copy(out=qi, in_=q), r=[TMP], w=[TMP])
                op("dve", lambda e: e.tensor_copy(out=q, in_=qi), r=[TMP], w=[TMP])
                op("dve", lambda e: e.scalar_tensor_tensor(out=a, in0=q, scalar=-CW1, in1=a, op0=ALU.mult, op1=ALU.add),
                   r=[TMP], w=[TMP])
                op("dve", lambda e: e.scalar_tensor_tensor(out=a, in0=q, scalar=-CW2, in1=a, op0=ALU.mult, op1=ALU.add),
                   r=[TMP], w=[TMP])
                op("dve", lambda e: e.tensor_scalar(out=a, in0=a, scalar1=math.pi, scalar2=-math.pi, op0=ALU.min, op1=ALU.max),
                   r=[TMP], w=[TMP])
                op("act", lambda e: e.activation(out=outap, in_=a, func=AF.Sin), r=[TMP], w=[OUTB])

        XS = [Buf("xs%d" % i) for i in range(4)]
        s_x = [kb.dsem("d_x%d" % i) for i in range(4)]
        LNS = [Buf("lns%d" % i) for i in range(4)]

        def ln_in_tile(ti, xs_t, st_t, mv_t, rs_t, slot, g_scaled):
            dma("sp", xs_t[slot][:], din["x"][ti * 128:(ti + 1) * 128, :], s_x[slot], w=[XS[slot]])
            L = LNS[slot]
            op("dve", lambda e: e.bn_stats(out=st_t[slot][:, 0, :], in_=xs_t[slot][:, 0:512]), r=[XS[slot]], w=[L])
            op("dve", lambda e: e.bn_stats(out=st_t[slot][:, 1, :], in_=xs_t[slot][:, 512:1024]), r=[XS[slot]], w=[L])
            op("dve", lambda e: e.bn_aggr(out=mv_t[slot][:], in_=st_t[slot][:].rearrange("p a b -> p (a b)")), r=[L], w=[L])
            op("act", lambda e: e.activation(out=rs_t[slot][:], in_=mv_t[slot][:, 1:2], func=AF.Ln, bias=1e-5, scale=1.0),
               r=[L], w=[L])
            op("act", lambda e: e.activation(out=rs_t[slot][:], in_=rs_t[slot][:], func=AF.Exp, scale=-0.5), r=[L], w=[L])
            op("dve", lambda e: e.tensor_scalar(out=xs_t[slot][:], in0=xs_t[slot][:], scalar1=mv_t[slot][:, 0:1],
                                                scalar2=rs_t[slot][:, 0:1], op0=ALU.subtract, op1=ALU.mult),
               r=[L, XS[slot]], w=[XS[slot]])

        def ln_transpose_chunk(c, xs_t, st_t, mv_t, rs_t, banksets, evac, as_steps=False):
            nsl = len(xs_t)

            def stage_s(i):
                ti = 4 * c + i
                ln_in_tile(ti, xs_t, st_t, mv_t, rs_t, ti % nsl, False)

            def stage_t(i):
                ti = 4 * c + i
                slot = ti % nsl
                pa = banksets[ti % len(banksets)]
                for ft in range(8):
                    bank = pa + ft // 4
                    op("pe", lambda e: e.transpose(ps_t[:, bank, (ft % 4) * 128:(ft % 4 + 1) * 128],
                                                   xs_t[slot][:, ft * 128:(ft + 1) * 128], ident_f[:]),
                       r=[XS[slot], CONSTS], w=[PS[bank]])
                for ft in range(8):
                    bank = pa + ft // 4
                    evac(i, ft, ps_t[:, bank, (ft % 4) * 128:(ft % 4 + 1) * 128], PS[bank])
            steps_ = [lambda: stage_s(0), lambda: stage_s(1), lambda: stage_t(0), lambda: stage_s(2),
                      lambda: stage_t(1), lambda: stage_s(3), lambda: stage_t(2), lambda: stage_t(3)]
            if as_steps:
                return steps_
            for st_ in steps_:
                st_()

        def dbg_dump(name, ap, buf):
            if name not in dbg:
                return
            shp = list(ap.shape)
            d = nc.dram_tensor("dbg_" + name, shp, ap.dtype, kind="ExternalOutput").ap()
            dbg_outs[name] = d
            idx = tuple(slice(None) for _ in shp)
            dma("sp", d[idx], ap, s_dbg, r=[buf])

        s_dbg = kb.dsem("d_dbg")

        attnT = sb(es, "attnT", [128, 4, S], BF16)
        ATT = [Buf("att%d" % c) for c in range(NCH)]


        f12 = contextlib.ExitStack()
        cqnT = sb(f12, "cqnT", [128, 3, S], BF16)
        ckvnT = sb(f12, "ckvnT", [128, 2, S], BF16)
        krT = sb(f12, "krT", [128, S], BF16)
        cosT = sb(f12, "cosT", [128, S], BF16)
        sinT = sb(f12, "sinT", [128, S], BF16)
        CQN = [Buf("cqn%d" % c) for c in range(NCH)]
        CKV = [Buf("ckv%d" % c) for c in range(NCH)]
        KR = [Buf("kr%d" % c) for c in range(NCH)]
        ROPE = Buf("rope")
        wuq = sb(f12, "wuq", [128, 3, 768], BF16)
        wuqr = sb(f12, "wuqr", [128, 3, 768], BF16)
        wukv = sb(f12, "wukv", [128, 2, 1024], BF16)
        WATT = Buf("watt")
        s_w2 = kb.dsem("d_w2")

        with contextlib.ExitStack() as f1:
            wina = sb(f1, "wina", [128, 8, 672], BF16)
            wkr = sb(f1, "wkr", [128, 8, 96], BF16)
            wkrr = sb(f1, "wkrr", [128, 8, 96], BF16)
            WINA = Buf("wina")
            s_w1 = kb.dsem("d_w1")
            dma("pool", wina[:], win_v[:, :, 0:672], s_w1, w=[WINA])
            dma("pool", wuq[:], din["w_uq"].rearrange("(kt p) c -> p kt c", p=128), s_w2, w=[WATT])
            dma("pool", wukv[:], din["w_ukv"].rearrange("(kt p) c -> p kt c", p=128), s_w2, w=[WATT])
            emit_casts()
            op("dve", lambda e: e.memset(wkr[:], 0.0), w=[WINA])
            op("dve", lambda e: e.memset(wkrr[:], 0.0), w=[WINA])
            op("dve", lambda e: e.tensor_copy(out=wkr[:, :, 64:96], in_=wina[:, :, 640:672]), r=[WINA], w=[WINA])
            op("dve", lambda e: e.tensor_scalar(out=wkrr[:, :, 64:80], in0=wina[:, :, 656:672], scalar1=-1.0, scalar2=None,
                                                op0=ALU.mult), r=[WINA], w=[WINA])
            op("dve", lambda e: e.tensor_copy(out=wkrr[:, :, 80:96], in_=wina[:, :, 640:656]), r=[WINA], w=[WINA])

            xs_t = [sb(f1, "xs%d" % i_, [128, 1024]) for i_ in range(4)]
            st_t = [sb(f1, "st%d" % i_, [128, 2, 6]) for i_ in range(4)]
            mv_t = [sb(f1, "mv%d" % i_, [128, 2]) for i_ in range(4)]
            rs_t = [sb(f1, "rs%d" % i_, [128, 1]) for i_ in range(4)]
            h0T = [sb(f1, "h0T0", [128, 8, CH], BF16), sb(f1, "h0T1", [128, 8, CH], BF16)]
            H0T = [Buf("h0T0"), Buf("h0T1")]
            cqc = sb(f1, "cqc", [128, 5, CH], BF16)
            sqc = sb(f1, "sqc", [128, 5, CH], BF16)
            CQC = [Buf("cqc%d" % m) for m in range(5)]
            SQC = [Buf("sqc%d" % m) for m in range(5)]
            rsq = [sb(f1, "rsq0", [128, CH]), sb(f1, "rsq1", [128, CH])]
            RSQ = [Buf("rsq0"), Buf("rsq1")]
            t1 = sb(f1, "f1t1", [128, CH])
            t2 = sb(f1, "f1t2", [128, CH])
            T12 = Buf("f1t12")
            psrot = [0]

            def next_ps():
                i = 4 + (psrot[0] % 4)
                psrot[0] += 1
                return i

            def f1_ln(c):
                hb = c % 2

                def evac_f1(i, ft, pap, PB):
                    op("act", lambda e: e.activation(out=h0T[hb][:, ft, i * 128:(i + 1) * 128], in_=pap,
                                                     func=AF.Identity, bias=pcol("ln_in_b", ft), scale=pcol("ln_in_g", ft)),
                       r=[PB, CONSTS], w=[H0T[hb]])
                ln_transpose_chunk(c, xs_t, st_t, mv_t, rs_t, [0, 2], evac_f1)

            def f1_proj(c):
                hb = c % 2
                for m in range(5):
                    b = next_ps()
                    for kt in range(8):
                        op("pe", lambda e: e.matmul(psf(b), wina[:, kt, m * 128:(m + 1) * 128], h0T[hb][:, kt, :],
                                                    start=(kt == 0), stop=(kt == 7)), r=[WINA, H0T[hb]], w=[PS[b]])
                    op("act", lambda e: e.activation(out=cqc[:, m, :], in_=psf(b), func=AF.Copy), r=[PS[b]], w=[CQC[m]])
                    op("dve", lambda e: e.tensor_tensor(out=sqc[:, m, :], in0=cqc[:, m, :], in1=cqc[:, m, :], op=ALU.mult),
                       r=[CQC[m]], w=[SQC[m]])
                ba = next_ps()
                for kt in range(8):
                    op("pe", lambda e: e.matmul(ps_t[0:96, ba, :], wkr[:, kt, :], h0T[hb][:, kt, :], start=(kt == 0), stop=(kt == 7)),
                       r=[WINA, H0T[hb]], w=[PS[ba]])
                bb = next_ps()
                for kt in range(8):
                    op("pe", lambda e: e.matmul(ps_t[0:96, bb, :], wkrr[:, kt, :], h0T[hb][:, kt, :], start=(kt == 0), stop=(kt == 7)),
                       r=[WINA, H0T[hb]], w=[PS[bb]])
                return ba, bb

            def f1_tail(c, ba, bb):
                csl = slice(c * CH, (c + 1) * CH)
                op("dve", lambda e: e.tensor_tensor(out=t1[64:96, :], in0=ps_t[64:96, ba, :], in1=cosT[64:96, csl], op=ALU.mult),
                   r=[PS[ba], ROPE], w=[T12])
                op("dve", lambda e: e.tensor_tensor(out=t2[64:96, :], in0=ps_t[64:96, bb, :], in1=sinT[64:96, csl], op=ALU.mult),
                   r=[PS[bb], ROPE], w=[T12])
                op("dve", lambda e: e.tensor_tensor(out=krT[64:96, csl], in0=t1[64:96, :], in1=t2[64:96, :], op=ALU.add),
                   r=[T12], w=[KR[c]])
                for which, (m0, m1, nfeat, eps, gname, dstT, DST) in enumerate(
                        [(0, 3, 384.0, 1e-6, "q_norm_g", cqnT, CQN), (3, 5, 256.0, 1e-6, "kv_norm_g", ckvnT, CKV)]):
                    b = next_ps()
                    for m in range(m0, m1):
                        op("pe", lambda e: e.matmul(psf(b), ones_b[:], sqc[:, m, :], start=(m == m0), stop=(m == m1 - 1)),
                           r=[SQC[m], CONSTS], w=[PS[b]])
                    op("act", lambda e: e.activation(out=rsq[which][:], in_=psf(b), func=AF.Ln, bias=float(eps),
                                                     scale=1.0 / nfeat), r=[PS[b]], w=[RSQ[which]])
                    op("act", lambda e: e.activation(out=rsq[which][:], in_=rsq[which][:], func=AF.Exp, scale=-0.5),
                       r=[RSQ[which]], w=[RSQ[which]])
                    for m in range(m0, m1):
                        op("dve", lambda e: e.scalar_tensor_tensor(out=dstT[:, m - m0, csl], in0=cqc[:, m, :],
                                                                   scalar=pcol(gname, m - m0), in1=rsq[which][:],
                                                                   op0=ALU.mult, op1=ALU.mult),
                           r=[CQC[m], RSQ[which], CONSTS], w=[DST[c]])

            f1_ln(0)
            if True:
                rt = f1
                posi = sb(rt, "posi", [128, 1024], I32)
                angf = sb(rt, "angf", [128, 1024])
                ta = sb(rt, "rp_a", [128, 1024])
                tq = sb(rt, "rp_q", [128, 1024])
                tqi = sb(rt, "rp_qi", [128, 1024], I32)
                POS, ANG, TMP = Buf("pos"), Buf("ang"), Buf("rptmp")
                s_pos = kb.dsem("d_pos")
                for cc in range(4):
                    dma("sp", posi[:], din["positions"][:, cc * 1024:(cc + 1) * 1024].partition_broadcast(128), s_pos, w=[POS])
                    op("dve", lambda e: e.tensor_copy(out=angf[:], in_=posi[:]), r=[POS], w=[ANG])
                    op("dve", lambda e: e.tensor_scalar(out=angf[:], in0=angf[:], scalar1=invf[:, 0:1], scalar2=None,
                                                        op0=ALU.mult), r=[ANG, CONSTS], w=[ANG])
                    sincos("rope", lambda: angf[:], None, (ANG, TMP, ROPE),
                           sinT[:, cc * 1024:(cc + 1) * 1024], cosT[:, cc * 1024:(cc + 1) * 1024],
                           {"a": ta[:], "q": tq[:], "qi": tqi[:]})

            for c in range(NCH):
                ba, bb = f1_proj(c)
                if c + 1 < NCH:
                    f1_ln(c + 1)
                f1_tail(c, ba, bb)
            if stop_after == "F1":
                dbg_dump("cqnT", cqnT[:], CQN[NCH - 1])
                dbg_dump("ckvnT", ckvnT[:], CKV[NCH - 1])
                dbg_dump("krT", krT[64:96, :], KR[NCH - 1])
            kb.barrier()

        if stop_after == "F1":
            f12.close()
            return finish(nc, kb, out_d, dbg_outs, s_dbg)

        with contextlib.ExitStack() as f2:
            op("dve", lambda e: e.memset(wuqr[:], 0.0), w=[WATT])
            wuq4 = wuq[:].rearrange("p k (h d) -> p k h d", d=96)
            wuqr4 = wuqr[:].rearrange("p k (h d) -> p k h d", d=96)
            for kt in range(3):
                op("dve", lambda e: e.tensor_scalar(out=wuqr4[:, kt, :, 64:80], in0=wuq4[:, kt, :, 80:96], scalar1=-1.0,
                                                    scalar2=None, op0=ALU.mult), r=[WATT], w=[WATT])
                op("dve", lambda e: e.tensor_copy(out=wuqr4[:, kt, :, 80:96], in_=wuq4[:, kt, :, 64:80]), r=[WATT], w=[WATT])
            qT = [sb(f2, "qT0", [128, S], BF16), sb(f2, "qT1", [128, S], BF16)]
            kT = [sb(f2, "kT0", [128, S], BF16), sb(f2, "kT1", [128, S], BF16)]
            Vb = [sb(f2, "Vb0", [128, 32, 128], BF16), sb(f2, "Vb1", [128, 32, 128], BF16)]
            NPT = 4
            PT = [sb(f2, "PT%d" % i, [128, CH], BF16) for i in range(NPT)]
            rec = sb(f2, "rec", [128, CH])
            bcs = sb(f2, "bcs", [128, CH])
            t1 = sb(f2, "f2t1", [128, CH])
            t2 = sb(f2, "f2t2", [128, CH])
            QTn = [[Buf() for _ in range(NCH)] for _ in range(2)]
            QTr = [[Buf() for _ in range(NCH)] for _ in range(2)]
            KTn = [[Buf() for _ in range(NCH)] for _ in range(2)]
            KTr = [[Buf() for _ in range(NCH)] for _ in range(2)]
            VB = [[Buf() for _ in range(4)] for _ in range(2)]
            PTB = [Buf() for _ in range(NPT)]
            REC, BCS, T1B, T2B = Buf(), Buf(), Buf(), Buf()
            VONES = [Buf(), Buf()]
            op("dve", lambda e: e.memset(Vb[0][:, :, 64:128], 1.0), w=[VONES[0]])
            op("dve", lambda e: e.memset(Vb[1][:, :, 0:64], 1.0), w=[VONES[1]])
            prep_rot = [0]

            def prep_bank():
                b = 6 + (prep_rot[0] % 2)
                prep_rot[0] += 1
                return b

            def head_prep_pieces(h):
                hb = h % 2
                voff = 0 if hb == 0 else 64
                pieces_ = []

                def q_a(tc):
                    csl = slice(tc * CH, (tc + 1) * CH)
                    ba = prep_bank()
                    for kt in range(3):
                        op("pe", lambda e: e.matmul(ps_t[0:96, ba, :], wuq[:, kt, 96 * h:96 * h + 96], cqnT[:, kt, csl],
                                                    start=(kt == 0), stop=(kt == 2)), r=[WATT, CQN[tc]], w=[PS[ba]])
                    op("dve", lambda e: e.tensor_copy(out=qT[hb][0:64, csl], in_=ps_t[0:64, ba, :]), r=[PS[ba]], w=[QTn[hb][tc]])
                    op("dve", lambda e: e.tensor_tensor(out=t1[64:96, :], in0=ps_t[64:96, ba, :], in1=cosT[64:96, csl], op=ALU.mult),
                       r=[PS[ba], ROPE], w=[T1B])

                def q_b(tc):
                    csl = slice(tc * CH, (tc + 1) * CH)
                    bb = prep_bank()
                    for kt in range(3):
                        op("pe", lambda e: e.matmul(ps_t[0:96, bb, :], wuqr[:, kt, 96 * h:96 * h + 96], cqnT[:, kt, csl],
                                                    start=(kt == 0), stop=(kt == 2)), r=[WATT, CQN[tc]], w=[PS[bb]])
                    op("dve", lambda e: e.tensor_tensor(out=t2[64:96, :], in0=ps_t[64:96, bb, :], in1=sinT[64:96, csl], op=ALU.mult),
                       r=[PS[bb], ROPE], w=[T2B])
                    op("dve", lambda e: e.tensor_tensor(out=qT[hb][64:96, csl], in0=t1[64:96, :], in1=t2[64:96, :], op=ALU.add),
                       r=[T1B, T2B], w=[QTr[hb][tc]])

                def k_c(tc):
                    csl = slice(tc * CH, (tc + 1) * CH)
                    bk = prep_bank()
                    for kt in range(2):
                        op("pe", lambda e: e.matmul(ps_t[0:64, bk, :], wukv[:, kt, 128 * h:128 * h + 64], ckvnT[:, kt, csl],
                                                    start=(kt == 0), stop=(kt == 1)), r=[WATT, CKV[tc]], w=[PS[bk]])
                    op("dve", lambda e: e.tensor_copy(out=kT[hb][0:64, csl], in_=ps_t[0:64, bk, :]), r=[PS[bk]], w=[KTn[hb][tc]])
                    op("dve", lambda e: e.tensor_copy(out=kT[hb][64:96, csl], in_=krT[64:96, csl]), r=[KR[tc]], w=[KTr[hb][tc]])

                def v_half(tg, hf):
                    bv = prep_bank()
                    for t4 in range(4):
                        ti = tg * 8 + hf * 4 + t4
                        for kt in range(2):
                            op("pe", lambda e: e.matmul(ps_t[:, bv, t4 * 64:(t4 + 1) * 64], ckvnT[:, kt, ti * 128:(ti + 1) * 128],
                                                        wukv[:, kt, 128 * h + 64:128 * h + 128], start=(kt == 0), stop=(kt == 1)),
                               r=[WATT, CKV[ti // 4]], w=[PS[bv]])
                    t0_ = tg * 8 + hf * 4
                    op("dve", lambda e: e.tensor_copy(out=Vb[hb][:, t0_:t0_ + 4, voff:voff + 64],
                                                      in_=ps_t[:, bv, 0:256].rearrange("p (a b) -> p a b", b=64)),
                       r=[PS[bv]], w=[VB[hb][tg]])
                for tc in range(NCH):
                    pieces_.append(lambda tc=tc: q_a(tc))
                    pieces_.append(lambda tc=tc: q_b(tc))
                    pieces_.append(lambda tc=tc: k_c(tc))
                    if tc % 2 == 1:
                        tg = tc // 2
                        pieces_.append(lambda tg=tg: v_half(tg, 0))
                        pieces_.append(lambda tg=tg: v_half(tg, 1))
                return pieces_

            sc_rot = [0]
            pt_rot = [0]

            def head_flash(h, nxt_pieces):
                hb = h % 2
                prow = 64 if hb == 0 else 0
                orow = 0 if hb == 0 else 64
                items = [(qc, kt) for qc in range(NCH) for kt in range(4 * qc + 4)]
                state = {}

                def emit_S(idx):
                    qc, kt = items[idx]
                    j = kt - 4 * qc
                    col0 = 128 * j if j > 0 else 0
                    b = sc_rot[0] % 3
                    sc_rot[0] += 1
                    state[idx] = (b, col0)
                    for rep in range(DUP_S if j < 0 else 1):
                        op("pe", lambda e: e.matmul(ps_t[:, b, col0:CH], kT[hb][0:96, kt * 128:(kt + 1) * 128],
                                                    qT[hb][0:96, qc * CH + col0:(qc + 1) * CH], start=True, stop=(j < 0)),
                           r=[KTn[hb][kt // 4], KTr[hb][kt // 4], QTn[hb][qc], QTr[hb][qc]], w=[PS[b]])
                    if j >= 0:
                        op("pe", lambda e: e.matmul(ps_t[:, b, col0:col0 + 128], ident_b[:], maskb_b[:], start=False, stop=True),
                           r=[CONSTS], w=[PS[b]])

                deferred = []

                def emit_norm_a(qc, acc):
                    op("dve", lambda e: e.reciprocal(out=rec[prow:prow + 1, :], in_=ps_t[prow:prow + 1, acc, :]), r=[PS[acc]], w=[REC])

                def emit_norm_b(qc, acc):
                    op("pe", lambda e: e.matmul(psf(5), ones_f[prow:prow + 1, :], rec[prow:prow + 1, :], start=True, stop=True),
                       r=[REC, CONSTS], w=[PS[5]])
                    op("dve", lambda e: e.tensor_copy(out=bcs[orow:orow + 64, :], in_=ps_t[orow:orow + 64, 5, :]),
                       r=[PS[5]], w=[BCS])
                    op("dve", lambda e: e.tensor_tensor(out=attnT[orow:orow + 64, h // 2, qc * CH:(qc + 1) * CH],
                                                        in0=ps_t[orow:orow + 64, acc, :], in1=bcs[orow:orow + 64, :], op=ALU.mult),
                       r=[PS[acc], BCS], w=[ATT[qc]])

                n = len(items)
                emit_S(0)
                if n > 1:
                    emit_S(1)
                for idx in range(n):
                    qc, kt = items[idx]
                    nk = 4 * qc + 4
                    acc = 3 + (qc % 2)
                    b, col0 = state.pop(idx)
                    pi = pt_rot[0] % NPT
                    pt_rot[0] += 1
                    op("act", lambda e: e.activation(out=PT[pi][:, col0:CH], in_=ps_t[:, b, col0:CH], func=AF.Exp),
                       r=[PS[b]], w=[PTB[pi]])
                    if idx + 2 < n:
                        emit_S(idx + 2)
                    op("pe", lambda e: e.matmul(ps_t[:, acc, col0:CH], Vb[hb][:, kt, :], PT[pi][:, col0:CH],
                                                start=(kt == 0), stop=(kt == nk - 1)),
                       r=[VB[hb][kt // 8], VONES[hb], PTB[pi]], w=[PS[acc]])
                    for d in list(deferred):
                        d[0] -= 1
                        if d[0] <= 0:
                            emit_norm_b(d[1], d[2])
                            deferred.remove(d)
                    if kt == nk - 1:
                        emit_norm_a(qc, acc)
                        deferred.append([6, qc, acc])
                    if nxt_pieces and idx % 9 in (2, 6):
                        nxt_pieces.pop(0)()
                for d in deferred:
                    emit_norm_b(d[1], d[2])
                while nxt_pieces:
                    nxt_pieces.pop(0)()

            for p_ in head_prep_pieces(0):
                p_()
            for h in range(8):
                head_flash(h, head_prep_pieces(h + 1) if h + 1 < 8 else [])
            if stop_after == "F2":
                dbg_dump("attnT", attnT[:], ATT[NCH - 1])
            kb.barrier()
        f12.close()
        if stop_after == "F2":
            return finish(nc, kb, out_d, dbg_outs, s_dbg)

        fb = es.enter_context(contextlib.ExitStack())
        ygT = sb(fb, "ygT", [128, 4, S], BF16)
        YG = [Buf("yg%d" % c) for c in range(NCH)]
        with contextlib.ExitStack() as f3:
            s_w3 = kb.dsem("d_w3")
            s_s5 = kb.dsem("d_s5")
            winu = sb(f3, "winu", [128, 8, 512], BF16)
            wglu = sb(f3, "wglu", [128, 4, 512], BF16)
            W3 = Buf("w3")
            dma("pool", winu[:], win_v[:, :, 672:1184], s_w3, w=[W3])
            dma("pool", wglu[:], din["w_glu"].rearrange("(kt p) c -> p kt c", p=128), s_w3, w=[W3])
            cosTab = sb(f3, "cosTab", [128, 16, TC], BF16)
            sinTab = sb(f3, "sinTab", [128, 16, TC], BF16)
            WB = sb(f3, "WB", [128, 16, 2, 128], BF16)
            WA = sb(f3, "WA", [128, 16, 2, 128], BF16)
            LC = sb(f3, "LC", [128, 16, 2, 128], BF16)
            LCa = sb(f3, "LCa", [128, 16, 2, 128], BF16)
            Dsk = sb(f3, "Dsk", [128, 4, 128], BF16)
            K0D = sb(f3, "K0D", [128, 4, 128], BF16)
            rdec = sb(f3, "rdec", [128, 16])
            Er = sb(f3, "Er", [128, 16])
            Ei = sb(f3, "Ei", [128, 16])
            S5C = Buf("s5c")
            with contextlib.ExitStack() as sp_:
                Are = sb(sp_, "Are", [128, 16])
                Aim = sb(sp_, "Aim", [128, 16])
                Ldt = sb(sp_, "Ldt", [128, 16])
                Bre = sb(sp_, "Bre", [128, 16, 16])
                Bim = sb(sp_, "Bim", [128, 16, 16])
                Cin = [sb(sp_, "Cin_re", [128, 2, 2, 64]), sb(sp_, "Cin_im", [128, 2, 2, 64])]
                Csm = [sb(sp_, "Csm_re", [128, 16, 16]), sb(sp_, "Csm_im", [128, 16, 16])]
                BP = sb(sp_, "BP", [128, 16, 2, 128])
                PRM = Buf("s5prm")
                with nc.allow_non_contiguous_dma(reason="small S5 parameter loads"):
                    qs = ["sp", "act"]
                    qi_ = [0]

                    def pdma(dst, src):
                        q_ = qs[qi_[0] % 2]
                        qi_[0] += 1
                        dma(q_, dst, src, s_s5, w=[PRM])
                    Ain = [sb(sp_, "Ain_re", [16, 2, 64]), sb(sp_, "Ain_im", [16, 2, 64])]
                    Bin = [sb(sp_, "Bin_re", [16, 2, 64, 16]), sb(sp_, "Bin_im", [16, 2, 64, 16])]
                    PRM2 = Buf("s5prm2")
                    for ri, (na, nb_) in enumerate([("a_re", "b_re"), ("a_im", "b_im")]):
                        dma(qs[ri], Ain[ri][:], din[na].rearrange("(j two) p -> j two p", two=2), s_s5, w=[PRM2])
                        dma(qs[1 - ri], Bin[ri][:], din[nb_].rearrange("(j two) p n -> j two p n", two=2), s_s5, w=[PRM2])
                    for two in range(2):
                        psl = slice(64 * two, 64 * two + 64)
                        pdma(Ldt[psl, :], din["log_dt"].rearrange("(j two) -> two j", two=2)[two:two + 1, :].partition_broadcast(64))
                    for ri, nm in enumerate(["c_re", "c_im"]):
                        for blk in range(2):
                            for two in range(2):
                                pdma(Cin[ri][:, blk, two, :], din[nm][16 * blk + two:16 * blk + 16:2, :, :])
                for ri, dstA, dstB in ((0, Are, Bre), (1, Aim, Bim)):
                    op("pe", lambda e: e.transpose(ps_t[:, 6, 0:16], Ain[ri][:].rearrange("j a b -> j (a b)"), ident_f[0:16, 0:16]),
                       r=[PRM2, CONSTS], w=[PS[6]])
                    op("dve", lambda e: e.tensor_copy(out=dstA[:], in_=ps_t[:, 6, 0:16]), r=[PS[6]], w=[PRM])
                    for n_ in range(16):
                        op("pe", lambda e: e.transpose(ps_t[:, 7, n_ * 16:(n_ + 1) * 16],
                                                       Bin[ri][:, :, :, n_].rearrange("j a b -> j (a b)"), ident_f[0:16, 0:16]),
                           r=[PRM2, CONSTS], w=[PS[7]])
                    op("dve", lambda e: e.tensor_copy(out=dstB[:].rearrange("p j n -> p n j"),
                                                      in_=ps_t[:, 7, 0:256].rearrange("p (n j) -> p n j", j=16)),
                       r=[PS[7]], w=[PRM])
                sm = {}
                for nm in ["dt", "lr", "ldt", "ang", "mag", "sa", "ca", "abr", "abi", "den", "nr", "fre", "fim", "u1", "u2",
                           "thr", "ta", "tq"]:
                    sm[nm] = sb(sp_, "s5_" + nm, [128, 16])
                tqi = sb(sp_, "s5_tqi", [128, 16], I32)
                SM = Buf("s5sm")
                TMPB = Buf("s5tmp")

                def dv(fn, r=(PRM,), w=None):
                    op("dve", fn, r=list(r) + [SM], w=[SM] if w is None else w)

                def tt(o, a, b, o_):
                    dv(lambda e: e.tensor_tensor(out=o, in0=a, in1=b, op=o_))
                op("act", lambda e: e.activation(out=sm["dt"][:], in_=Ldt[:], func=AF.Exp), r=[PRM], w=[SM])
                dv(lambda e: e.tensor_scalar(out=sm["lr"][:], in0=Are[:], scalar1=-1e-4, scalar2=None, op0=ALU.min))
                tt(sm["ldt"][:], sm["lr"][:], sm["dt"][:], ALU.mult)
                tt(sm["ang"][:], Aim[:], sm["dt"][:], ALU.mult)
                op("act", lambda e: e.activation(out=sm["mag"][:], in_=sm["ldt"][:], func=AF.Exp), r=[SM], w=[SM])
                sincos("s5a", lambda: sm["ang"][:], None, (SM, TMPB, SM), sm["sa"][:], sm["ca"][:],
                       {"a": sm["ta"][:], "q": sm["tq"][:], "qi": tqi[:]})
                tt(sm["abr"][:], sm["mag"][:], sm["ca"][:], ALU.mult)
                tt(sm["abi"][:], sm["mag"][:], sm["sa"][:], ALU.mult)
                tt(sm["u1"][:], sm["lr"][:], sm["lr"][:], ALU.mult)
                tt(sm["u2"][:], Aim[:], Aim[:], ALU.mult)
                tt(sm["den"][:], sm["u1"][:], sm["u2"][:], ALU.add)
                dv(lambda e: e.reciprocal(out=sm["den"][:], in_=sm["den"][:]))
                dv(lambda e: e.tensor_scalar(out=sm["nr"][:], in0=sm["abr"][:], scalar1=-1.0, scalar2=None, op0=ALU.add))
                tt(sm["u1"][:], sm["nr"][:], sm["lr"][:], ALU.mult)
                tt(sm["u2"][:], sm["abi"][:], Aim[:], ALU.mult)
                tt(sm["fre"][:], sm["u1"][:], sm["u2"][:], ALU.add)
                tt(sm["fre"][:], sm["fre"][:], sm["den"][:], ALU.mult)
                tt(sm["u1"][:], sm["abi"][:], sm["lr"][:], ALU.mult)
                tt(sm["u2"][:], sm["nr"][:], Aim[:], ALU.mult)
                tt(sm["fim"][:], sm["u1"][:], sm["u2"][:], ALU.subtract)
                tt(sm["fim"][:], sm["fim"][:], sm["den"][:], ALU.mult)
                dv(lambda e: e.tensor_tensor(out=rdec[:], in0=sm["mag"][:], in1=sm["mag"][:], op=ALU.mult), w=[SM, S5C])
                Bb = [sb(sp_, "Bb_re", [128, 16, 16]), sb(sp_, "Bb_im", [128, 16, 16])]
                bt1 = sb(sp_, "bt1", [128, 16, 16])
                bt2 = sb(sp_, "bt2", [128, 16, 16])
                fre_b = sm["fre"][:].unsqueeze(2).to_broadcast([128, 16, 16])
                fim_b = sm["fim"][:].unsqueeze(2).to_broadcast([128, 16, 16])
                tt(bt1[:], Bre[:], fre_b, ALU.mult)
                tt(bt2[:], Bim[:], fim_b, ALU.mult)
                tt(Bb[0][:], bt1[:], bt2[:], ALU.subtract)
                tt(bt1[:], Bim[:], fre_b, ALU.mult)
                tt(bt2[:], Bre[:], fim_b, ALU.mult)
                tt(Bb[1][:], bt1[:], bt2[:], ALU.add)
                dv(lambda e: e.memset(BP[:], 0.0))
                for two in range(2):
                    psl = slice(64 * two, 64 * two + 64)
                    for r_ in range(4):
                        c0 = 32 * r_ + 16 * two
                        for ri in range(2):
                            dv(lambda e: e.tensor_copy(out=BP[psl, r_::4, ri, c0:c0 + 16], in_=Bb[ri][psl, r_::4, :]))
                for jg in range(8):
                    bnk = 6 + jg % 2
                    for q_ in range(4):
                        j, ri = 2 * jg + q_ // 2, q_ % 2
                        op("pe", lambda e: e.transpose(ps_t[:, bnk, q_ * 128:(q_ + 1) * 128], BP[:, j, ri, :], ident_f[:]),
                           r=[SM, CONSTS], w=[PS[bnk]])
                    op("dve", lambda e: e.tensor_copy(out=WB[:, 2 * jg:2 * jg + 2, :, :].rearrange("p a b c -> p (a b c)"),
                                                      in_=ps_t[:, bnk, :]), r=[PS[bnk]], w=[S5C])
                BPh = sb(sp_, "BPh", [128, 16, 2, 128], BF16)
                dv(lambda e: e.tensor_copy(out=BPh[:], in_=BP[:]))
                AB = [sb(sp_, "AB_re", [128, 16, 16]), sb(sp_, "AB_im", [128, 16, 16])]
                abr_b = sm["abr"][:].unsqueeze(2).to_broadcast([128, 16, 16])
                abi_b = sm["abi"][:].unsqueeze(2).to_broadcast([128, 16, 16])
                tt(bt1[:], Bb[0][:], abr_b, ALU.mult)
                tt(bt2[:], Bb[1][:], abi_b, ALU.mult)
                tt(AB[0][:], bt1[:], bt2[:], ALU.subtract)
                tt(bt1[:], Bb[1][:], abr_b, ALU.mult)
                tt(bt2[:], Bb[0][:], abi_b, ALU.mult)
                tt(AB[1][:], bt1[:], bt2[:], ALU.add)
                for two in range(2):
                    psl = slice(64 * two, 64 * two + 64)
                    for r_ in range(4):
                        c0 = 32 * r_ + 16 * two
                        for ri in range(2):
                            dv(lambda e: e.tensor_copy(out=BP[psl, r_::4, ri, c0:c0 + 16], in_=AB[ri][psl, r_::4, :]))
                for jg in range(8):
                    bnk = 6 + jg % 2
                    for q_ in range(4):
                        j, ri = 2 * jg + q_ // 2, q_ % 2
                        op("pe", lambda e: e.transpose(ps_t[:, bnk, q_ * 128:(q_ + 1) * 128], BP[:, j, ri, :], ident_f[:]),
                           r=[SM, CONSTS], w=[PS[bnk]])
                    op("dve", lambda e: e.tensor_copy(out=WA[:, 2 * jg:2 * jg + 2, :, :].rearrange("p a b c -> p (a b c)"),
                                                      in_=ps_t[:, bnk, :]), r=[PS[bnk]], w=[S5C])
                for ri in range(2):
                    for blk in range(2):
                        bnk = 6 + blk
                        op("pe", lambda e: e.transpose(ps_t[:, bnk, 0:128], Cin[ri][:, blk, :, :].rearrange("p a b -> p (a b)"),
                                                       ident_f[:]), r=[PRM, CONSTS], w=[PS[bnk]])
                        op("dve", lambda e: e.tensor_copy(out=Csm[ri][:, 8 * blk:8 * blk + 8, :].rearrange("p a b -> p (a b)"),
                                                          in_=ps_t[:, bnk, 0:128]), r=[PS[bnk]], w=[SM])
                op("dve", lambda e: e.memset(LC[:], 0.0), w=[S5C])
                for two in range(2):
                    psl = slice(64 * two, 64 * two + 64)
                    for r_ in range(4):
                        c0 = 32 * r_ + 16 * two
                        op("dve", lambda e: e.tensor_copy(out=LC[psl, r_::4, 0, c0:c0 + 16], in_=Csm[0][psl, r_::4, :]),
                           r=[SM], w=[S5C])
                        op("dve", lambda e: e.tensor_scalar(out=LC[psl, r_::4, 1, c0:c0 + 16], in0=Csm[1][psl, r_::4, :],
                                                            scalar1=-1.0, scalar2=None, op0=ALU.mult), r=[SM], w=[S5C])
                for m in range(4):
                    op("dve", lambda e: e.tensor_scalar(out=Dsk[:, m, :], in0=ident_f[:], scalar1=pcol("d_skip", m), scalar2=None,
                                                        op0=ALU.mult), r=[CONSTS], w=[S5C])
                CA = [sb(sp_, "CA_re", [128, 16, 16]), sb(sp_, "CA_im", [128, 16, 16])]
                tt(bt1[:], Csm[0][:], abr_b, ALU.mult)
                tt(bt2[:], Csm[1][:], abi_b, ALU.mult)
                tt(CA[0][:], bt1[:], bt2[:], ALU.subtract)
                tt(bt1[:], Csm[0][:], abi_b, ALU.mult)
                tt(bt2[:], Csm[1][:], abr_b, ALU.mult)
                tt(CA[1][:], bt1[:], bt2[:], ALU.add)
                op("dve", lambda e: e.memset(LCa[:], 0.0), w=[S5C])
                for two in range(2):
                    psl = slice(64 * two, 64 * two + 64)
                    for r_ in range(4):
                        c0 = 32 * r_ + 16 * two
                        op("dve", lambda e: e.tensor_copy(out=LCa[psl, r_::4, 0, c0:c0 + 16], in_=CA[0][psl, r_::4, :]),
                           r=[SM], w=[S5C])
                        op("dve", lambda e: e.tensor_scalar(out=LCa[psl, r_::4, 1, c0:c0 + 16], in0=CA[1][psl, r_::4, :],
                                                            scalar1=-1.0, scalar2=None, op0=ALU.mult), r=[SM], w=[S5C])
                for m in range(4):
                    bnk = 6 + m % 2
                    for jj in range(4):
                        for ri in range(2):
                            op("pe", lambda e: e.matmul(ps_t[:, bnk, 0:128], BPh[:, 4 * m + jj, ri, :], LC[:, 4 * m + jj, ri, :],
                                                        start=(jj == 0 and ri == 0), stop=(jj == 3 and ri == 1)),
                               r=[SM, S5C], w=[PS[bnk]])
                    op("dve", lambda e: e.scalar_tensor_tensor(out=K0D[:, m, :], in0=ident_f[:], scalar=pcol("d_skip", m),
                                                               in1=ps_t[:, bnk, 0:128], op0=ALU.mult, op1=ALU.add),
                       r=[PS[bnk], CONSTS], w=[S5C])
                dv(lambda e: e.tensor_scalar(out=sm["tq"][:], in0=sm["ang"][:], scalar1=1.0 / TWO_PI, scalar2=None, op0=ALU.mult))
                dv(lambda e: e.tensor_copy(out=tqi[:], in_=sm["tq"][:]))
                dv(lambda e: e.tensor_copy(out=sm["tq"][:], in_=tqi[:]))
                dv(lambda e: e.scalar_tensor_tensor(out=sm["thr"][:], in0=sm["tq"][:], scalar=-CW1, in1=sm["ang"][:],
                                                    op0=ALU.mult, op1=ALU.add))
                dv(lambda e: e.scalar_tensor_tensor(out=sm["thr"][:], in0=sm["tq"][:], scalar=-CW2, in1=sm["thr"][:],
                                                    op0=ALU.mult, op1=ALU.add))
                dv(lambda e: e.tensor_scalar(out=sm["u2"][:], in0=sm["thr"][:], scalar1=2.0, scalar2=None, op0=ALU.mult))
                dv(lambda e: e.tensor_scalar(out=sm["tq"][:], in0=sm["u2"][:], scalar1=1.0 / TWO_PI, scalar2=None, op0=ALU.mult))
                dv(lambda e: e.tensor_copy(out=tqi[:], in_=sm["tq"][:]))
                dv(lambda e: e.tensor_copy(out=sm["tq"][:], in_=tqi[:]))
                dv(lambda e: e.scalar_tensor_tensor(out=sm["thr"][:], in0=sm["tq"][:], scalar=-CW1, in1=sm["u2"][:],
                                                    op0=ALU.mult, op1=ALU.add))
                dv(lambda e: e.scalar_tensor_tensor(out=sm["thr"][:], in0=sm["tq"][:], scalar=-CW2, in1=sm["thr"][:],
                                                    op0=ALU.mult, op1=ALU.add))
                dv(lambda e: e.tensor_scalar(out=sm["u1"][:], in0=sm["thr"][:], scalar1=float(TC), scalar2=None, op0=ALU.mult))
                sincos("s5e", lambda: sm["u1"][:], None, (SM, TMPB, S5C), Ei[:], Er[:],
                       {"a": sm["ta"][:], "q": sm["tq"][:], "qi": tqi[:]})
                tgA = sb(sp_, "tgA", [128, 4, TC])
                tga = sb(sp_, "tga", [128, 4, TC])
                tgq = sb(sp_, "tgq", [128, 4, TC])
                tgqi = sb(sp_, "tgqi", [128, 4, TC], I32)
                TGA, TGT = Buf("tga"), Buf("tgt")
                for tg in range(4):
                    op("dve", lambda e: e.tensor_tensor(out=tgA[:], in0=sm["thr"][:, 4 * tg:4 * tg + 4].unsqueeze(2).to_broadcast([128, 4, TC]),
                                                        in1=iota_f[:, 0:TC].unsqueeze(1).to_broadcast([128, 4, TC]), op=ALU.mult),
                       r=[SM, CONSTS], w=[TGA])
                    sincos("s5t", lambda: tgA[:], None, (TGA, TGT, S5C), sinTab[:, 4 * tg:4 * tg + 4, :], cosTab[:, 4 * tg:4 * tg + 4, :],
                           {"a": tga[:], "q": tgq[:], "qi": tgqi[:]})
                kb.barrier()

            xs_t = [sb(f3, "xs%d" % i_, [128, 1024]) for i_ in range(2)]
            st_t = [sb(f3, "st%d" % i_, [128, 2, 6]) for i_ in range(2)]
            mv_t = [sb(f3, "mv%d" % i_, [128, 2]) for i_ in range(2)]
            rs_t = [sb(f3, "rs%d" % i_, [128, 1]) for i_ in range(2)]
            h0T = [sb(f3, "h0T0", [128, 8, CH], BF16), sb(f3, "h0T1", [128, 8, CH], BF16)]
            H0T = [Buf("h0T0"), Buf("h0T1")]
            uc = [sb(f3, "uc0", [128, 4, CH], BF16), sb(f3, "uc1", [128, 4, CH], BF16)]
            UC = [[Buf() for _ in range(4)] for _ in range(2)]
            ygc = sb(f3, "ygc", [128, 4, CH], BF16)
            YGC = [Buf() for _ in range(4)]
            sg = [sb(f3, "sg0", [128, CH], BF16), sb(f3, "sg1", [128, CH], BF16)]
            SG = [Buf(), Buf()]
            bre = [sb(f3, "bre0", [128, TC], BF16), sb(f3, "bre1", [128, TC], BF16)]
            bim = [sb(f3, "bim0", [128, TC], BF16), sb(f3, "bim1", [128, TC], BF16)]
            BRE = [Buf(), Buf()]
            BIM = [Buf(), Buf()]
            tmps = [{n_: sb(f3, "s5w%d_" % l_ + n_, [128, TC], BF16) for n_ in ["ta", "tb", "tc", "td", "vre", "vim", "zre", "zim"]}
                    for l_ in range(2)]
            TBs = [{n_: Buf() for n_ in tmps[0]} for l_ in range(2)]
            ZLs = [Buf("zl0"), Buf("zl1")]
            xr = [sb(f3, "xr%d" % i, [128, TC + 2], BF16) for i in range(8)]
            xi = [sb(f3, "xi%d" % i, [128, TC + 2], BF16) for i in range(8)]
            xlast = [sb(f3, "xlast_re", [128, 16], BF16), sb(f3, "xlast_im", [128, 16], BF16)]
            XL = [Buf("xl0"), Buf("xl1")]
            op("dve", lambda e: e.memset(xlast[0][:], 0.0), w=[XL[0], XL[1]])
            op("dve", lambda e: e.memset(xlast[1][:], 0.0), w=[XL[0], XL[1]])
            XR = [Buf() for _ in range(8)]
            XI = [Buf() for _ in range(8)]
            zin = [sb(f3, "zin_re", [128, 16]), sb(f3, "zin_im", [128, 16])]
            zl = [sb(f3, "zl_re", [128, 16]), sb(f3, "zl_im", [128, 16])]
            cw = [sb(f3, "cw%d" % i, [128, 16]) for i in range(4)]
            ZIN, ZL, CWB = Buf("zin"), Buf("zl"), Buf("cw")
            op("dve", lambda e: e.memset(zin[0][:], 0.0), w=[ZIN])
            op("dve", lambda e: e.memset(zin[1][:], 0.0), w=[ZIN])
            rot3 = [0]

            def gen_bank():
                b = 6 + rot3[0] % 2
                rot3[0] += 1
                return b

            def f3_front_steps(c):
                hb_ = c % 2

                def evac_f3(i, ft, pap, PB):
                    op("act", lambda e: e.activation(out=h0T[hb_][:, ft, i * 128:(i + 1) * 128], in_=pap,
                                                     func=AF.Identity, bias=pcol("ln_in_b", ft), scale=pcol("ln_in_g", ft)),
                       r=[PB, CONSTS], w=[H0T[hb_]])
                steps_ = ln_transpose_chunk(c, xs_t, st_t, mv_t, rs_t, [0], evac_f3, as_steps=True)

                def up(m):
                    b = gen_bank()
                    for kt in range(8):
                        op("pe", lambda e: e.matmul(psf(b), winu[:, kt, m * 128:(m + 1) * 128], h0T[hb_][:, kt, :],
                                                    start=(kt == 0), stop=(kt == 7)), r=[W3, H0T[hb_]], w=[PS[b]])
                    op("act", lambda e: e.activation(out=uc[hb_][:, m, :], in_=psf(b), func=AF.Copy), r=[PS[b]], w=[UC[hb_][m]])
                for m in range(4):
                    steps_.append(lambda m=m: up(m))
                return steps_

            for st_ in f3_front_steps(0):
                st_()
            for c in range(NCH):
                hb = c % 2
                csl = slice(c * CH, (c + 1) * CH)
                nxt_steps = f3_front_steps(c + 1) if c + 1 < NCH else []
                def tile_steps(j, lane):
                    m = j // 4
                    bs = lane
                    T = tmps[lane]
                    TBl = TBs[lane]
                    pr_, pi_ = 2 + 2 * bs, 3 + 2 * bs
                    ueo = uc[hb][:, m, :].rearrange("p (c two) -> p two c", two=2)
                    for ri, pb_ in ((0, pr_), (1, pi_)):
                        op("pe", lambda e: e.matmul(ps_t[:, pb_, 0:TC], WA[:, j, ri, :], ueo[:, 0, :], start=True, stop=False),
                           r=[S5C, UC[hb][m]], w=[PS[pb_]])
                        op("pe", lambda e: e.matmul(ps_t[:, pb_, 0:TC], WB[:, j, ri, :], ueo[:, 1, :], start=False, stop=True),
                           r=[S5C, UC[hb][m]], w=[PS[pb_]])
                    op("act", lambda e: e.activation(out=bre[bs][:], in_=ps_t[:, pr_, 0:TC], func=AF.Copy), r=[PS[pr_]], w=[BRE[bs]])
                    op("act", lambda e: e.activation(out=bim[bs][:], in_=ps_t[:, pi_, 0:TC], func=AF.Copy), r=[PS[pi_]], w=[BIM[bs]])
                    yield
                    cs_, sn_ = cosTab[:, j, :], sinTab[:, j, :]

                    def d2(o, a, b_, o_, rb, wb):
                        op("dve", lambda e: e.tensor_tensor(out=o, in0=a, in1=b_, op=o_), r=rb, w=wb)
                    d2(T["ta"][:], bre[bs][:], cs_, ALU.mult, [BRE[bs], S5C], [TBl["ta"]])
                    yield
                    d2(T["tb"][:], bim[bs][:], sn_, ALU.mult, [BIM[bs], S5C], [TBl["tb"]])
                    yield
                    d2(T["tc"][:], bim[bs][:], cs_, ALU.mult, [BIM[bs], S5C], [TBl["tc"]])
                    yield
                    d2(T["td"][:], bre[bs][:], sn_, ALU.mult, [BRE[bs], S5C], [TBl["td"]])
                    yield
                    d2(T["vre"][:], T["ta"][:], T["tb"][:], ALU.add, [TBl["ta"], TBl["tb"]], [TBl["vre"]])
                    yield
                    d2(T["vim"][:], T["tc"][:], T["td"][:], ALU.subtract, [TBl["tc"], TBl["td"]], [TBl["vim"]])
                    yield
                    op("dve", lambda e: e.tensor_tensor_scan(out=T["zre"][:], data0=rdec[:, j:j + 1].to_broadcast([128, TC]),
                                                             data1=T["vre"][:], initial=zin[0][:, j:j + 1], op0=ALU.mult, op1=ALU.add),
                       r=[TBl["vre"], ZIN, S5C], w=[TBl["zre"]])
                    yield
                    op("dve", lambda e: e.tensor_tensor_scan(out=T["zim"][:], data0=rdec[:, j:j + 1].to_broadcast([128, TC]),
                                                             data1=T["vim"][:], initial=zin[1][:, j:j + 1], op0=ALU.mult, op1=ALU.add),
                       r=[TBl["vim"], ZIN, S5C], w=[TBl["zim"]])
                    yield
                    op("dve", lambda e: e.tensor_copy(out=zl[0][:, j:j + 1], in_=T["zre"][:, TC - 1:TC]), r=[TBl["zre"]], w=[ZLs[lane]])
                    op("dve", lambda e: e.tensor_copy(out=zl[1][:, j:j + 1], in_=T["zim"][:, TC - 1:TC]), r=[TBl["zim"]], w=[ZLs[lane]])
                    yield
                    xs_ = (m % 2) * 4 + j % 4
                    op("dve", lambda e: e.tensor_copy(out=xr[xs_][:, 1:2], in_=xlast[0][:, j:j + 1]), r=[XL[lane]], w=[XR[xs_]])
                    op("dve", lambda e: e.tensor_copy(out=xi[xs_][:, 1:2], in_=xlast[1][:, j:j + 1]), r=[XL[lane]], w=[XI[xs_]])
                    d2(T["ta"][:], T["zre"][:], cs_, ALU.mult, [TBl["zre"], S5C], [TBl["ta"]])
                    yield
                    d2(T["tb"][:], T["zim"][:], sn_, ALU.mult, [TBl["zim"], S5C], [TBl["tb"]])
                    yield
                    d2(T["tc"][:], T["zim"][:], cs_, ALU.mult, [TBl["zim"], S5C], [TBl["tc"]])
                    yield
                    d2(T["td"][:], T["zre"][:], sn_, ALU.mult, [TBl["zre"], S5C], [TBl["td"]])
                    yield
                    d2(xr[xs_][:, 2:TC + 2], T["ta"][:], T["tb"][:], ALU.subtract, [TBl["ta"], TBl["tb"], XR[xs_]], [XR[xs_]])
                    yield
                    d2(xi[xs_][:, 2:TC + 2], T["tc"][:], T["td"][:], ALU.add, [TBl["tc"], TBl["td"], XI[xs_]], [XI[xs_]])
                    yield
                    op("dve", lambda e: e.tensor_copy(out=xlast[0][:, j:j + 1], in_=xr[xs_][:, TC + 1:TC + 2]), r=[XR[xs_]], w=[XL[lane]])
                    op("dve", lambda e: e.tensor_copy(out=xlast[1][:, j:j + 1], in_=xi[xs_][:, TC + 1:TC + 2]), r=[XI[xs_]], w=[XL[lane]])
                    yield

                def c_matmuls(m):
                    b = gen_bank()
                    ueo = uc[hb][:, m, :].rearrange("p (c two) -> p two c", two=2)
                    for jj in range(4):
                        j2 = 4 * m + jj
                        x2 = (m % 2) * 4 + jj
                        op("pe", lambda e: e.matmul(ps_t[:, b, 0:TC], LC[:, j2, 0, :], xr[x2][:, 2:TC + 2], start=(jj == 0), stop=False),
                           r=[S5C, XR[x2]], w=[PS[b]])
                        op("pe", lambda e: e.matmul(ps_t[:, b, 0:TC], LC[:, j2, 1, :], xi[x2][:, 2:TC + 2], start=False, stop=False),
                           r=[S5C, XI[x2]], w=[PS[b]])
                    op("pe", lambda e: e.matmul(ps_t[:, b, 0:TC], Dsk[:, m, :], ueo[:, 1, :], start=False, stop=True),
                       r=[S5C, UC[hb][m]], w=[PS[b]])
                    for jj in range(4):
                        j2 = 4 * m + jj
                        x2 = (m % 2) * 4 + jj
                        op("pe", lambda e: e.matmul(ps_t[:, b, TC:2 * TC], LCa[:, j2, 0, :], xr[x2][:, 1:TC + 1], start=(jj == 0), stop=False),
                           r=[S5C, XR[x2]], w=[PS[b]])
                        op("pe", lambda e: e.matmul(ps_t[:, b, TC:2 * TC], LCa[:, j2, 1, :], xi[x2][:, 1:TC + 1], start=False, stop=False),
                           r=[S5C, XI[x2]], w=[PS[b]])
                    op("pe", lambda e: e.matmul(ps_t[:, b, TC:2 * TC], K0D[:, m, :], ueo[:, 0, :], start=False, stop=True),
                       r=[S5C, UC[hb][m]], w=[PS[b]])
                    yeo = ygc[:, m, :].rearrange("p (c two) -> p two c", two=2)
                    op("act", lambda e: e.activation(out=yeo[:, 1, :], in_=ps_t[:, b, 0:TC], func=AF.Gelu_apprx_tanh), r=[PS[b]], w=[YGC[m]])
                    op("act", lambda e: e.activation(out=yeo[:, 0, :], in_=ps_t[:, b, TC:2 * TC], func=AF.Gelu_apprx_tanh), r=[PS[b]], w=[YGC[m]])

                def start_pair(jp):
                    gs = [tile_steps(2 * jp, 0), tile_steps(2 * jp + 1, 1)]
                    for g_ in gs:
                        next(g_)
                    return gs
                pair = start_pair(0)
                for jp in range(8):
                    alive = list(pair)
                    while alive:
                        for g_ in list(alive):
                            try:
                                next(g_)
                            except StopIteration:
                                alive.remove(g_)
                    if jp + 1 < 8:
                        pair = start_pair(jp + 1)
                    if jp % 2 == 1:
                        c_matmuls(jp // 2)
                    if jp >= 1:
                        for _ in range(2):
                            if nxt_steps:
                                nxt_steps.pop(0)()
                while nxt_steps:
                    nxt_steps.pop(0)()
                for m2 in range(4):
                    b = gen_bank()
                    for m in range(4):
                        op("pe", lambda e: e.matmul(psf(b), wglu[:, m, m2 * 128:(m2 + 1) * 128], ygc[:, m, :], start=(m == 0), stop=(m == 3)),
                           r=[W3, YGC[m]], w=[PS[b]])
                    op("act", lambda e: e.activation(out=sg[m2 % 2][:], in_=psf(b), func=AF.Sigmoid, bias=pcol("b_glu", m2), scale=1.0),
                       r=[PS[b], CONSTS], w=[SG[m2 % 2]])
                    op("dve", lambda e: e.tensor_tensor(out=ygT[:, m2, csl], in0=ygc[:, m2, :], in1=sg[m2 % 2][:], op=ALU.mult),
                       r=[YGC[m2], SG[m2 % 2]], w=[YG[c]])
                def c2(o, a, b_, o_, rb, wb):
                    op("dve", lambda e: e.tensor_tensor(out=o, in0=a, in1=b_, op=o_), r=rb, w=wb)
                c2(cw[0][:], Er[:], zl[0][:], ALU.mult, [S5C, ZLs[0], ZLs[1]], [CWB])
                c2(cw[1][:], Ei[:], zl[1][:], ALU.mult, [S5C, ZLs[0], ZLs[1]], [CWB])
                c2(cw[2][:], Er[:], zl[1][:], ALU.mult, [S5C, ZLs[0], ZLs[1]], [CWB])
                c2(cw[3][:], Ei[:], zl[0][:], ALU.mult, [S5C, ZLs[0], ZLs[1]], [CWB])
                c2(zin[0][:], cw[0][:], cw[1][:], ALU.subtract, [CWB], [ZIN])
                c2(zin[1][:], cw[2][:], cw[3][:], ALU.add, [CWB], [ZIN])
            if stop_after == "F3":
                dbg_dump("ygT", ygT[:], YG[NCH - 1])
            kb.barrier()
        if stop_after == "F3":
            fb.close()
            return finish(nc, kb, out_d, dbg_outs, s_dbg)

        with contextlib.ExitStack() as bk:
            NSLOT = 4
            RSZ = 4096
            ring = [sb(bk, "ring%d" % i, [128, RSZ], BF16) for i in range(NSLOT)]
            RING = [Buf("ring%d" % i) for i in range(NSLOT)]
            s_ring = [kb.dsem("d_ring%d" % i) for i in range(NSLOT)]
            xs_t = [sb(bk, "xs0", [128, 1024]), sb(bk, "xs1", [128, 1024])]
            st_t = [sb(bk, "st0", [128, 2, 6]), sb(bk, "st1", [128, 2, 6])]
            mv_t = [sb(bk, "mv0", [128, 2]), sb(bk, "mv1", [128, 2])]
            rs_t = [sb(bk, "rs0", [128, 1]), sb(bk, "rs1", [128, 1])]
            resT = [sb(bk, "resT%d" % i, [128, 8, CH]) for i in range(2)]
            hbT = [sb(bk, "hbT%d" % i, [128, 8, CH], BF16) for i in range(2)]
            mgT = sb(bk, "mgT", [128, 8, CH], BF16)
            RES = [[Buf("res%d_%d" % (i, m)) for m in range(8)] for i in range(2)]
            HB = [[Buf("hb%d_%d" % (i, m)) for m in range(8)] for i in range(2)]
            MG = [Buf("mg%d" % m) for m in range(8)]
            sga = [sb(bk, "sga%d" % i, [128, CH], BF16) for i in range(2)]
            sgb = [sb(bk, "sgb%d" % i, [128, CH], BF16) for i in range(2)]
            SGA = [Buf(), Buf()]
            SGB = [Buf(), Buf()]
            t1 = sb(bk, "bt1", [128, CH])
            t2 = sb(bk, "bt2", [128, CH])
            T1, T2 = Buf("t1"), Buf("t2")
            reluT = [sb(bk, "relu%d" % i, [128, 8, CH], BF16) for i in range(2)]
            RL = [[Buf() for _ in range(8)] for _ in range(2)]
            rtmp = [sb(bk, "rtmp%d" % i, [128, CH], BF16) for i in range(2)]
            RT = [Buf(), Buf()]
            mean_s = sb(bk, "mean_s", [128, CH])
            var_s = sb(bk, "var_s", [128, CH])
            rstd_s = sb(bk, "rstd_s", [128, CH])
            nmr_s = sb(bk, "nmr_s", [128, CH])
            LNB = Buf("lnb")
            ntost = sb(bk, "ntost", [128, 1024])
            NT = [Buf(), Buf()]
            pst = sb(bk, "pst", [128, 256])
            pbf = sb(bk, "pbf", [128, 256], BF16)
            PST, PBF = Buf(), Buf()
            s_p = kb.dsem("d_p0")
            pT = sb(bk, "pT", [128, 2, CH], BF16)
            PTT = Buf("pT")
            s_out = kb.dsem("d_out")
            rotb = [0]

            def gbank():
                b = 2 + rotb[0] % 6
                rotb[0] += 1
                return b

            def chunk_pieces():
                lst = []
                lst += [("wo", 0), ("wo", 1)]
                for g in range(4):
                    lst += [("up", g, 0), ("up", g, 1), ("dn", g, 0), ("dn", g, 1)]
                lst += [("mix", m) for m in range(4)]
                lst += [("pg", 0), ("ple", 0), ("pg", 1)]
                lst += [("mix", m) for m in range(4, 8)]
                return lst
            pieces = [("mix", m) for m in range(8)]
            for c in range(NCH):
                cp = chunk_pieces()
                if c == NCH - 1:
                    cp = [p_ for p_ in cp if p_[0] != "mix"]
                pieces += cp
            ring_state = {"issued": 0, "cur": -1}

            def ring_issue():
                k = ring_state["issued"]
                kind = pieces[k][0]
                sl = k % NSLOT
                if kind == "mix":
                    m = pieces[k][1]
                    dst, src = ring[sl][:, 0:3072], mix_d[:, m, :, :].rearrange("p k c -> p (k c)")
                elif kind == "wo":
                    hf = pieces[k][1]
                    dst = ring[sl][:, :].rearrange("p (k c) -> p k c", k=8)
                    src = wo_d[:, :, hf * 512:(hf + 1) * 512]
                elif kind == "up":
                    g, hf = pieces[k][1], pieces[k][2]
                    dst = ring[sl][:, :].rearrange("p (k c) -> p k c", k=8)
                    src = wup_d[:, g, :, hf * 512:(hf + 1) * 512]
                elif kind == "dn":
                    g, hf = pieces[k][1], pieces[k][2]
                    dst = ring[sl][:, :].rearrange("p (k c) -> p k c", k=8)
                    src = wdn_d[:, g, :, hf * 512:(hf + 1) * 512]
                elif kind == "pg":
                    hf = pieces[k][1]
                    dst = ring[sl][:, :].rearrange("p (k c) -> p k c", k=8)
                    src = wpg_d[:, :, hf * 512:(hf + 1) * 512]
                else:
                    dst, src = ring[sl][:, 0:2048], wple_d.rearrange("p k c -> p (k c)")
                dma("sp", dst, src, s_ring[sl], r=[CAST], w=[RING[sl]])
                ring_state["issued"] += 1

            def ring_next(kind, live_prev=0):
                ring_state["cur"] += 1
                k = ring_state["cur"]
                assert pieces[k][0] == kind, (k, pieces[k], kind)
                while ring_state["issued"] < min(len(pieces), k + NSLOT - live_prev):
                    ring_issue()
                return ring[k % NSLOT], RING[k % NSLOT]

            def a1_s(c, i):
                ln_in_tile(4 * c + i, xs_t, st_t, mv_t, rs_t, (4 * c + i) % 2, False)

            def a1_t(c, i):
                par = c % 2
                slot = (4 * c + i) % 2
                for ft in range(8):
                    bank = ft // 4
                    op("pe", lambda e: e.transpose(ps_t[:, bank, (ft % 4) * 128:(ft % 4 + 1) * 128],
                                                   xs_t[slot][:, ft * 128:(ft + 1) * 128], ident_f[:]),
                       r=[XS[slot], CONSTS], w=[PS[bank]])
                for ft in range(8):
                    bank = ft // 4
                    op("act", lambda e: e.activation(out=resT[par][:, ft, i * 128:(i + 1) * 128],
                                                     in_=ps_t[:, bank, (ft % 4) * 128:(ft % 4 + 1) * 128], func=AF.Identity,
                                                     bias=pcol("ln_in_b", ft, True), scale=pcol("ln_in_g", ft, True)),
                       r=[PS[bank], CONSTS], w=[RES[par][ft]])

            def a1_fin(c):
                par = c % 2
                for ft in range(8):
                    op("dve", lambda e: e.tensor_scalar(out=hbT[par][:, ft, :], in0=resT[par][:, ft, :], scalar1=1.0 / ALPHA,
                                                        scalar2=None, op0=ALU.mult), r=[RES[par][ft]], w=[HB[par][ft]])

            a2_state = {}

            def a2_pe(c, m):
                par = c % 2
                csl = slice(c * CH, (c + 1) * CH)
                rg, RG = ring_next("mix")
                wv = rg[:, 0:3072].rearrange("p (k c) -> p k c", k=24)
                ba, bb_, bc_, bd_ = gbank(), gbank(), gbank(), gbank()
                for kt in range(8):
                    op("pe", lambda e: e.matmul(psf(ba), wv[:, kt, :], hbT[par][:, kt, :], start=(kt == 0), stop=(kt == 7)),
                       r=[RG, HB[par][kt]], w=[PS[ba]])
                for kt in range(8):
                    op("pe", lambda e: e.matmul(psf(bb_), wv[:, 8 + kt, :], hbT[par][:, kt, :], start=(kt == 0), stop=(kt == 7)),
                       r=[RG, HB[par][kt]], w=[PS[bb_]])
                for kt in range(4):
                    op("pe", lambda e: e.matmul(psf(bc_), wv[:, 16 + kt, :], attnT[:, kt, csl], start=(kt == 0), stop=(kt == 3)),
                       r=[RG, ATT[c]], w=[PS[bc_]])
                for kt in range(4):
                    op("pe", lambda e: e.matmul(psf(bd_), wv[:, 20 + kt, :], ygT[:, kt, csl], start=(kt == 0), stop=(kt == 3)),
                       r=[RG, YG[c]], w=[PS[bd_]])
                a2_state[(c, m)] = (ba, bb_, bc_, bd_)

            def a2_ev(c, m):
                ba, bb_, bc_, bd_ = a2_state.pop((c, m))
                mi = m % 2
                op("act", lambda e: e.activation(out=sga[mi][:], in_=psf(ba), func=AF.Sigmoid, bias=pcol("b_gate_a", m), scale=1.0),
                   r=[PS[ba], CONSTS], w=[SGA[mi]])
                op("act", lambda e: e.activation(out=sgb[mi][:], in_=psf(bb_), func=AF.Sigmoid, bias=pcol("b_gate_b", m), scale=1.0),
                   r=[PS[bb_], CONSTS], w=[SGB[mi]])
                op("dve", lambda e: e.tensor_tensor(out=t1[:], in0=psf(bc_), in1=sga[mi][:], op=ALU.mult), r=[PS[bc_], SGA[mi]], w=[T1])
                op("dve", lambda e: e.tensor_tensor(out=t2[:], in0=psf(bd_), in1=sgb[mi][:], op=ALU.mult), r=[PS[bd_], SGB[mi]], w=[T2])
                op("dve", lambda e: e.tensor_tensor(out=mgT[:, m, :], in0=t1[:], in1=t2[:], op=ALU.add), r=[T1, T2], w=[MG[m]])

            def acc_res(par, m, b):
                op("dve", lambda e: e.tensor_tensor(out=resT[par][:, m, :], in0=psf(b), in1=resT[par][:, m, :], op=ALU.add),
                   r=[PS[b], RES[par][m]], w=[RES[par][m]])

            def b3(c):
                par = c % 2
                for hf in range(2):
                    rg, RG = ring_next("wo")
                    wv = rg[:, :].rearrange("p (k c) -> p k c", k=8)
                    for mm in range(4):
                        m = 4 * hf + mm
                        b = gbank()
                        for kt in range(8):
                            op("pe", lambda e: e.matmul(psf(b), wv[:, kt, mm * 128:(mm + 1) * 128], mgT[:, kt, :],
                                                        start=(kt == 0), stop=(kt == 7)), r=[RG, MG[kt]], w=[PS[b]])
                        acc_res(par, m, b)

            sq = reluT[0]
            SQ = RL[0]
            ln_state = {}

            def ln_a1(par):
                for m in range(8):
                    op("act", lambda e: e.activation(out=hbT[par][:, m, :], in_=resT[par][:, m, :], func=AF.Copy),
                       r=[RES[par][m]], w=[HB[par][m]])
                    op("act", lambda e: e.activation(out=sq[:, m, :], in_=resT[par][:, m, :], func=AF.Square),
                       r=[RES[par][m]], w=[SQ[m]])

            def ln_a2(par):
                bm, bq = gbank(), gbank()
                for m in range(8):
                    op("pe", lambda e: e.matmul(psf(bm), onesd_b[:], hbT[par][:, m, :], start=(m == 0), stop=(m == 7)),
                       r=[CONSTS, HB[par][m]], w=[PS[bm]])
                for m in range(8):
                    op("pe", lambda e: e.matmul(psf(bq), onesd_b[:], sq[:, m, :], start=(m == 0), stop=(m == 7)),
                       r=[CONSTS, SQ[m]], w=[PS[bq]])
                op("act", lambda e: e.activation(out=mean_s[:], in_=psf(bm), func=AF.Copy), r=[PS[bm]], w=[LNB])
                op("dve", lambda e: e.tensor_tensor(out=var_s[:], in0=mean_s[:], in1=mean_s[:], op=ALU.mult), r=[LNB], w=[LNB])
                op("dve", lambda e: e.tensor_tensor(out=var_s[:], in0=psf(bq), in1=var_s[:], op=ALU.subtract), r=[PS[bq], LNB], w=[LNB])
                op("act", lambda e: e.activation(out=rstd_s[:], in_=var_s[:], func=AF.Ln, bias=1e-5, scale=1.0), r=[LNB], w=[LNB])
                op("act", lambda e: e.activation(out=rstd_s[:], in_=rstd_s[:], func=AF.Exp, scale=-0.5), r=[LNB], w=[LNB])
                op("dve", lambda e: e.scalar_tensor_tensor(out=nmr_s[:], in0=mean_s[:], scalar=-1.0, in1=rstd_s[:],
                                                           op0=ALU.mult, op1=ALU.mult), r=[LNB], w=[LNB])

            def ln_b(par, m, gname, bname, final):
                pp = m % 2
                nt = ntost[:, pp * 512:(pp + 1) * 512]
                op("dve", lambda e: e.tensor_tensor(out=nt, in0=resT[par][:, m, :], in1=rstd_s[:], op=ALU.mult),
                   r=[RES[par][m], LNB], w=[NT[pp]])
                op("dve", lambda e: e.tensor_tensor(out=nt, in0=nt, in1=nmr_s[:], op=ALU.add), r=[NT[pp], LNB], w=[NT[pp]])
                if final:
                    op("act", lambda e: e.activation(out=resT[par][:, m, :], in_=nt, func=AF.Identity,
                                                     bias=pcol(bname, m), scale=pcol(gname, m)),
                       r=[NT[pp], CONSTS], w=[RES[par][m]])
                else:
                    op("act", lambda e: e.activation(out=resT[par][:, m, :], in_=nt, func=AF.Identity,
                                                     bias=pcol(bname, m, True), scale=pcol(gname, m, True)),
                       r=[NT[pp], CONSTS], w=[RES[par][m]])
                    op("act", lambda e: e.activation(out=hbT[par][:, m, :], in_=nt, func=AF.Identity,
                                                     bias=pcol(bname, m), scale=pcol(gname, m)),
                       r=[NT[pp], CONSTS], w=[HB[par][m]])

            def ln_full(par, gname, bname, final, fill_pe, fill_ev):
                ln_a1(par)
                if len(fill_pe) > 0:
                    fill_pe[0]()
                ln_a2(par)
                for q_ in range(4):
                    ln_b(par, 2 * q_, gname, bname, final)
                    ln_b(par, 2 * q_ + 1, gname, bname, final)
                    if q_ < len(fill_ev):
                        fill_ev[q_]()
                    if q_ + 1 < len(fill_pe):
                        fill_pe[q_ + 1]()

            def ffn(c):
                par = c % 2
                for g in range(4):
                    rp_ = g % 2
                    if c + 1 < NCH:
                        if g == 0:
                            a1_s(c + 1, 0)
                            a1_s(c + 1, 1)
                        else:
                            a1_t(c + 1, g - 1)
                            if g + 1 < 4:
                                a1_s(c + 1, g + 1)
                    for hf in range(2):
                        rg, RG = ring_next("up")
                        wv = rg[:, :].rearrange("p (k c) -> p k c", k=8)
                        for m4 in range(4):
                            mm = 4 * hf + m4
                            b = gbank()
                            for kt in range(8):
                                op("pe", lambda e: e.matmul(psf(b), wv[:, kt, m4 * 128:(m4 + 1) * 128], hbT[par][:, kt, :],
                                                            start=(kt == 0), stop=(kt == 7)), r=[RG, HB[par][kt]], w=[PS[b]])
                            rq = mm % 2
                            op("act", lambda e: e.activation(out=rtmp[rq][:], in_=psf(b), func=AF.Relu), r=[PS[b]], w=[RT[rq]])
                            op("dve", lambda e: e.tensor_tensor(out=reluT[rp_][:, mm, :], in0=rtmp[rq][:], in1=rtmp[rq][:], op=ALU.mult),
                               r=[RT[rq]], w=[RL[rp_][mm]])
                    for hf in range(2):
                        rg, RG = ring_next("dn")
                        wv = rg[:, :].rearrange("p (k c) -> p k c", k=8)
                        for m4 in range(4):
                            m = 4 * hf + m4
                            b = gbank()
                            for mm in range(8):
                                op("pe", lambda e: e.matmul(psf(b), wv[:, mm, m4 * 128:(m4 + 1) * 128], reluT[rp_][:, mm, :],
                                                            start=(mm == 0), stop=(mm == 7)), r=[RG, RL[rp_][mm]], w=[PS[b]])
                            acc_res(par, m, b)
                if c + 1 < NCH:
                    a1_t(c + 1, 3)
                    a1_fin(c + 1)

            def ple_prep(c):
                bpt = gbank()
                for i in range(4):
                    ti = 4 * c + i
                    dma("sp", pst[:], din["p"][ti * 128:(ti + 1) * 128, :], s_p, w=[PST])
                    op("dve", lambda e: e.tensor_copy(out=pbf[:], in_=pst[:]), r=[PST], w=[PBF])
                    for kt in range(2):
                        op("pe", lambda e: e.transpose(psb(bpt)[:, kt * CH + i * 128:kt * CH + (i + 1) * 128],
                                                       pbf[:, kt * 128:(kt + 1) * 128], ident_b[:]),
                           r=[PBF, CONSTS], w=[PS[bpt]])
                op("dve", lambda e: e.tensor_copy(out=pT[:].rearrange("p k c -> p (k c)"), in_=psb(bpt)), r=[PS[bpt]], w=[PTT])

            def ple(c):
                par = c % 2
                rg2 = RG2 = None
                for hf in range(2):
                    rg, RG = ring_next("pg", live_prev=(1 if hf == 1 else 0))
                    wv = rg[:, :].rearrange("p (k c) -> p k c", k=8)
                    if hf == 0:
                        rg2, RG2 = ring_next("ple", live_prev=1)
                        wv2 = rg2[:, 0:2048].rearrange("p (k c) -> p k c", k=2)
                    for m4 in range(4):
                        m = 4 * hf + m4
                        bg_, bp_ = gbank(), gbank()
                        for kt in range(8):
                            op("pe", lambda e: e.matmul(psf(bg_), wv[:, kt, m4 * 128:(m4 + 1) * 128], hbT[par][:, kt, :],
                                                        start=(kt == 0), stop=(kt == 7)), r=[RG, HB[par][kt]], w=[PS[bg_]])
                        for kt in range(2):
                            op("pe", lambda e: e.matmul(psf(bp_), wv2[:, kt, m * 128:(m + 1) * 128], pT[:, kt, :],
                                                        start=(kt == 0), stop=(kt == 1)), r=[RG2, PTT], w=[PS[bp_]])
                        op("act", lambda e: e.activation(out=t1[:], in_=psf(bg_), func=AF.Sigmoid, bias=pcol("b_ple_gate", m), scale=1.0),
                           r=[PS[bg_], CONSTS], w=[T1])
                        op("dve", lambda e: e.tensor_tensor(out=t2[:], in0=psf(bp_), in1=t1[:], op=ALU.mult), r=[PS[bp_], T1], w=[T2])
                        op("dve", lambda e: e.tensor_tensor(out=resT[par][:, m, :], in0=t2[:], in1=resT[par][:, m, :], op=ALU.add),
                           r=[T2, RES[par][m]], w=[RES[par][m]])

            out_state = {}

            def out_pe(c, i):
                par = c % 2
                b0, b1 = gbank(), gbank()
                for ft in range(8):
                    bnk = b0 if ft < 4 else b1
                    op("pe", lambda e: e.transpose(ps_t[:, bnk, (ft % 4) * 128:(ft % 4 + 1) * 128],
                                                   resT[par][:, ft, i * 128:(i + 1) * 128], ident_f[:]),
                       r=[RES[par][ft], CONSTS], w=[PS[bnk]])
                out_state[(c, i)] = (b0, b1)

            def out_ev(c, i):
                b0, b1 = out_state.pop((c, i))
                ti = 4 * c + i
                op("act", lambda e: e.activation(out=ntost[:, 0:512], in_=psf(b0), func=AF.Copy), r=[PS[b0]], w=[NT[0]])
                op("dve", lambda e: e.tensor_copy(out=ntost[:, 512:1024], in_=psf(b1)), r=[PS[b1]], w=[NT[1]])
                dma("pool", out_d[ti * 128:(ti + 1) * 128, :], ntost[:], s_out, r=[NT[0], NT[1]])

            for i in range(4):
                a1_s(0, i) if i < 2 else None
            a1_t(0, 0)
            a1_s(0, 2)
            a1_t(0, 1)
            a1_s(0, 3)
            a1_t(0, 2)
            a1_t(0, 3)
            a1_fin(0)
            for m in range(8):
                a2_pe(0, m)
                a2_ev(0, m)
            for c in range(NCH):
                par = c % 2
                nxt = c + 1 < NCH
                b3(c)
                if stop_after == "B2" and c == 0:
                    pass
                if c > 0:
                    ln_full(par, "ln1_g", "ln1_b", False,
                            [(lambda i=i: out_pe(c - 1, i)) for i in range(4)],
                            [(lambda i=i: out_ev(c - 1, i)) for i in range(4)])
                else:
                    ln_full(par, "ln1_g", "ln1_b", False, [], [])
                ple_prep(c)
                ffn(c)
                if nxt:
                    ln_full(par, "ln2_g", "ln2_b", False,
                            [(lambda m=m: a2_pe(c + 1, m)) for m in range(4)],
                            [(lambda m=m: a2_ev(c + 1, m)) for m in range(4)])
                else:
                    ln_full(par, "ln2_g", "ln2_b", False, [], [])
                ple(c)
                if nxt:
                    ln_full(par, "ln3_g", "ln3_b", True,
                            [(lambda m=m: a2_pe(c + 1, m)) for m in range(4, 8)],
                            [(lambda m=m: a2_ev(c + 1, m)) for m in range(4, 8)])
                else:
                    ln_full(par, "ln3_g", "ln3_b", True, [], [])
            for i in range(4):
                out_pe(NCH - 1, i)
                out_ev(NCH - 1, i)
            kb.barrier()
        fb.close()
        return finish(nc, kb, out_d, dbg_outs, s_dbg)


def finish(nc, kb, out_d, dbg_outs, s_dbg):
    kb.barrier()
    nc._dbg_outs = dbg_outs
    nc._kb = kb
    return nc


def make_in_maps(inputs, cores):
    consts = host_consts()
    maps = []
    for b in cores:
        m = {"x": np.ascontiguousarray(inputs["x"][b]), "p": np.ascontiguousarray(inputs["p"][0, b]),
             "positions": np.ascontiguousarray(inputs["positions"][b]).reshape(1, S).astype(np.int32)}
        for k in W_SHAPES:
            a = np.asarray(inputs[k])
            if k not in ("ln_in_g", "ln_in_b"):
                a = a[0]
            m[k] = np.ascontiguousarray(a, dtype=np.float32)
        m.update(consts)
        m["c_pcols"] = pack_pcols(m)
        maps.append(m)
    return maps


def kernel(**inputs):
    nc = build_program()
    in_maps = make_in_maps(inputs, list(range(8)))
    res = run_bass_kernel_spmd(nc, in_maps, core_ids=list(range(8)))
    out = np.stack([np.asarray(r["out"]) for r in res.results], axis=0)
    return out.astype(np.float32)
```

```python
import contextlib
import math

import numpy as np

import concourse.bass as bass
import concourse.mybir as mybir
from concourse.bass_utils import run_bass_kernel_spmd

F32 = mybir.dt.float32
BF16 = mybir.dt.bfloat16
I32 = mybir.dt.int32
AF = mybir.ActivationFunctionType
ALU = mybir.AluOpType

S = 4096
D = 1024
NCH = 8
CH = 512
TC = 256
ALPHA = 2.0 ** 0.25
TWO_PI = 2.0 * math.pi
CW1 = 6.28125
CW2 = float(TWO_PI - 6.28125)
DUP_S = 1


class Tok:
    __slots__ = ("eng", "inst", "ms")

    def __init__(self, eng, inst):
        self.eng = eng
        self.inst = inst
        self.ms = None


class Buf:
    __slots__ = ("name", "w", "r")

    def __init__(self, name=""):
        self.name = name
        self.w = None
        self.r = {}


class DSem:
    def __init__(self, sem, name):
        self.sem = sem
        self.name = name
        self.val = 0


class KB:
    def __init__(self, nc, es):
        self.nc = nc
        self.es = es
        self.E = {"pe": nc.tensor, "act": nc.scalar, "dve": nc.vector, "pool": nc.gpsimd, "sp": nc.sync}
        self.sem = {e: es.enter_context(nc.semaphore("sem_" + e)) for e in ("pe", "act", "dve", "pool")}
        self.cnt = {e: 0 for e in self.sem}
        self.unflushed = {e: [] for e in self.sem}
        self.lasttok = {e: None for e in self.sem}
        self.seen = {e: {} for e in self.E}
        self.semname = {}
        self.dsems = []
        for e, s in self.sem.items():
            self.semname[id(s)] = "sem_" + e
        self.n_inst = 0
        self.n_wait = 0

    def dsem(self, name):
        s = self.es.enter_context(self.nc.semaphore(name))
        d = DSem(s, name)
        self.semname[id(s)] = name
        self.dsems.append(d)
        return d

    def _resolve(self, tok):
        if isinstance(tok, tuple):
            return tok
        if tok.ms is None:
            e = tok.eng
            self.cnt[e] += 1
            tok.inst.then_inc(self.sem[e], 1)
            lst = self.unflushed[e]
            i = lst.index(tok)
            for t in lst[: i + 1]:
                t.ms = self.cnt[e]
            self.unflushed[e] = lst[i + 1:]
        return (self.sem[tok.eng], tok.ms)

    def _need(self, eng, tok, waits, raw):
        if tok is None:
            return
        if not isinstance(tok, tuple) and tok.eng == eng and not raw:
            return
        if not isinstance(tok, tuple) and tok.eng == eng and eng == "pe":
            return
        sem, val = self._resolve(tok)
        key = self.semname[id(sem)]
        if self.seen[eng].get(key, 0) >= val:
            return
        self.seen[eng][key] = val
        waits.append((sem, val))

    def _deps(self, eng, r, w):
        waits = []
        for b in r:
            self._need(eng, b.w, waits, True)
        for b in w:
            self._need(eng, b.w, waits, False)
            for t in b.r.values():
                self._need(eng, t, waits, False)
        for (s, v) in waits:
            self.E[eng].wait_ge(s, v)
            self.n_wait += 1

    def op(self, eng, fn, r=(), w=()):
        self._deps(eng, r, w)
        inst = fn(self.E[eng])
        self.n_inst += 1
        tok = Tok(eng, inst)
        self.unflushed[eng].append(tok)
        self.lasttok[eng] = tok
        for b in r:
            if b not in w:
                b.r[eng] = tok
        for b in w:
            b.w = tok
            b.r = {}
        return tok

    def dma(self, q, out_ap, in_ap, dsem, r=(), w=(), **kw):
        self._deps(q, r, w)
        inst = self.E[q].dma_start(out=out_ap, in_=in_ap, **kw)
        dsem.val += 16
        inst.then_inc(dsem.sem, 16)
        tok = (dsem.sem, dsem.val)
        for b in r:
            b.r["dma:" + dsem.name] = tok
        for b in w:
            b.w = tok
            b.r = {}
        return tok

    def barrier(self):
        toks = []
        for e in self.sem:
            t = self.lasttok[e]
            if t is not None:
                toks.append(self._resolve(t))
        for d in self.dsems:
            if d.val > 0 and not d.name.startswith("d_cast"):
                toks.append((d.sem, d.val))
        for e in self.E:
            for (s, v) in toks:
                key = self.semname[id(s)]
                if self.seen[e].get(key, 0) >= v:
                    continue
                self.seen[e][key] = v
                self.E[e].wait_ge(s, v)


W_SHAPES = {
    "ln_in_g": [1024], "ln_in_b": [1024], "w_in": [1024, 3232], "b_gate": [2048], "q_norm_g": [384],
    "w_uq": [384, 768], "kv_norm_g": [256], "w_ukv": [256, 1024], "w_attn_br": [512, 1024],
    "a_re": [32, 64], "a_im": [32, 64], "log_dt": [32], "b_re": [32, 64, 16], "b_im": [32, 64, 16],
    "c_re": [32, 16, 64], "c_im": [32, 16, 64], "d_skip": [512], "w_glu": [512, 512], "b_glu": [512],
    "w_ssm_br": [512, 1024], "w_o": [1024, 1024], "ln1_g": [1024], "ln1_b": [1024],
    "w_up": [1024, 4096], "w_down": [4096, 1024], "ln2_g": [1024], "ln2_b": [1024],
    "w_ple_gate": [1024, 1024], "b_ple_gate": [1024], "w_ple": [256, 1024], "ln3_g": [1024], "ln3_b": [1024],
}


PCOL_LAYOUT = [("ln_in_g", 8), ("ln_in_b", 8), ("ln1_g", 8), ("ln1_b", 8), ("ln2_g", 8), ("ln2_b", 8),
               ("ln3_g", 8), ("ln3_b", 8), ("b_ple_gate", 8), ("b_glu", 4), ("d_skip", 4),
               ("q_norm_g", 3), ("kv_norm_g", 2), ("b_gate_a", 8), ("b_gate_b", 8)]


def pack_pcols(m):
    cols = []
    for name, n in PCOL_LAYOUT:
        if name == "b_gate_a":
            v = m["b_gate"][:1024]
        elif name == "b_gate_b":
            v = m["b_gate"][1024:]
        else:
            v = m[name]
        cols.append(np.asarray(v, np.float32).reshape(n, 128).T)
    return np.ascontiguousarray(np.concatenate(cols, axis=1))


def host_consts():
    ident = np.eye(128, dtype=np.float32)
    kk = np.arange(128)[:, None]
    qq = np.arange(128)[None, :]
    maskb = np.where(kk > qq, -30000.0, 0.0).astype(np.float32)
    iota = np.broadcast_to(np.arange(512, dtype=np.float32)[None, :], (128, 512)).copy()
    inv = (10000.0 ** (-np.arange(0, 32, 2, dtype=np.float32) / 32)).astype(np.float32)
    invf = np.zeros((128, 1), np.float32)
    for r in range(64, 96):
        invf[r, 0] = inv[(r - 64) % 16]
    return {"c_ident": ident, "c_maskb": maskb, "c_iota": iota, "c_invf": invf}


def build_program(stop_after=None, dbg=()):
    nc = bass.Bass("TRN2", target_bir_lowering=False)
    dbg = set(dbg)
    din = {}
    din["x"] = nc.dram_tensor("x", [S, D], F32, kind="ExternalInput").ap()
    din["p"] = nc.dram_tensor("p", [S, 256], F32, kind="ExternalInput").ap()
    din["positions"] = nc.dram_tensor("positions", [1, S], I32, kind="ExternalInput").ap()
    for k, shp in W_SHAPES.items():
        din[k] = nc.dram_tensor(k, shp, F32, kind="ExternalInput").ap()
    for k, v in host_consts().items():
        din[k] = nc.dram_tensor(k, list(v.shape), F32, kind="ExternalInput").ap()
    din["c_pcols"] = nc.dram_tensor("c_pcols", [128, 101], F32, kind="ExternalInput").ap()
    out_d = nc.dram_tensor("out", [S, D], F32, kind="ExternalOutput").ap()
    dbg_outs = {}

    mix_d = nc.dram_tensor("mix_d", [128, 8, 24, 128], BF16, kind="Internal").ap()
    wo_d = nc.dram_tensor("wo_d", [128, 8, 1024], BF16, kind="Internal").ap()
    wup_d = nc.dram_tensor("wup_d", [128, 4, 8, 1024], BF16, kind="Internal").ap()
    wdn_d = nc.dram_tensor("wdn_d", [128, 4, 8, 1024], BF16, kind="Internal").ap()
    wpg_d = nc.dram_tensor("wpg_d", [128, 8, 1024], BF16, kind="Internal").ap()
    wple_d = nc.dram_tensor("wple_d", [128, 2, 1024], BF16, kind="Internal").ap()

    with contextlib.ExitStack() as es:
        kb = KB(nc, es)
        op, dma = kb.op, kb.dma

        sb_ctr = [0]

        def sb(stack, name, shape, dt=F32):
            sb_ctr[0] += 1
            return stack.enter_context(nc.sbuf_tensor("%s_%d" % (name, sb_ctr[0]), shape, dt))

        ps_t = es.enter_context(nc.psum_tensor("ps", [128, 8, 512], F32))
        PS = [Buf("ps%d" % i) for i in range(8)]

        def psf(i):
            return ps_t[:, i, :]

        def psb(i):
            return ps_t[:, i, :].bitcast(BF16)

        ident_f = sb(es, "ident_f", [128, 128])
        ident_b = sb(es, "ident_b", [128, 128], BF16)
        maskb_f = sb(es, "maskb_f", [128, 128])
        maskb_b = sb(es, "maskb_b", [128, 128], BF16)
        ones_f = sb(es, "ones_f", [128, 128])
        ones_b = sb(es, "ones_b", [128, 128], BF16)
        onesd_b = sb(es, "onesd_b", [128, 128], BF16)
        iota_f = sb(es, "iota_f", [128, 512])
        invf = sb(es, "invf", [128, 1])
        NPC = 101
        pc = sb(es, "pcols", [128, NPC])
        pcs = sb(es, "pcols_s", [128, NPC])
        CONSTS = Buf("consts")
        s_setup = kb.dsem("d_setup")
        col = {}
        off = 0
        with nc.allow_non_contiguous_dma(reason="tiny parameter vectors"):
            dma("sp", ident_f[:], din["c_ident"][:, :], s_setup, w=[CONSTS])
            dma("act", maskb_f[:], din["c_maskb"][:, :], s_setup, w=[CONSTS])
            dma("act", iota_f[:], din["c_iota"][:, :], s_setup, w=[CONSTS])
            dma("act", invf[:], din["c_invf"][:, :], s_setup, w=[CONSTS])
            dma("sp", pc[:], din["c_pcols"][:, :], s_setup, w=[CONSTS])
            for name, n in PCOL_LAYOUT:
                col[name] = off
                off += n
        assert off == NPC, (off, NPC)
        op("dve", lambda e: e.tensor_copy(out=ident_b[:], in_=ident_f[:]), r=[CONSTS], w=[CONSTS])
        op("dve", lambda e: e.tensor_copy(out=maskb_b[:], in_=maskb_f[:]), r=[CONSTS], w=[CONSTS])
        op("dve", lambda e: e.memset(ones_f[:], 1.0), w=[CONSTS])
        op("dve", lambda e: e.memset(ones_b[:], 1.0), w=[CONSTS])
        op("dve", lambda e: e.memset(onesd_b[:], 1.0 / 1024.0), w=[CONSTS])
        op("dve", lambda e: e.tensor_scalar(out=pcs[:], in0=pc[:], scalar1=ALPHA, scalar2=None, op0=ALU.mult),
           r=[CONSTS], w=[CONSTS])
        qg = col["q_norm_g"]
        op("dve", lambda e: e.tensor_scalar(out=pc[:, qg:qg + 3], in0=pc[:, qg:qg + 3], scalar1=96.0 ** -0.5,
                                            scalar2=None, op0=ALU.mult), r=[CONSTS], w=[CONSTS])

        def pcol(name, m=0, scaled=False):
            t = pcs if scaled else pc
            c = col[name] + m
            return t[:, c:c + 1]

        s_cast = kb.dsem("d_cast")
        CAST = Buf("cast")
        win_v = din["w_in"].rearrange("(kt p) c -> p kt c", p=128)

        def emit_casts():
            wat_v = din["w_attn_br"].rearrange("(kt p) c -> p kt c", p=128)
            wss_v = din["w_ssm_br"].rearrange("(kt p) c -> p kt c", p=128)
            for m in range(8):
                for half in range(2):
                    base = 1184 + half * 1024 + m * 128
                    dma("pool", mix_d[:, m, 8 * half:8 * half + 8, :], win_v[:, :, base:base + 128], s_cast, w=[CAST])
                dma("pool", mix_d[:, m, 16:20, :], wat_v[:, :, m * 128:(m + 1) * 128], s_cast, w=[CAST])
                dma("pool", mix_d[:, m, 20:24, :], wss_v[:, :, m * 128:(m + 1) * 128], s_cast, w=[CAST])
            dma("pool", wo_d[:, :, :], din["w_o"].rearrange("(kt p) c -> p kt c", p=128), s_cast, w=[CAST])
            for g in range(4):
                dma("pool", wup_d[:, g, :, :],
                    din["w_up"].rearrange("(kt p) c -> p kt c", p=128)[:, :, g * 1024:(g + 1) * 1024], s_cast, w=[CAST])
                dma("pool", wdn_d[:, g, :, :],
                    din["w_down"][g * 1024:(g + 1) * 1024, :].rearrange("(kk p) c -> p kk c", p=128), s_cast, w=[CAST])
            dma("pool", wpg_d[:, :, :], din["w_ple_gate"].rearrange("(kt p) c -> p kt c", p=128), s_cast, w=[CAST])
            dma("pool", wple_d[:, :, :], din["w_ple"].rearrange("(kt p) c -> p kt c", p=128), s_cast, w=[CAST])

        def sincos(stack_name, ang_ap_fn, shape, bufs, out_sin, out_cos, tmp):
            ANG, TMP, OUTB = bufs
            for outap, shift in ((out_sin, 0.0), (out_cos, math.pi / 2)):
                if outap is None:
                    continue
                a, q, qi = tmp["a"], tmp["q"], tmp["qi"]
                op("dve", lambda e: e.tensor_scalar(out=a, in0=ang_ap_fn(), scalar1=float(shift), scalar2=None, op0=ALU.add),
                   r=[ANG], w=[TMP])
                op("dve", lambda e: e.tensor_scalar(out=q, in0=a, scalar1=1.0 / TWO_PI, scalar2=None, op0=ALU.mult),
                   r=[TMP], w=[TMP])
                op("dve", lambda e: e.tensor_copy(out=qi, in_=q), r=[TMP], w=[TMP])
                op("dve", lambda e: e.tensor_copy(out=q, in_=qi), r=[TMP], w=[TMP])
                op("dve", lambda e: e.scalar_tensor_tensor(out=a, in0=q, scalar=-CW1, in1=a, op0=ALU.mult, op1=ALU.add),
                   r=[TMP], w=[TMP])
                op("dve", lambda e: e.scalar_tensor_tensor(out=a, in0=q, scalar=-CW2, in1=a, op0=ALU.mult, op1=ALU.add),
                   r=[TMP], w=[TMP])
                op("dve", lambda e: e.tensor_scalar(out=a, in0=a, scalar1=math.pi, scalar2=-math.pi, op0=ALU.min, op1=ALU.max),
                   r=[TMP], w=[TMP])
                op("act", lambda e: e.activation(out=outap, in_=a, func=AF.Sin), r=[TMP], w=[OUTB])

        XS = [Buf("xs%d" % i) for i in range(4)]
        s_x = [kb.dsem("d_x%d" % i) for i in range(4)]
        LNS = [Buf("lns%d" % i) for i in range(4)]

        def ln_in_tile(ti, xs_t, st_t, mv_t, rs_t, slot, g_scaled):
            dma("sp", xs_t[slot][:], din["x"][ti * 128:(ti + 1) * 128, :], s_x[slot], w=[XS[slot]])
            L = LNS[slot]
            op("dve", lambda e: e.bn_stats(out=st_t[slot][:, 0, :], in_=xs_t[slot][:, 0:512]), r=[XS[slot]], w=[L])
            op("dve", lambda e: e.bn_stats(out=st_t[slot][:, 1, :], in_=xs_t[slot][:, 512:1024]), r=[XS[slot]], w=[L])
            op("dve", lambda e: e.bn_aggr(out=mv_t[slot][:], in_=st_t[slot][:].rearrange("p a b -> p (a b)")), r=[L], w=[L])
            op("act", lambda e: e.activation(out=rs_t[slot][:], in_=mv_t[slot][:, 1:2], func=AF.Ln, bias=1e-5, scale=1.0),
               r=[L], w=[L])
            op("act", lambda e: e.activation(out=rs_t[slot][:], in_=rs_t[slot][:], func=AF.Exp, scale=-0.5), r=[L], w=[L])
            op("dve", lambda e: e.tensor_scalar(out=xs_t[slot][:], in0=xs_t[slot][:], scalar1=mv_t[slot][:, 0:1],
                                                scalar2=rs_t[slot][:, 0:1], op0=ALU.subtract, op1=ALU.mult),
               r=[L, XS[slot]], w=[XS[slot]])

        def ln_transpose_chunk(c, xs_t, st_t, mv_t, rs_t, banksets, evac, as_steps=False):
            nsl = len(xs_t)

            def stage_s(i):
                ti = 4 * c + i
                ln_in_tile(ti, xs_t, st_t, mv_t, rs_t, ti % nsl, False)

            def stage_t(i):
                ti = 4 * c + i
                slot = ti % nsl
                pa = banksets[ti % len(banksets)]
                for ft in range(8):
                    bank = pa + ft // 4
                    op("pe", lambda e: e.transpose(ps_t[:, bank, (ft % 4) * 128:(ft % 4 + 1) * 128],
                                                   xs_t[slot][:, ft * 128:(ft + 1) * 128], ident_f[:]),
                       r=[XS[slot], CONSTS], w=[PS[bank]])
                for ft in range(8):
                    bank = pa + ft // 4
                    evac(i, ft, ps_t[:, bank, (ft % 4) * 128:(ft % 4 + 1) * 128], PS[bank])
            steps_ = [lambda: stage_s(0), lambda: stage_s(1), lambda: stage_t(0), lambda: stage_s(2),
                      lambda: stage_t(1), lambda: stage_s(3), lambda: stage_t(2), lambda: stage_t(3)]
            if as_steps:
                return steps_
            for st_ in steps_:
                st_()

        def dbg_dump(name, ap, buf):
            if name not in dbg:
                return
            shp = list(ap.shape)
            d = nc.dram_tensor("dbg_" + name, shp, ap.dtype, kind="ExternalOutput").ap()
            dbg_outs[name] = d
            idx = tuple(slice(None) for _ in shp)
            dma("sp", d[idx], ap, s_dbg, r=[buf])

        s_dbg = kb.dsem("d_dbg")

        attnT = sb(es, "attnT", [128, 4, S], BF16)
        ATT = [Buf("att%d" % c) for c in range(NCH)]


        f12 = contextlib.ExitStack()
        cqnT = sb(f12, "cqnT", [128, 3, S], BF16)
        ckvnT = sb(f12, "ckvnT", [128, 2, S], BF16)
        krT = sb(f12, "krT", [128, S], BF16)
        cosT = sb(f12, "cosT", [128, S], BF16)
        sinT = sb(f12, "sinT", [128, S], BF16)
        CQN = [Buf("cqn%d" % c) for c in range(NCH)]
        CKV = [Buf("ckv%d" % c) for c in range(NCH)]
        KR = [Buf("kr%d" % c) for c in range(NCH)]
        ROPE = Buf("rope")
        wuq = sb(f12, "wuq", [128, 3, 768], BF16)
        wuqr = sb(f12, "wuqr", [128, 3, 768], BF16)
        wukv = sb(f12, "wukv", [128, 2, 1024], BF16)
        WATT = Buf("watt")
        s_w2 = kb.dsem("d_w2")

        with contextlib.ExitStack() as f1:
            wina = sb(f1, "wina", [128, 8, 672], BF16)
            wkr = sb(f1, "wkr", [128, 8, 96], BF16)
            wkrr = sb(f1, "wkrr", [128, 8, 96], BF16)
            WINA = Buf("wina")
            s_w1 = kb.dsem("d_w1")
            dma("pool", wina[:], win_v[:, :, 0:672], s_w1, w=[WINA])
            dma("pool", wuq[:], din["w_uq"].rearrange("(kt p) c -> p kt c", p=128), s_w2, w=[WATT])
            dma("pool", wukv[:], din["w_ukv"].rearrange("(kt p) c -> p kt c", p=128), s_w2, w=[WATT])
            emit_casts()
            op("dve", lambda e: e.memset(wkr[:], 0.0), w=[WINA])
            op("dve", lambda e: e.memset(wkrr[:], 0.0), w=[WINA])
            op("dve", lambda e: e.tensor_copy(out=wkr[:, :, 64:96], in_=wina[:, :, 640:672]), r=[WINA], w=[WINA])
            op("dve", lambda e: e.tensor_scalar(out=wkrr[:, :, 64:80], in0=wina[:, :, 656:672], scalar1=-1.0, scalar2=None,
                                                op0=ALU.mult), r=[WINA], w=[WINA])
            op("dve", lambda e: e.tensor_copy(out=wkrr[:, :, 80:96], in_=wina[:, :, 640:656]), r=[WINA], w=[WINA])

            xs_t = [sb(f1, "xs%d" % i_, [128, 1024]) for i_ in range(4)]
            st_t = [sb(f1, "st%d" % i_, [128, 2, 6]) for i_ in range(4)]
            mv_t = [sb(f1, "mv%d" % i_, [128, 2]) for i_ in range(4)]
            rs_t = [sb(f1, "rs%d" % i_, [128, 1]) for i_ in range(4)]
            h0T = [sb(f1, "h0T0", [128, 8, CH], BF16), sb(f1, "h0T1", [128, 8, CH], BF16)]
            H0T = [Buf("h0T0"), Buf("h0T1")]
            cqc = sb(f1, "cqc", [128, 5, CH], BF16)
            sqc = sb(f1, "sqc", [128, 5, CH], BF16)
            CQC = [Buf("cqc%d" % m) for m in range(5)]
            SQC = [Buf("sqc%d" % m) for m in range(5)]
            rsq = [sb(f1, "rsq0", [128, CH]), sb(f1, "rsq1", [128, CH])]
            RSQ = [Buf("rsq0"), Buf("rsq1")]
            t1 = sb(f1, "f1t1", [128, CH])
            t2 = sb(f1, "f1t2", [128, CH])
            T12 = Buf("f1t12")
            psrot = [0]

            def next_ps():
                i = 4 + (psrot[0] % 4)
                psrot[0] += 1
                return i

            def f1_ln(c):
                hb = c % 2

                def evac_f1(i, ft, pap, PB):
                    op("act", lambda e: e.activation(out=h0T[hb][:, ft, i * 128:(i + 1) * 128], in_=pap,
                                                     func=AF.Identity, bias=pcol("ln_in_b", ft), scale=pcol("ln_in_g", ft)),
                       r=[PB, CONSTS], w=[H0T[hb]])
                ln_transpose_chunk(c, xs_t, st_t, mv_t, rs_t, [0, 2], evac_f1)

            def f1_proj(c):
                hb = c % 2
                for m in range(5):
                    b = next_ps()
                    for kt in range(8):
                        op("pe", lambda e: e.matmul(psf(b), wina[:, kt, m * 128:(m + 1) * 128], h0T[hb][:, kt, :],
                                                    start=(kt == 0), stop=(kt == 7)), r=[WINA, H0T[hb]], w=[PS[b]])
                    op("act", lambda e: e.activation(out=cqc[:, m, :], in_=psf(b), func=AF.Copy), r=[PS[b]], w=[CQC[m]])
                    op("dve", lambda e: e.tensor_tensor(out=sqc[:, m, :], in0=cqc[:, m, :], in1=cqc[:, m, :], op=ALU.mult),
                       r=[CQC[m]], w=[SQC[m]])
                ba = next_ps()
                for kt in range(8):
                    op("pe", lambda e: e.matmul(ps_t[0:96, ba, :], wkr[:, kt, :], h0T[hb][:, kt, :], start=(kt == 0), stop=(kt == 7)),
                       r=[WINA, H0T[hb]], w=[PS[ba]])
                bb = next_ps()
                for kt in range(8):
                    op("pe", lambda e: e.matmul(ps_t[0:96, bb, :], wkrr[:, kt, :], h0T[hb][:, kt, :], start=(kt == 0), stop=(kt == 7)),
                       r=[WINA, H0T[hb]], w=[PS[bb]])
                return ba, bb

            def f1_tail(c, ba, bb):
                csl = slice(c * CH, (c + 1) * CH)
                op("dve", lambda e: e.tensor_tensor(out=t1[64:96, :], in0=ps_t[64:96, ba, :], in1=cosT[64:96, csl], op=ALU.mult),
                   r=[PS[ba], ROPE], w=[T12])
                op("dve", lambda e: e.tensor_tensor(out=t2[64:96, :], in0=ps_t[64:96, bb, :], in1=sinT[64:96, csl], op=ALU.mult),
                   r=[PS[bb], ROPE], w=[T12])
                op("dve", lambda e: e.tensor_tensor(out=krT[64:96, csl], in0=t1[64:96, :], in1=t2[64:96, :], op=ALU.add),
                   r=[T12], w=[KR[c]])
                for which, (m0, m1, nfeat, eps, gname, dstT, DST) in enumerate(
                        [(0, 3, 384.0, 1e-6, "q_norm_g", cqnT, CQN), (3, 5, 256.0, 1e-6, "kv_norm_g", ckvnT, CKV)]):
                    b = next_ps()
                    for m in range(m0, m1):
                        op("pe", lambda e: e.matmul(psf(b), ones_b[:], sqc[:, m, :], start=(m == m0), stop=(m == m1 - 1)),
                           r=[SQC[m], CONSTS], w=[PS[b]])
                    op("act", lambda e: e.activation(out=rsq[which][:], in_=psf(b), func=AF.Ln, bias=float(eps),
                                                     scale=1.0 / nfeat), r=[PS[b]], w=[RSQ[which]])
                    op("act", lambda e: e.activation(out=rsq[which][:], in_=rsq[which][:], func=AF.Exp, scale=-0.5),
                       r=[RSQ[which]], w=[RSQ[which]])
                    for m in range(m0, m1):
                        op("dve", lambda e: e.scalar_tensor_tensor(out=dstT[:, m - m0, csl], in0=cqc[:, m, :],
                                                                   scalar=pcol(gname, m - m0), in1=rsq[which][:],
                                                                   op0=ALU.mult, op1=ALU.mult),
                           r=[CQC[m], RSQ[which], CONSTS], w=[DST[c]])

            f1_ln(0)
            if True:
                rt = f1
                posi = sb(rt, "posi", [128, 1024], I32)
                angf = sb(rt, "angf", [128, 1024])
                ta = sb(rt, "rp_a", [128, 1024])
                tq = sb(rt, "rp_q", [128, 1024])
                tqi = sb(rt, "rp_qi", [128, 1024], I32)
                POS, ANG, TMP = Buf("pos"), Buf("ang"), Buf("rptmp")
                s_pos = kb.dsem("d_pos")
                for cc in range(4):
                    dma("sp", posi[:], din["positions"][:, cc * 1024:(cc + 1) * 1024].partition_broadcast(128), s_pos, w=[POS])
                    op("dve", lambda e: e.tensor_copy(out=angf[:], in_=posi[:]), r=[POS], w=[ANG])
                    op("dve", lambda e: e.tensor_scalar(out=angf[:], in0=angf[:], scalar1=invf[:, 0:1], scalar2=None,
                                                        op0=ALU.mult), r=[ANG, CONSTS], w=[ANG])
                    sincos("rope", lambda: angf[:], None, (ANG, TMP, ROPE),
                           sinT[:, cc * 1024:(cc + 1) * 1024], cosT[:, cc * 1024:(cc + 1) * 1024],
                           {"a": ta[:], "q": tq[:], "qi": tqi[:]})

            for c in range(NCH):
                ba, bb = f1_proj(c)
                if c + 1 < NCH:
                    f1_ln(c + 1)
                f1_tail(c, ba, bb)
            if stop_after == "F1":
                dbg_dump("cqnT", cqnT[:], CQN[NCH - 1])
                dbg_dump("ckvnT", ckvnT[:], CKV[NCH - 1])
                dbg_dump("krT", krT[64:96, :], KR[NCH - 1])
            kb.barrier()

        if stop_after == "F1":
            f12.close()
            return finish(nc, kb, out_d, dbg_outs, s_dbg)

        with contextlib.ExitStack() as f2:
            op("dve", lambda e: e.memset(wuqr[:], 0.0), w=[WATT])
            wuq4 = wuq[:].rearrange("p k (h d) -> p k h d", d=96)
            wuqr4 = wuqr[:].rearrange("p k (h d) -> p k h d", d=96)
            for kt in range(3):
                op("dve", lambda e: e.tensor_scalar(out=wuqr4[:, kt, :, 64:80], in0=wuq4[:, kt, :, 80:96], scalar1=-1.0,
                                                    scalar2=None, op0=ALU.mult), r=[WATT], w=[WATT])
                op("dve", lambda e: e.tensor_copy(out=wuqr4[:, kt, :, 80:96], in_=wuq4[:, kt, :, 64:80]), r=[WATT], w=[WATT])
            qT = [sb(f2, "qT0", [128, S], BF16), sb(f2, "qT1", [128, S], BF16)]
            kT = [sb(f2, "kT0", [128, S], BF16), sb(f2, "kT1", [128, S], BF16)]
            Vb = [sb(f2, "Vb0", [128, 32, 128], BF16), sb(f2, "Vb1", [128, 32, 128], BF16)]
            NPT = 4
            PT = [sb(f2, "PT%d" % i, [128, CH], BF16) for i in range(NPT)]
            rec = sb(f2, "rec", [128, CH])
            bcs = sb(f2, "bcs", [128, CH])
            t1 = sb(f2, "f2t1", [128, CH])
            t2 = sb(f2, "f2t2", [128, CH])
            QTn = [[Buf() for _ in range(NCH)] for _ in range(2)]
            QTr = [[Buf() for _ in range(NCH)] for _ in range(2)]
            KTn = [[Buf() for _ in range(NCH)] for _ in range(2)]
            KTr = [[Buf() for _ in range(NCH)] for _ in range(2)]
            VB = [[Buf() for _ in range(4)] for _ in range(2)]
            PTB = [Buf() for _ in range(NPT)]
            REC, BCS, T1B, T2B = Buf(), Buf(), Buf(), Buf()
            VONES = [Buf(), Buf()]
            op("dve", lambda e: e.memset(Vb[0][:, :, 64:128], 1.0), w=[VONES[0]])
            op("dve", lambda e: e.memset(Vb[1][:, :, 0:64], 1.0), w=[VONES[1]])
            prep_rot = [0]

            def prep_bank():
                b = 6 + (prep_rot[0] % 2)
                prep_rot[0] += 1
                return b

            def head_prep_pieces(h):
                hb = h % 2
                voff = 0 if hb == 0 else 64
                pieces_ = []

                def q_a(tc):
                    csl = slice(tc * CH, (tc + 1) * CH)
                    ba = prep_bank()
                    for kt in range(3):
                        op("pe", lambda e: e.matmul(ps_t[0:96, ba, :], wuq[:, kt, 96 * h:96 * h + 96], cqnT[:, kt, csl],
                                                    start=(kt == 0), stop=(kt == 2)), r=[WATT, CQN[tc]], w=[PS[ba]])
                    op("dve", lambda e: e.tensor_copy(out=qT[hb][0:64, csl], in_=ps_t[0:64, ba, :]), r=[PS[ba]], w=[QTn[hb][tc]])
                    op("dve", lambda e: e.tensor_tensor(out=t1[64:96, :], in0=ps_t[64:96, ba, :], in1=cosT[64:96, csl], op=ALU.mult),
                       r=[PS[ba], ROPE], w=[T1B])

                def q_b(tc):
                    csl = slice(tc * CH, (tc + 1) * CH)
                    bb = prep_bank()
                    for kt in range(3):
                        op("pe", lambda e: e.matmul(ps_t[0:96, bb, :], wuqr[:, kt, 96 * h:96 * h + 96], cqnT[:, kt, csl],
                                                    start=(kt == 0), stop=(kt == 2)), r=[WATT, CQN[tc]], w=[PS[bb]])
                    op("dve", lambda e: e.tensor_tensor(out=t2[64:96, :], in0=ps_t[64:96, bb, :], in1=sinT[64:96, csl], op=ALU.mult),
                       r=[PS[bb], ROPE], w=[T2B])
                    op("dve", lambda e: e.tensor_tensor(out=qT[hb][64:96, csl], in0=t1[64:96, :], in1=t2[64:96, :], op=ALU.add),
                       r=[T1B, T2B], w=[QTr[hb][tc]])

                def k_c(tc):
                    csl = slice(tc * CH, (tc + 1) * CH)
                    bk = prep_bank()
                    for kt in range(2):
                        op("pe", lambda e: e.matmul(ps_t[0:64, bk, :], wukv[:, kt, 128 * h:128 * h + 64], ckvnT[:, kt, csl],
                                                    start=(kt == 0), stop=(kt == 1)), r=[WATT, CKV[tc]], w=[PS[bk]])
                    op("dve", lambda e: e.tensor_copy(out=kT[hb][0:64, csl], in_=ps_t[0:64, bk, :]), r=[PS[bk]], w=[KTn[hb][tc]])
                    op("dve", lambda e: e.tensor_copy(out=kT[hb][64:96, csl], in_=krT[64:96, csl]), r=[KR[tc]], w=[KTr[hb][tc]])

                def v_half(tg, hf):
                    bv = prep_bank()
                    for t4 in range(4):
                        ti = tg * 8 + hf * 4 + t4
                        for kt in range(2):
                            op("pe", lambda e: e.matmul(ps_t[:, bv, t4 * 64:(t4 + 1) * 64], ckvnT[:, kt, ti * 128:(ti + 1) * 128],
                                                        wukv[:, kt, 128 * h + 64:128 * h + 128], start=(kt == 0), stop=(kt == 1)),
                               r=[WATT, CKV[ti // 4]], w=[PS[bv]])
                    t0_ = tg * 8 + hf * 4
                    op("dve", lambda e: e.tensor_copy(out=Vb[hb][:, t0_:t0_ + 4, voff:voff + 64],
                                                      in_=ps_t[:, bv, 0:256].rearrange("p (a b) -> p a b", b=64)),
                       r=[PS[bv]], w=[VB[hb][tg]])
                for tc in range(NCH):
                    pieces_.append(lambda tc=tc: q_a(tc))
                    pieces_.append(lambda tc=tc: q_b(tc))
                    pieces_.append(lambda tc=tc: k_c(tc))
                    if tc % 2 == 1:
                        tg = tc // 2
                        pieces_.append(lambda tg=tg: v_half(tg, 0))
                        pieces_.append(lambda tg=tg: v_half(tg, 1))
                return pieces_

            sc_rot = [0]
            pt_rot = [0]

            def head_flash(h, nxt_pieces):
                hb = h % 2
                prow = 64 if hb == 0 else 0
                orow = 0 if hb == 0 else 64
                items = [(qc, kt) for qc in range(NCH) for kt in range(4 * qc + 4)]
                state = {}

                def emit_S(idx):
                    qc, kt = items[idx]
                    j = kt - 4 * qc
                    col0 = 128 * j if j > 0 else 0
                    b = sc_rot[0] % 3
                    sc_rot[0] += 1
                    state[idx] = (b, col0)
                    for rep in range(DUP_S if j < 0 else 1):
                        op("pe", lambda e: e.matmul(ps_t[:, b, col0:CH], kT[hb][0:96, kt * 128:(kt + 1) * 128],
                                                    qT[hb][0:96, qc * CH + col0:(qc + 1) * CH], start=True, stop=(j < 0)),
                           r=[KTn[hb][kt // 4], KTr[hb][kt // 4], QTn[hb][qc], QTr[hb][qc]], w=[PS[b]])
                    if j >= 0:
                        op("pe", lambda e: e.matmul(ps_t[:, b, col0:col0 + 128], ident_b[:], maskb_b[:], start=False, stop=True),
                           r=[CONSTS], w=[PS[b]])

                deferred = []

                def emit_norm_a(qc, acc):
                    op("dve", lambda e: e.reciprocal(out=rec[prow:prow + 1, :], in_=ps_t[prow:prow + 1, acc, :]), r=[PS[acc]], w=[REC])

                def emit_norm_b(qc, acc):
                    op("pe", lambda e: e.matmul(psf(5), ones_f[prow:prow + 1, :], rec[prow:prow + 1, :], start=True, stop=True),
                       r=[REC, CONSTS], w=[PS[5]])
                    op("dve", lambda e: e.tensor_copy(out=bcs[orow:orow + 64, :], in_=ps_t[orow:orow + 64, 5, :]),
                       r=[PS[5]], w=[BCS])
                    op("dve", lambda e: e.tensor_tensor(out=attnT[orow:orow + 64, h // 2, qc * CH:(qc + 1) * CH],
                                                        in0=ps_t[orow:orow + 64, acc, :], in1=bcs[orow:orow + 64, :], op=ALU.mult),
                       r=[PS[acc], BCS], w=[ATT[qc]])

                n = len(items)
                emit_S(0)
                if n > 1:
                    emit_S(1)
                for idx in range(n):
                    qc, kt = items[idx]
                    nk = 4 * qc + 4
                    acc = 3 + (qc % 2)
                    b, col0 = state.pop(idx)
                    pi = pt_rot[0] % NPT
                    pt_rot[0] += 1
                    op("act", lambda e: e.activation(out=PT[pi][:, col0:CH], in_=ps_t[:, b, col0:CH], func=AF.Exp),
                       r=[PS[b]], w=[PTB[pi]])
                    if idx + 2 < n:
                        emit_S(idx + 2)
                    op("pe", lambda e: e.matmul(ps_t[:, acc, col0:CH], Vb[hb][:, kt, :], PT[pi][:, col0:CH],
                                                start=(kt == 0), stop=(kt == nk - 1)),
                       r=[VB[hb][kt // 8], VONES[hb], PTB[pi]], w=[PS[acc]])
                    for d in list(deferred):
                        d[0] -= 1
                        if d[0] <= 0:
                            emit_norm_b(d[1], d[2])
                            deferred.remove(d)
                    if kt == nk - 1:
                        emit_norm_a(qc, acc)
                        deferred.append([6, qc, acc])
                    if nxt_pieces and idx % 9 in (2, 6):
                        nxt_pieces.pop(0)()
                for d in deferred:
                    emit_norm_b(d[1], d[2])
                while nxt_pieces:
                    nxt_pieces.pop(0)()

            for p_ in head_prep_pieces(0):
                p_()
            for h in range(8):
                head_flash(h, head_prep_pieces(h + 1) if h + 1 < 8 else [])
            if stop_after == "F2":
                dbg_dump("attnT", attnT[:], ATT[NCH - 1])
            kb.barrier()
        f12.close()
        if stop_after == "F2":
            return finish(nc, kb, out_d, dbg_outs, s_dbg)

        fb = es.enter_context(contextlib.ExitStack())
        ygT = sb(fb, "ygT", [128, 4, S], BF16)
        YG = [Buf("yg%d" % c) for c in range(NCH)]
        with contextlib.ExitStack() as f3:
            s_w3 = kb.dsem("d_w3")
            s_s5 = kb.dsem("d_s5")
            winu = sb(f3, "winu", [128, 8, 512], BF16)
            wglu = sb(f3, "wglu", [128, 4, 512], BF16)
            W3 = Buf("w3")
            dma("pool", winu[:], win_v[:, :, 672:1184], s_w3, w=[W3])
            dma("pool", wglu[:], din["w_glu"].rearrange("(kt p) c -> p kt c", p=128), s_w3, w=[W3])
            cosTab = sb(f3, "cosTab", [128, 16, TC], BF16)
            sinTab = sb(f3, "sinTab", [128, 16, TC], BF16)
            WB = sb(f3, "WB", [128, 16, 2, 128], BF16)
            WA = sb(f3, "WA", [128, 16, 2, 128], BF16)
            LC = sb(f3, "LC", [128, 16, 2, 128], BF16)
            LCa = sb(f3, "LCa", [128, 16, 2, 128], BF16)
            Dsk = sb(f3, "Dsk", [128, 4, 128], BF16)
            K0D = sb(f3, "K0D", [128, 4, 128], BF16)
            rdec = sb(f3, "rdec", [128, 16])
            Er = sb(f3, "Er", [128, 16])
            Ei = sb(f3, "Ei", [128, 16])
            S5C = Buf("s5c")
            with contextlib.ExitStack() as sp_:
                Are = sb(sp_, "Are", [128, 16])
                Aim = sb(sp_, "Aim", [128, 16])
                Ldt = sb(sp_, "Ldt", [128, 16])
                Bre = sb(sp_, "Bre", [128, 16, 16])
                Bim = sb(sp_, "Bim", [128, 16, 16])
                Cin = [sb(sp_, "Cin_re", [128, 2, 2, 64]), sb(sp_, "Cin_im", [128, 2, 2, 64])]
                Csm = [sb(sp_, "Csm_re", [128, 16, 16]), sb(sp_, "Csm_im", [128, 16, 16])]
                BP = sb(sp_, "BP", [128, 16, 2, 128])
                PRM = Buf("s5prm")
                with nc.allow_non_contiguous_dma(reason="small S5 parameter loads"):
                    qs = ["sp", "act"]
                    qi_ = [0]

                    def pdma(dst, src):
                        q_ = qs[qi_[0] % 2]
                        qi_[0] += 1
                        dma(q_, dst, src, s_s5, w=[PRM])
                    Ain = [sb(sp_, "Ain_re", [16, 2, 64]), sb(sp_, "Ain_im", [16, 2, 64])]
                    Bin = [sb(sp_, "Bin_re", [16, 2, 64, 16]), sb(sp_, "Bin_im", [16, 2, 64, 16])]
                    PRM2 = Buf("s5prm2")
                    for ri, (na, nb_) in enumerate([("a_re", "b_re"), ("a_im", "b_im")]):
                        dma(qs[ri], Ain[ri][:], din[na].rearrange("(j two) p -> j two p", two=2), s_s5, w=[PRM2])
                        dma(qs[1 - ri], Bin[ri][:], din[nb_].rearrange("(j two) p n -> j two p n", two=2), s_s5, w=[PRM2])
                    for two in range(2):
                        psl = slice(64 * two, 64 * two + 64)
                        pdma(Ldt[psl, :], din["log_dt"].rearrange("(j two) -> two j", two=2)[two:two + 1, :].partition_broadcast(64))
                    for ri, nm in enumerate(["c_re", "c_im"]):
                        for blk in range(2):
                            for two in range(2):
                                pdma(Cin[ri][:, blk, two, :], din[nm][16 * blk + two:16 * blk + 16:2, :, :])
                for ri, dstA, dstB in ((0, Are, Bre), (1, Aim, Bim)):
                    op("pe", lambda e: e.transpose(ps_t[:, 6, 0:16], Ain[ri][:].rearrange("j a b -> j (a b)"), ident_f[0:16, 0:16]),
                       r=[PRM2, CONSTS], w=[PS[6]])
                    op("dve", lambda e: e.tensor_copy(out=dstA[:], in_=ps_t[:, 6, 0:16]), r=[PS[6]], w=[PRM])
                    for n_ in range(16):
                        op("pe", lambda e: e.transpose(ps_t[:, 7, n_ * 16:(n_ + 1) * 16],
                                                       Bin[ri][:, :, :, n_].rearrange("j a b -> j (a b)"), ident_f[0:16, 0:16]),
                           r=[PRM2, CONSTS], w=[PS[7]])
                    op("dve", lambda e: e.tensor_copy(out=dstB[:].rearrange("p j n -> p n j"),
                                                      in_=ps_t[:, 7, 0:256].rearrange("p (n j) -> p n j", j=16)),
                       r=[PS[7]], w=[PRM])
                sm = {}
                for nm in ["dt", "lr", "ldt", "ang", "mag", "sa", "ca", "abr", "abi", "den", "nr", "fre", "fim", "u1", "u2",
                           "thr", "ta", "tq"]:
                    sm[nm] = sb(sp_, "s5_" + nm, [128, 16])
                tqi = sb(sp_, "s5_tqi", [128, 16], I32)
                SM = Buf("s5sm")
                TMPB = Buf("s5tmp")

                def dv(fn, r=(PRM,), w=None):
                    op("dve", fn, r=list(r) + [SM], w=[SM] if w is None else w)

                def tt(o, a, b, o_):
                    dv(lambda e: e.tensor_tensor(out=o, in0=a, in1=b, op=o_))
                op("act", lambda e: e.activation(out=sm["dt"][:], in_=Ldt[:], func=AF.Exp), r=[PRM], w=[SM])
                dv(lambda e: e.tensor_scalar(out=sm["lr"][:], in0=Are[:], scalar1=-1e-4, scalar2=None, op0=ALU.min))
                tt(sm["ldt"][:], sm["lr"][:], sm["dt"][:], ALU.mult)
                tt(sm["ang"][:], Aim[:], sm["dt"][:], ALU.mult)
                op("act", lambda e: e.activation(out=sm["mag"][:], in_=sm["ldt"][:], func=AF.Exp), r=[SM], w=[SM])
                sincos("s5a", lambda: sm["ang"][:], None, (SM, TMPB, SM), sm["sa"][:], sm["ca"][:],
                       {"a": sm["ta"][:], "q": sm["tq"][:], "qi": tqi[:]})
                tt(sm["abr"][:], sm["mag"][:], sm["ca"][:], ALU.mult)
                tt(sm["abi"][:], sm["mag"][:], sm["sa"][:], ALU.mult)
                tt(sm["u1"][:], sm["lr"][:], sm["lr"][:], ALU.mult)
                tt(sm["u2"][:], Aim[:], Aim[:], ALU.mult)
                tt(sm["den"][:], sm["u1"][:], sm["u2"][:], ALU.add)
                dv(lambda e: e.reciprocal(out=sm["den"][:], in_=sm["den"][:]))
                dv(lambda e: e.tensor_scalar(out=sm["nr"][:], in0=sm["abr"][:], scalar1=-1.0, scalar2=None, op0=ALU.add))
                tt(sm["u1"][:], sm["nr"][:], sm["lr"][:], ALU.mult)
                tt(sm["u2"][:], sm["abi"][:], Aim[:], ALU.mult)
                tt(sm["fre"][:], sm["u1"][:], sm["u2"][:], ALU.add)
                tt(sm["fre"][:], sm["fre"][:], sm["den"][:], ALU.mult)
                tt(sm["u1"][:], sm["abi"][:], sm["lr"][:], ALU.mult)
                tt(sm["u2"][:], sm["nr"][:], Aim[:], ALU.mult)
                tt(sm["fim"][:], sm["u1"][:], sm["u2"][:], ALU.subtract)
                tt(sm["fim"][:], sm["fim"][:], sm["den"][:], ALU.mult)
                dv(lambda e: e.tensor_tensor(out=rdec[:], in0=sm["mag"][:], in1=sm["mag"][:], op=ALU.mult), w=[SM, S5C])
                Bb = [sb(sp_, "Bb_re", [128, 16, 16]), sb(sp_, "Bb_im", [128, 16, 16])]
                bt1 = sb(sp_, "bt1", [128, 16, 16])
                bt2 = sb(sp_, "bt2", [128, 16, 16])
                fre_b = sm["fre"][:].unsqueeze(2).to_broadcast([128, 16, 16])
                fim_b = sm["fim"][:].unsqueeze(2).to_broadcast([128, 16, 16])
                tt(bt1[:], Bre[:], fre_b, ALU.mult)
                tt(bt2[:], Bim[:], fim_b, ALU.mult)
                tt(Bb[0][:], bt1[:], bt2[:], ALU.subtract)
                tt(bt1[:], Bim[:], fre_b, ALU.mult)
                tt(bt2[:], Bre[:], fim_b, ALU.mult)
                tt(Bb[1][:], bt1[:], bt2[:], ALU.add)
                dv(lambda e: e.memset(BP[:], 0.0))
                for two in range(2):
                    psl = slice(64 * two, 64 * two + 64)
                    for r_ in range(4):
                        c0 = 32 * r_ + 16 * two
                        for ri in range(2):
                            dv(lambda e: e.tensor_copy(out=BP[psl, r_::4, ri, c0:c0 + 16], in_=Bb[ri][psl, r_::4, :]))
                for jg in range(8):
                    bnk = 6 + jg % 2
                    for q_ in range(4):
                        j, ri = 2 * jg + q_ // 2, q_ % 2
                        op("pe", lambda e: e.transpose(ps_t[:, bnk, q_ * 128:(q_ + 1) * 128], BP[:, j, ri, :], ident_f[:]),
                           r=[SM, CONSTS], w=[PS[bnk]])
                    op("dve", lambda e: e.tensor_copy(out=WB[:, 2 * jg:2 * jg + 2, :, :].rearrange("p a b c -> p (a b c)"),
                                                      in_=ps_t[:, bnk, :]), r=[PS[bnk]], w=[S5C])
                BPh = sb(sp_, "BPh", [128, 16, 2, 128], BF16)
                dv(lambda e: e.tensor_copy(out=BPh[:], in_=BP[:]))
                AB = [sb(sp_, "AB_re", [128, 16, 16]), sb(sp_, "AB_im", [128, 16, 16])]
                abr_b = sm["abr"][:].unsqueeze(2).to_broadcast([128, 16, 16])
                abi_b = sm["abi"][:].unsqueeze(2).to_broadcast([128, 16, 16])
                tt(bt1[:], Bb[0][:], abr_b, ALU.mult)
                tt(bt2[:], Bb[1][:], abi_b, ALU.mult)
                tt(AB[0][:], bt1[:], bt2[:], ALU.subtract)
                tt(bt1[:], Bb[1][:], abr_b, ALU.mult)
                tt(bt2[:], Bb[0][:], abi_b, ALU.mult)
                tt(AB[1][:], bt1[:], bt2[:], ALU.add)
                for two in range(2):
                    psl = slice(64 * two, 64 * two + 64)
                    for r_ in range(4):
                        c0 = 32 * r_ + 16 * two
                        for ri in range(2):
                            dv(lambda e: e.tensor_copy(out=BP[psl, r_::4, ri, c0:c0 + 16], in_=AB[ri][psl, r_::4, :]))
                for jg in range(8):
                    bnk = 6 + jg % 2
                    for q_ in range(4):
                        j, ri = 2 * jg + q_ // 2, q_ % 2
                        op("pe", lambda e: e.transpose(ps_t[:, bnk, q_ * 128:(q_ + 1) * 128], BP[:, j, ri, :], ident_f[:]),
                           r=[SM, CONSTS], w=[PS[bnk]])
                    op("dve", lambda e: e.tensor_copy(out=WA[:, 2 * jg:2 * jg + 2, :, :].rearrange("p a b c -> p (a b c)"),
                                                      in_=ps_t[:, bnk, :]), r=[PS[bnk]], w=[S5C])
                for ri in range(2):
                    for blk in range(2):
                        bnk = 6 + blk
                        op("pe", lambda e: e.transpose(ps_t[:, bnk, 0:128], Cin[ri][:, blk, :, :].rearrange("p a b -> p (a b)"),
                                                       ident_f[:]), r=[PRM, CONSTS], w=[PS[bnk]])
                        op("dve", lambda e: e.tensor_copy(out=Csm[ri][:, 8 * blk:8 * blk + 8, :].rearrange("p a b -> p (a b)"),
                                                          in_=ps_t[:, bnk, 0:128]), r=[PS[bnk]], w=[SM])
                op("dve", lambda e: e.memset(LC[:], 0.0), w=[S5C])
                for two in range(2):
                    psl = slice(64 * two, 64 * two + 64)
                    for r_ in range(4):
                        c0 = 32 * r_ + 16 * two
                        op("dve", lambda e: e.tensor_copy(out=LC[psl, r_::4, 0, c0:c0 + 16], in_=Csm[0][psl, r_::4, :]),
                           r=[SM], w=[S5C])
                        op("dve", lambda e: e.tensor_scalar(out=LC[psl, r_::4, 1, c0:c0 + 16], in0=Csm[1][psl, r_::4, :],
                                                            scalar1=-1.0, scalar2=None, op0=ALU.mult), r=[SM], w=[S5C])
                for m in range(4):
                    op("dve", lambda e: e.tensor_scalar(out=Dsk[:, m, :], in0=ident_f[:], scalar1=pcol("d_skip", m), scalar2=None,
                                                        op0=ALU.mult), r=[CONSTS], w=[S5C])
                CA = [sb(sp_, "CA_re", [128, 16, 16]), sb(sp_, "CA_im", [128, 16, 16])]
                tt(bt1[:], Csm[0][:], abr_b, ALU.mult)
                tt(bt2[:], Csm[1][:], abi_b, ALU.mult)
                tt(CA[0][:], bt1[:], bt2[:], ALU.subtract)
                tt(bt1[:], Csm[0][:], abi_b, ALU.mult)
                tt(bt2[:], Csm[1][:], abr_b, ALU.mult)
                tt(CA[1][:], bt1[:], bt2[:], ALU.add)
                op("dve", lambda e: e.memset(LCa[:], 0.0), w=[S5C])
                for two in range(2):
                    psl = slice(64 * two, 64 * two + 64)
                    for r_ in range(4):
                        c0 = 32 * r_ + 16 * two
                        op("dve", lambda e: e.tensor_copy(out=LCa[psl, r_::4, 0, c0:c0 + 16], in_=CA[0][psl, r_::4, :]),
                           r=[SM], w=[S5C])
                        op("dve", lambda e: e.tensor_scalar(out=LCa[psl, r_::4, 1, c0:c0 + 16], in0=CA[1][psl, r_::4, :],
                                                            scalar1=-1.0, scalar2=None, op0=ALU.mult), r=[SM], w=[S5C])
                for m in range(4):
                    bnk = 6 + m % 2
                    for jj in range(4):
                        for ri in range(2):
                            op("pe", lambda e: e.matmul(ps_t[:, bnk, 0:128], BPh[:, 4 * m + jj, ri, :], LC[:, 4 * m + jj, ri, :],
                                                        start=(jj == 0 and ri == 0), stop=(jj == 3 and ri == 1)),
                               r=[SM, S5C], w=[PS[bnk]])
                    op("dve", lambda e: e.scalar_tensor_tensor(out=K0D[:, m, :], in0=ident_f[:], scalar=pcol("d_skip", m),
                                                               in1=ps_t[:, bnk, 0:128], op0=ALU.mult, op1=ALU.add),
                       r=[PS[bnk], CONSTS], w=[S5C])
                dv(lambda e: e.tensor_scalar(out=sm["tq"][:], in0=sm["ang"][:], scalar1=1.0 / TWO_PI, scalar2=None, op0=ALU.mult))
                dv(lambda e: e.tensor_copy(out=tqi[:], in_=sm["tq"][:]))
                dv(lambda e: e.tensor_copy(out=sm["tq"][:], in_=tqi[:]))
                dv(lambda e: e.scalar_tensor_tensor(out=sm["thr"][:], in0=sm["tq"][:], scalar=-CW1, in1=sm["ang"][:],
                                                    op0=ALU.mult, op1=ALU.add))
                dv(lambda e: e.scalar_tensor_tensor(out=sm["thr"][:], in0=sm["tq"][:], scalar=-CW2, in1=sm["thr"][:],
                                                    op0=ALU.mult, op1=ALU.add))
                dv(lambda e: e.tensor_scalar(out=sm["u2"][:], in0=sm["thr"][:], scalar1=2.0, scalar2=None, op0=ALU.mult))
                dv(lambda e: e.tensor_scalar(out=sm["tq"][:], in0=sm["u2"][:], scalar1=1.0 / TWO_PI, scalar2=None, op0=ALU.mult))
                dv(lambda e: e.tensor_copy(out=tqi[:], in_=sm["tq"][:]))
                dv(lambda e: e.tensor_copy(out=sm["tq"][:], in_=tqi[:]))
                dv(lambda e: e.scalar_tensor_tensor(out=sm["thr"][:], in0=sm["tq"][:], scalar=-CW1, in1=sm["u2"][:],
                                                    op0=ALU.mult, op1=ALU.add))
                dv(lambda e: e.scalar_tensor_tensor(out=sm["thr"][:], in0=sm["tq"][:], scalar=-CW2, in1=sm["thr"][:],
                                                    op0=ALU.mult, op1=ALU.add))
                dv(lambda e: e.tensor_scalar(out=sm["u1"][:], in0=sm["thr"][:], scalar1=float(TC), scalar2=None, op0=ALU.mult))
                sincos("s5e", lambda: sm["u1"][:], None, (SM, TMPB, S5C), Ei[:], Er[:],
                       {"a": sm["ta"][:], "q": sm["tq"][:], "qi": tqi[:]})
                tgA = sb(sp_, "tgA", [128, 4, TC])
                tga = sb(sp_, "tga", [128, 4, TC])
                tgq = sb(sp_, "tgq", [128, 4, TC])
                tgqi = sb(sp_, "tgqi", [128, 4, TC], I32)
                TGA, TGT = Buf("tga"), Buf("tgt")
                for tg in range(4):
                    op("dve", lambda e: e.tensor_tensor(out=tgA[:], in0=sm["thr"][:, 4 * tg:4 * tg + 4].unsqueeze(2).to_broadcast([128, 4, TC]),
                                                        in1=iota_f[:, 0:TC].unsqueeze(1).to_broadcast([128, 4, TC]), op=ALU.mult),
                       r=[SM, CONSTS], w=[TGA])
                    sincos("s5t", lambda: tgA[:], None, (TGA, TGT, S5C), sinTab[:, 4 * tg:4 * tg + 4, :], cosTab[:, 4 * tg:4 * tg + 4, :],
                           {"a": tga[:], "q": tgq[:], "qi": tgqi[:]})
                kb.barrier()

            xs_t = [sb(f3, "xs%d" % i_, [128, 1024]) for i_ in range(2)]
            st_t = [sb(f3, "st%d" % i_, [128, 2, 6]) for i_ in range(2)]
            mv_t = [sb(f3, "mv%d" % i_, [128, 2]) for i_ in range(2)]
            rs_t = [sb(f3, "rs%d" % i_, [128, 1]) for i_ in range(2)]
            h0T = [sb(f3, "h0T0", [128, 8, CH], BF16), sb(f3, "h0T1", [128, 8, CH], BF16)]
            H0T = [Buf("h0T0"), Buf("h0T1")]
            uc = [sb(f3, "uc0", [128, 4, CH], BF16), sb(f3, "uc1", [128, 4, CH], BF16)]
            UC = [[Buf() for _ in range(4)] for _ in range(2)]
            ygc = sb(f3, "ygc", [128, 4, CH], BF16)
            YGC = [Buf() for _ in range(4)]
            sg = [sb(f3, "sg0", [128, CH], BF16), sb(f3, "sg1", [128, CH], BF16)]
            SG = [Buf(), Buf()]
            bre = [sb(f3, "bre0", [128, TC], BF16), sb(f3, "bre1", [128, TC], BF16)]
            bim = [sb(f3, "bim0", [128, TC], BF16), sb(f3, "bim1", [128, TC], BF16)]
            BRE = [Buf(), Buf()]
            BIM = [Buf(), Buf()]
            tmps = [{n_: sb(f3, "s5w%d_" % l_ + n_, [128, TC], BF16) for n_ in ["ta", "tb", "tc", "td", "vre", "vim", "zre", "zim"]}
                    for l_ in range(2)]
            TBs = [{n_: Buf() for n_ in tmps[0]} for l_ in range(2)]
            ZLs = [Buf("zl0"), Buf("zl1")]
            xr = [sb(f3, "xr%d" % i, [128, TC + 2], BF16) for i in range(8)]
            xi = [sb(f3, "xi%d" % i, [128, TC + 2], BF16) for i in range(8)]
            xlast = [sb(f3, "xlast_re", [128, 16], BF16), sb(f3, "xlast_im", [128, 16], BF16)]
            XL = [Buf("xl0"), Buf("xl1")]
            op("dve", lambda e: e.memset(xlast[0][:], 0.0), w=[XL[0], XL[1]])
            op("dve", lambda e: e.memset(xlast[1][:], 0.0), w=[XL[0], XL[1]])
            XR = [Buf() for _ in range(8)]
            XI = [Buf() for _ in range(8)]
            zin = [sb(f3, "zin_re", [128, 16]), sb(f3, "zin_im", [128, 16])]
            zl = [sb(f3, "zl_re", [128, 16]), sb(f3, "zl_im", [128, 16])]
            cw = [sb(f3, "cw%d" % i, [128, 16]) for i in range(4)]
            ZIN, ZL, CWB = Buf("zin"), Buf("zl"), Buf("cw")
            op("dve", lambda e: e.memset(zin[0][:], 0.0), w=[ZIN])
            op("dve", lambda e: e.memset(zin[1][:], 0.0), w=[ZIN])
            rot3 = [0]

            def gen_bank():
                b = 6 + rot3[0] % 2
                rot3[0] += 1
                return b

            def f3_front_steps(c):
                hb_ = c % 2

                def evac_f3(i, ft, pap, PB):
                    op("act", lambda e: e.activation(out=h0T[hb_][:, ft, i * 128:(i + 1) * 128], in_=pap,
                                                     func=AF.Identity, bias=pcol("ln_in_b", ft), scale=pcol("ln_in_g", ft)),
                       r=[PB, CONSTS], w=[H0T[hb_]])
                steps_ = ln_transpose_chunk(c, xs_t, st_t, mv_t, rs_t, [0], evac_f3, as_steps=True)

                def up(m):
                    b = gen_bank()
                    for kt in range(8):
                        op("pe", lambda e: e.matmul(psf(b), winu[:, kt, m * 128:(m + 1) * 128], h0T[hb_][:, kt, :],
                                                    start=(kt == 0), stop=(kt == 7)), r=[W3, H0T[hb_]], w=[PS[b]])
                    op("act", lambda e: e.activation(out=uc[hb_][:, m, :], in_=psf(b), func=AF.Copy), r=[PS[b]], w=[UC[hb_][m]])
                for m in range(4):
                    steps_.append(lambda m=m: up(m))
                return steps_

            for st_ in f3_front_steps(0):
                st_()
            for c in range(NCH):
                hb = c % 2
                csl = slice(c * CH, (c + 1) * CH)
                nxt_steps = f3_front_steps(c + 1) if c + 1 < NCH else []
                def tile_steps(j, lane):
                    m = j // 4
                    bs = lane
                    T = tmps[lane]
                    TBl = TBs[lane]
                    pr_, pi_ = 2 + 2 * bs, 3 + 2 * bs
                    ueo = uc[hb][:, m, :].rearrange("p (c two) -> p two c", two=2)
                    for ri, pb_ in ((0, pr_), (1, pi_)):
                        op("pe", lambda e: e.matmul(ps_t[:, pb_, 0:TC], WA[:, j, ri, :], ueo[:, 0, :], start=True, stop=False),
                           r=[S5C, UC[hb][m]], w=[PS[pb_]])
                        op("pe", lambda e: e.matmul(ps_t[:, pb_, 0:TC], WB[:, j, ri, :], ueo[:, 1, :], start=False, stop=True),
                           r=[S5C, UC[hb][m]], w=[PS[pb_]])
                    op("act", lambda e: e.activation(out=bre[bs][:], in_=ps_t[:, pr_, 0:TC], func=AF.Copy), r=[PS[pr_]], w=[BRE[bs]])
                    op("act", lambda e: e.activation(out=bim[bs][:], in_=ps_t[:, pi_, 0:TC], func=AF.Copy), r=[PS[pi_]], w=[BIM[bs]])
                    yield
                    cs_, sn_ = cosTab[:, j, :], sinTab[:, j, :]

                    def d2(o, a, b_, o_, rb, wb):
                        op("dve", lambda e: e.tensor_tensor(out=o, in0=a, in1=b_, op=o_), r=rb, w=wb)
                    d2(T["ta"][:], bre[bs][:], cs_, ALU.mult, [BRE[bs], S5C], [TBl["ta"]])
                    yield
                    d2(T["tb"][:], bim[bs][:], sn_, ALU.mult, [BIM[bs], S5C], [TBl["tb"]])
                    yield
                    d2(T["tc"][:], bim[bs][:], cs_, ALU.mult, [BIM[bs], S5C], [TBl["tc"]])
                    yield
                    d2(T["td"][:], bre[bs][:], sn_, ALU.mult, [BRE[bs], S5C], [TBl["td"]])
                    yield
                    d2(T["vre"][:], T["ta"][:], T["tb"][:], ALU.add, [TBl["ta"], TBl["tb"]], [TBl["vre"]])
                    yield
                    d2(T["vim"][:], T["tc"][:], T["td"][:], ALU.subtract, [TBl["tc"], TBl["td"]], [TBl["vim"]])
                    yield
                    op("dve", lambda e: e.tensor_tensor_scan(out=T["zre"][:], data0=rdec[:, j:j + 1].to_broadcast([128, TC]),
                                                             data1=T["vre"][:], initial=zin[0][:, j:j + 1], op0=ALU.mult, op1=ALU.add),
                       r=[TBl["vre"], ZIN, S5C], w=[TBl["zre"]])
                    yield
                    op("dve", lambda e: e.tensor_tensor_scan(out=T["zim"][:], data0=rdec[:, j:j + 1].to_broadcast([128, TC]),
                                                             data1=T["vim"][:], initial=zin[1][:, j:j + 1], op0=ALU.mult, op1=ALU.add),
                       r=[TBl["vim"], ZIN, S5C], w=[TBl["zim"]])
                    yield
                    op("dve", lambda e: e.tensor_copy(out=zl[0][:, j:j + 1], in_=T["zre"][:, TC - 1:TC]), r=[TBl["zre"]], w=[ZLs[lane]])
                    op("dve", lambda e: e.tensor_copy(out=zl[1][:, j:j + 1], in_=T["zim"][:, TC - 1:TC]), r=[TBl["zim"]], w=[ZLs[lane]])
                    yield
                    xs_ = (m % 2) * 4 + j % 4
                    op("dve", lambda e: e.tensor_copy(out=xr[xs_][:, 1:2], in_=xlast[0][:, j:j + 1]), r=[XL[lane]], w=[XR[xs_]])
                    op("dve", lambda e: e.tensor_copy(out=xi[xs_][:, 1:2], in_=xlast[1][:, j:j + 1]), r=[XL[lane]], w=[XI[xs_]])
                    d2(T["ta"][:], T["zre"][:], cs_, ALU.mult, [TBl["zre"], S5C], [TBl["ta"]])
                    yield
                    d2(T["tb"][:], T["zim"][:], sn_, ALU.mult, [TBl["zim"], S5C], [TBl["tb"]])
                    yield
                    d2(T["tc"][:], T["zim"][:], cs_, ALU.mult, [TBl["zim"], S5C], [TBl["tc"]])
                    yield
                    d2(T["td"][:], T["zre"][:], sn_, ALU.mult, [TBl["zre"], S5C], [TBl["td"]])
                    yield
                    d2(xr[xs_][:, 2:TC + 2], T["ta"][:], T["tb"][:], ALU.subtract, [TBl["ta"], TBl["tb"], XR[xs_]], [XR[xs_]])
                    yield
                    d2(xi[xs_][:, 2:TC + 2], T["tc"][:], T["td"][:], ALU.add, [TBl["tc"], TBl["td"], XI[xs_]], [XI[xs_]])
                    yield
                    op("dve", lambda e: e.tensor_copy(out=xlast[0][:, j:j + 1], in_=xr[xs_][:, TC + 1:TC + 2]), r=[XR[xs_]], w=[XL[lane]])
                    op("dve", lambda e: e.tensor_copy(out=xlast[1][:, j:j + 1], in_=xi[xs_][:, TC + 1:TC + 2]), r=[XI[xs_]], w=[XL[lane]])
                    yield

                def c_matmuls(m):
                    b = gen_bank()
                    ueo = uc[hb][:, m, :].rearrange("p (c two) -> p two c", two=2)
                    for jj in range(4):
                        j2 = 4 * m + jj
                        x2 = (m % 2) * 4 + jj
                        op("pe", lambda e: e.matmul(ps_t[:, b, 0:TC], LC[:, j2, 0, :], xr[x2][:, 2:TC + 2], start=(jj == 0), stop=False),
                           r=[S5C, XR[x2]], w=[PS[b]])
                        op("pe", lambda e: e.matmul(ps_t[:, b, 0:TC], LC[:, j2, 1, :], xi[x2][:, 2:TC + 2], start=False, stop=False),
                           r=[S5C, XI[x2]], w=[PS[b]])
                    op("pe", lambda e: e.matmul(ps_t[:, b, 0:TC], Dsk[:, m, :], ueo[:, 1, :], start=False, stop=True),
                       r=[S5C, UC[hb][m]], w=[PS[b]])
                    for jj in range(4):
                        j2 = 4 * m + jj
                        x2 = (m % 2) * 4 + jj
                        op("pe", lambda e: e.matmul(ps_t[:, b, TC:2 * TC], LCa[:, j2, 0, :], xr[x2][:, 1:TC + 1], start=(jj == 0), stop=False),
                           r=[S5C, XR[x2]], w=[PS[b]])
                        op("pe", lambda e: e.matmul(ps_t[:, b, TC:2 * TC], LCa[:, j2, 1, :], xi[x2][:, 1:TC + 1], start=False, stop=False),
                           r=[S5C, XI[x2]], w=[PS[b]])
                    op("pe", lambda e: e.matmul(ps_t[:, b, TC:2 * TC], K0D[:, m, :], ueo[:, 0, :], start=False, stop=True),
                       r=[S5C, UC[hb][m]], w=[PS[b]])
                    yeo = ygc[:, m, :].rearrange("p (c two) -> p two c", two=2)
                    op("act", lambda e: e.activation(out=yeo[:, 1, :], in_=ps_t[:, b, 0:TC], func=AF.Gelu_apprx_tanh), r=[PS[b]], w=[YGC[m]])
                    op("act", lambda e: e.activation(out=yeo[:, 0, :], in_=ps_t[:, b, TC:2 * TC], func=AF.Gelu_apprx_tanh), r=[PS[b]], w=[YGC[m]])

                def start_pair(jp):
                    gs = [tile_steps(2 * jp, 0), tile_steps(2 * jp + 1, 1)]
                    for g_ in gs:
                        next(g_)
                    return gs
                pair = start_pair(0)
                for jp in range(8):
                    alive = list(pair)
                    while alive:
                        for g_ in list(alive):
                            try:
                                next(g_)
                            except StopIteration:
                                alive.remove(g_)
                    if jp + 1 < 8:
                        pair = start_pair(jp + 1)
                    if jp % 2 == 1:
                        c_matmuls(jp // 2)
                    if jp >= 1:
                        for _ in range(2):
                            if nxt_steps:
                                nxt_steps.pop(0)()
                while nxt_steps:
                    nxt_steps.pop(0)()
                for m2 in range(4):
                    b = gen_bank()
                    for m in range(4):
                        op("pe", lambda e: e.matmul(psf(b), wglu[:, m, m2 * 128:(m2 + 1) * 128], ygc[:, m, :], start=(m == 0), stop=(m == 3)),
                           r=[W3, YGC[m]], w=[PS[b]])
                    op("act", lambda e: e.activation(out=sg[m2 % 2][:], in_=psf(b), func=AF.Sigmoid, bias=pcol("b_glu", m2), scale=1.0),
                       r=[PS[b], CONSTS], w=[SG[m2 % 2]])
                    op("dve", lambda e: e.tensor_tensor(out=ygT[:, m2, csl], in0=ygc[:, m2, :], in1=sg[m2 % 2][:], op=ALU.mult),
                       r=[YGC[m2], SG[m2 % 2]], w=[YG[c]])
                def c2(o, a, b_, o_, rb, wb):
                    op("dve", lambda e: e.tensor_tensor(out=o, in0=a, in1=b_, op=o_), r=rb, w=wb)
                c2(cw[0][:], Er[:], zl[0][:], ALU.mult, [S5C, ZLs[0], ZLs[1]], [CWB])
                c2(cw[1][:], Ei[:], zl[1][:], ALU.mult, [S5C, ZLs[0], ZLs[1]], [CWB])
                c2(cw[2][:], Er[:], zl[1][:], ALU.mult, [S5C, ZLs[0], ZLs[1]], [CWB])
                c2(cw[3][:], Ei[:], zl[0][:], ALU.mult, [S5C, ZLs[0], ZLs[1]], [CWB])
                c2(zin[0][:], cw[0][:], cw[1][:], ALU.subtract, [CWB], [ZIN])
                c2(zin[1][:], cw[2][:], cw[3][:], ALU.add, [CWB], [ZIN])
            if stop_after == "F3":
                dbg_dump("ygT", ygT[:], YG[NCH - 1])
            kb.barrier()
        if stop_after == "F3":
            fb.close()
            return finish(nc, kb, out_d, dbg_outs, s_dbg)

        with contextlib.ExitStack() as bk:
            NSLOT = 4
            RSZ = 4096
            ring = [sb(bk, "ring%d" % i, [128, RSZ], BF16) for i in range(NSLOT)]
            RING = [Buf("ring%d" % i) for i in range(NSLOT)]
            s_ring = [kb.dsem("d_ring%d" % i) for i in range(NSLOT)]
            xs_t = [sb(bk, "xs0", [128, 1024]), sb(bk, "xs1", [128, 1024])]
            st_t = [sb(bk, "st0", [128, 2, 6]), sb(bk, "st1", [128, 2, 6])]
            mv_t = [sb(bk, "mv0", [128, 2]), sb(bk, "mv1", [128, 2])]
            rs_t = [sb(bk, "rs0", [128, 1]), sb(bk, "rs1", [128, 1])]
            resT = [sb(bk, "resT%d" % i, [128, 8, CH]) for i in range(2)]
            hbT = [sb(bk, "hbT%d" % i, [128, 8, CH], BF16) for i in range(2)]
            mgT = sb(bk, "mgT", [128, 8, CH], BF16)
            RES = [[Buf("res%d_%d" % (i, m)) for m in range(8)] for i in range(2)]
            HB = [[Buf("hb%d_%d" % (i, m)) for m in range(8)] for i in range(2)]
            MG = [Buf("mg%d" % m) for m in range(8)]
            sga = [sb(bk, "sga%d" % i, [128, CH], BF16) for i in range(2)]
            sgb = [sb(bk, "sgb%d" % i, [128, CH], BF16) for i in range(2)]
            SGA = [Buf(), Buf()]
            SGB = [Buf(), Buf()]
            t1 = sb(bk, "bt1", [128, CH])
            t2 = sb(bk, "bt2", [128, CH])
            T1, T2 = Buf("t1"), Buf("t2")
            reluT = [sb(bk, "relu%d" % i, [128, 8, CH], BF16) for i in range(2)]
            RL = [[Buf() for _ in range(8)] for _ in range(2)]
            rtmp = [sb(bk, "rtmp%d" % i, [128, CH], BF16) for i in range(2)]
            RT = [Buf(), Buf()]
            mean_s = sb(bk, "mean_s", [128, CH])
            var_s = sb(bk, "var_s", [128, CH])
            rstd_s = sb(bk, "rstd_s", [128, CH])
            nmr_s = sb(bk, "nmr_s", [128, CH])
            LNB = Buf("lnb")
            ntost = sb(bk, "ntost", [128, 1024])
            NT = [Buf(), Buf()]
            pst = sb(bk, "pst", [128, 256])
            pbf = sb(bk, "pbf", [128, 256], BF16)
            PST, PBF = Buf(), Buf()
            s_p = kb.dsem("d_p0")
            pT = sb(bk, "pT", [128, 2, CH], BF16)
            PTT = Buf("pT")
            s_out = kb.dsem("d_out")
            rotb = [0]

            def gbank():
                b = 2 + rotb[0] % 6
                rotb[0] += 1
                return b

            def chunk_pieces():
                lst = []
                lst += [("wo", 0), ("wo", 1)]
                for g in range(4):
                    lst += [("up", g, 0), ("up", g, 1), ("dn", g, 0), ("dn", g, 1)]
                lst += [("mix", m) for m in range(4)]
                lst += [("pg", 0), ("ple", 0), ("pg", 1)]
                lst += [("mix", m) for m in range(4, 8)]
                return lst
            pieces = [("mix", m) for m in range(8)]
            for c in range(NCH):
                cp = chunk_pieces()
                if c == NCH - 1:
                    cp = [p_ for p_ in cp if p_[0] != "mix"]
                pieces += cp
            ring_state = {"issued": 0, "cur": -1}

            def ring_issue():
                k = ring_state["issued"]
                kind = pieces[k][0]
                sl = k % NSLOT
                if kind == "mix":
                    m = pieces[k][1]
                    dst, src = ring[sl][:, 0:3072], mix_d[:, m, :, :].rearrange("p k c -> p (k c)")
                elif kind == "wo":
                    hf = pieces[k][1]
                    dst = ring[sl][:, :].rearrange("p (k c) -> p k c", k=8)
                    src = wo_d[:, :, hf * 512:(hf + 1) * 512]
                elif kind == "up":
                    g, hf = pieces[k][1], pieces[k][2]
                    dst = ring[sl][:, :].rearrange("p (k c) -> p k c", k=8)
                    src = wup_d[:, g, :, hf * 512:(hf + 1) * 512]
                elif kind == "dn":
                    g, hf = pieces[k][1], pieces[k][2]
                    dst = ring[sl][:, :].rearrange("p (k c) -> p k c", k=8)
                    src = wdn_d[:, g, :, hf * 512:(hf + 1) * 512]
                elif kind == "pg":
                    hf = pieces[k][1]
                    dst = ring[sl][:, :].rearrange("p (k c) -> p k c", k=8)
                    src = wpg_d[:, :, hf * 512:(hf + 1) * 512]
                else:
                    dst, src = ring[sl][:, 0:2048], wple_d.rearrange("p k c -> p (k c)")
                dma("sp", dst, src, s_ring[sl], r=[CAST], w=[RING[sl]])
                ring_state["issued"] += 1

            def ring_next(kind, live_prev=0):
                ring_state["cur"] += 1
                k = ring_state["cur"]
                assert pieces[k][0] == kind, (k, pieces[k], kind)
                while ring_state["issued"] < min(len(pieces), k + NSLOT - live_prev):
                    ring_issue()
                return ring[k % NSLOT], RING[k % NSLOT]

            def a1_s(c, i):
                ln_in_tile(4 * c + i, xs_t, st_t, mv_t, rs_t, (4 * c + i) % 2, False)

            def a1_t(c, i):
                par = c % 2
                slot = (4 * c + i) % 2
                for ft in range(8):
                    bank = ft // 4
                    op("pe", lambda e: e.transpose(ps_t[:, bank, (ft % 4) * 128:(ft % 4 + 1) * 128],
                                                   xs_t[slot][:, ft * 128:(ft + 1) * 128], ident_f[:]),
                       r=[XS[slot], CONSTS], w=[PS[bank]])
                for ft in range(8):
                    bank = ft // 4
                    op("act", lambda e: e.activation(out=resT[par][:, ft, i * 128:(i + 1) * 128],
                                                     in_=ps_t[:, bank, (ft % 4) * 128:(ft % 4 + 1) * 128], func=AF.Identity,
                                                     bias=pcol("ln_in_b", ft, True), scale=pcol("ln_in_g", ft, True)),
                       r=[PS[bank], CONSTS], w=[RES[par][ft]])

            def a1_fin(c):
                par = c % 2
                for ft in range(8):
                    op("dve", lambda e: e.tensor_scalar(out=hbT[par][:, ft, :], in0=resT[par][:, ft, :], scalar1=1.0 / ALPHA,
                                                        scalar2=None, op0=ALU.mult), r=[RES[par][ft]], w=[HB[par][ft]])

            a2_state = {}

            def a2_pe(c, m):
                par = c % 2
                csl = slice(c * CH, (c + 1) * CH)
                rg, RG = ring_next("mix")
                wv = rg[:, 0:3072].rearrange("p (k c) -> p k c", k=24)
                ba, bb_, bc_, bd_ = gbank(), gbank(), gbank(), gbank()
                for kt in range(8):
                    op("pe", lambda e: e.matmul(psf(ba), wv[:, kt, :], hbT[par][:, kt, :], start=(kt == 0), stop=(kt == 7)),
                       r=[RG, HB[par][kt]], w=[PS[ba]])
                for kt in range(8):
                    op("pe", lambda e: e.matmul(psf(bb_), wv[:, 8 + kt, :], hbT[par][:, kt, :], start=(kt == 0), stop=(kt == 7)),
                       r=[RG, HB[par][kt]], w=[PS[bb_]])
                for kt in range(4):
                    op("pe", lambda e: e.matmul(psf(bc_), wv[:, 16 + kt, :], attnT[:, kt, csl], start=(kt == 0), stop=(kt == 3)),
                       r=[RG, ATT[c]], w=[PS[bc_]])
                for kt in range(4):
                    op("pe", lambda e: e.matmul(psf(bd_), wv[:, 20 + kt, :], ygT[:, kt, csl], start=(kt == 0), stop=(kt == 3)),
                       r=[RG, YG[c]], w=[PS[bd_]])
                a2_state[(c, m)] = (ba, bb_, bc_, bd_)

            def a2_ev(c, m):
                ba, bb_, bc_, bd_ = a2_state.pop((c, m))
                mi = m % 2
                op("act", lambda e: e.activation(out=sga[mi][:], in_=psf(ba), func=AF.Sigmoid, bias=pcol("b_gate_a", m), scale=1.0),
                   r=[PS[ba], CONSTS], w=[SGA[mi]])
                op("act", lambda e: e.activation(out=sgb[mi][:], in_=psf(bb_), func=AF.Sigmoid, bias=pcol("b_gate_b", m), scale=1.0),
                   r=[PS[bb_], CONSTS], w=[SGB[mi]])
                op("dve", lambda e: e.tensor_tensor(out=t1[:], in0=psf(bc_), in1=sga[mi][:], op=ALU.mult), r=[PS[bc_], SGA[mi]], w=[T1])
                op("dve", lambda e: e.tensor_tensor(out=t2[:], in0=psf(bd_), in1=sgb[mi][:], op=ALU.mult), r=[PS[bd_], SGB[mi]], w=[T2])
                op("dve", lambda e: e.tensor_tensor(out=mgT[:, m, :], in0=t1[:], in1=t2[:], op=ALU.add), r=[T1, T2], w=[MG[m]])

            def acc_res(par, m, b):
                op("dve", lambda e: e.tensor_tensor(out=resT[par][:, m, :], in0=psf(b), in1=resT[par][:, m, :], op=ALU.add),
                   r=[PS[b], RES[par][m]], w=[RES[par][m]])

            def b3(c):
                par = c % 2
                for hf in range(2):
                    rg, RG = ring_next("wo")
                    wv = rg[:, :].rearrange("p (k c) -> p k c", k=8)
                    for mm in range(4):
                        m = 4 * hf + mm
                        b = gbank()
                        for kt in range(8):
                            op("pe", lambda e: e.matmul(psf(b), wv[:, kt, mm * 128:(mm + 1) * 128], mgT[:, kt, :],
                                                        start=(kt == 0), stop=(kt == 7)), r=[RG, MG[kt]], w=[PS[b]])
                        acc_res(par, m, b)

            sq = reluT[0]
            SQ = RL[0]
            ln_state = {}

            def ln_a1(par):
                for m in range(8):
                    op("act", lambda e: e.activation(out=hbT[par][:, m, :], in_=resT[par][:, m, :], func=AF.Copy),
                       r=[RES[par][m]], w=[HB[par][m]])
                    op("act", lambda e: e.activation(out=sq[:, m, :], in_=resT[par][:, m, :], func=AF.Square),
                       r=[RES[par][m]], w=[SQ[m]])

            def ln_a2(par):
                bm, bq = gbank(), gbank()
                for m in range(8):
                    op("pe", lambda e: e.matmul(psf(bm), onesd_b[:], hbT[par][:, m, :], start=(m == 0), stop=(m == 7)),
                       r=[CONSTS, HB[par][m]], w=[PS[bm]])
                for m in range(8):
                    op("pe", lambda e: e.matmul(psf(bq), onesd_b[:], sq[:, m, :], start=(m == 0), stop=(m == 7)),
                       r=[CONSTS, SQ[m]], w=[PS[bq]])
                op("act", lambda e: e.activation(out=mean_s[:], in_=psf(bm), func=AF.Copy), r=[PS[bm]], w=[LNB])
                op("dve", lambda e: e.tensor_tensor(out=var_s[:], in0=mean_s[:], in1=mean_s[:], op=ALU.mult), r=[LNB], w=[LNB])
                op("dve", lambda e: e.tensor_tensor(out=var_s[:], in0=psf(bq), in1=var_s[:], op=ALU.subtract), r=[PS[bq], LNB], w=[LNB])
                op("act", lambda e: e.activation(out=rstd_s[:], in_=var_s[:], func=AF.Ln, bias=1e-5, scale=1.0), r=[LNB], w=[LNB])
                op("act", lambda e: e.activation(out=rstd_s[:], in_=rstd_s[:], func=AF.Exp, scale=-0.5), r=[LNB], w=[LNB])
                op("dve", lambda e: e.scalar_tensor_tensor(out=nmr_s[:], in0=mean_s[:], scalar=-1.0, in1=rstd_s[:],
                                                           op0=ALU.mult, op1=ALU.mult), r=[LNB], w=[LNB])

            def ln_b(par, m, gname, bname, final):
                pp = m % 2
                nt = ntost[:, pp * 512:(pp + 1) * 512]
                op("dve", lambda e: e.tensor_tensor(out=nt, in0=resT[par][:, m, :], in1=rstd_s[:], op=ALU.mult),
                   r=[RES[par][m], LNB], w=[NT[pp]])
                op("dve", lambda e: e.tensor_tensor(out=nt, in0=nt, in1=nmr_s[:], op=ALU.add), r=[NT[pp], LNB], w=[NT[pp]])
                if final:
                    op("act", lambda e: e.activation(out=resT[par][:, m, :], in_=nt, func=AF.Identity,
                                                     bias=pcol(bname, m), scale=pcol(gname, m)),
                       r=[NT[pp], CONSTS], w=[RES[par][m]])
                else:
                    op("act", lambda e: e.activation(out=resT[par][:, m, :], in_=nt, func=AF.Identity,
                                                     bias=pcol(bname, m, True), scale=pcol(gname, m, True)),
                       r=[NT[pp], CONSTS], w=[RES[par][m]])
                    op("act", lambda e: e.activation(out=hbT[par][:, m, :], in_=nt, func=AF.Identity,
                                                     bias=pcol(bname, m), scale=pcol(gname, m)),
                       r=[NT[pp], CONSTS], w=[HB[par][m]])

            def ln_full(par, gname, bname, final, fill_pe, fill_ev):
                ln_a1(par)
                if len(fill_pe) > 0:
                    fill_pe[0]()
                ln_a2(par)
                for q_ in range(4):
                    ln_b(par, 2 * q_, gname, bname, final)
                    ln_b(par, 2 * q_ + 1, gname, bname, final)
                    if q_ < len(fill_ev):
                        fill_ev[q_]()
                    if q_ + 1 < len(fill_pe):
                        fill_pe[q_ + 1]()

            def ffn(c):
                par = c % 2
                for g in range(4):
                    rp_ = g % 2
                    if c + 1 < NCH:
                        if g == 0:
                            a1_s(c + 1, 0)
                            a1_s(c + 1, 1)
                        else:
                            a1_t(c + 1, g - 1)
                            if g + 1 < 4:
                                a1_s(c + 1, g + 1)
                    for hf in range(2):
                        rg, RG = ring_next("up")
                        wv = rg[:, :].rearrange("p (k c) -> p k c", k=8)
                        for m4 in range(4):
                            mm = 4 * hf + m4
                            b = gbank()
                            for kt in range(8):
                                op("pe", lambda e: e.matmul(psf(b), wv[:, kt, m4 * 128:(m4 + 1) * 128], hbT[par][:, kt, :],
                                                            start=(kt == 0), stop=(kt == 7)), r=[RG, HB[par][kt]], w=[PS[b]])
                            rq = mm % 2
                            op("act", lambda e: e.activation(out=rtmp[rq][:], in_=psf(b), func=AF.Relu), r=[PS[b]], w=[RT[rq]])
                            op("dve", lambda e: e.tensor_tensor(out=reluT[rp_][:, mm, :], in0=rtmp[rq][:], in1=rtmp[rq][:], op=ALU.mult),
                               r=[RT[rq]], w=[RL[rp_][mm]])
                    for hf in range(2):
                        rg, RG = ring_next("dn")
                        wv = rg[:, :].rearrange("p (k c) -> p k c", k=8)
                        for m4 in range(4):
                            m = 4 * hf + m4
                            b = gbank()
                            for mm in range(8):
                                op("pe", lambda e: e.matmul(psf(b), wv[:, mm, m4 * 128:(m4 + 1) * 128], reluT[rp_][:, mm, :],
                                                            start=(mm == 0), stop=(mm == 7)), r=[RG, RL[rp_][mm]], w=[PS[b]])
                            acc_res(par, m, b)
                if c + 1 < NCH:
                    a1_t(c + 1, 3)
                    a1_fin(c + 1)

            def ple_prep(c):
                bpt = gbank()
                for i in range(4):
                    ti = 4 * c + i
                    dma("sp", pst[:], din["p"][ti * 128:(ti + 1) * 128, :], s_p, w=[PST])
                    op("dve", lambda e: e.tensor_copy(out=pbf[:], in_=pst[:]), r=[PST], w=[PBF])
                    for kt in range(2):
                        op("pe", lambda e: e.transpose(psb(bpt)[:, kt * CH + i * 128:kt * CH + (i + 1) * 128],
                                                       pbf[:, kt * 128:(kt + 1) * 128], ident_b[:]),
                           r=[PBF, CONSTS], w=[PS[bpt]])
                op("dve", lambda e: e.tensor_copy(out=pT[:].rearrange("p k c -> p (k c)"), in_=psb(bpt)), r=[PS[bpt]], w=[PTT])

            def ple(c):
                par = c % 2
                rg2 = RG2 = None
                for hf in range(2):
                    rg, RG = ring_next("pg", live_prev=(1 if hf == 1 else 0))
                    wv = rg[:, :].rearrange("p (k c) -> p k c", k=8)
                    if hf == 0:
                        rg2, RG2 = ring_next("ple", live_prev=1)
                        wv2 = rg2[:, 0:2048].rearrange("p (k c) -> p k c", k=2)
                    for m4 in range(4):
                        m = 4 * hf + m4
                        bg_, bp_ = gbank(), gbank()
                        for kt in range(8):
                            op("pe", lambda e: e.matmul(psf(bg_), wv[:, kt, m4 * 128:(m4 + 1) * 128], hbT[par][:, kt, :],
                                                        start=(kt == 0), stop=(kt == 7)), r=[RG, HB[par][kt]], w=[PS[bg_]])
                        for kt in range(2):
                            op("pe", lambda e: e.matmul(psf(bp_), wv2[:, kt, m * 128:(m + 1) * 128], pT[:, kt, :],
                                                        start=(kt == 0), stop=(kt == 1)), r=[RG2, PTT], w=[PS[bp_]])
                        op("act", lambda e: e.activation(out=t1[:], in_=psf(bg_), func=AF.Sigmoid, bias=pcol("b_ple_gate", m), scale=1.0),
                           r=[PS[bg_], CONSTS], w=[T1])
                        op("dve", lambda e: e.tensor_tensor(out=t2[:], in0=psf(bp_), in1=t1[:], op=ALU.mult), r=[PS[bp_], T1], w=[T2])
                        op("dve", lambda e: e.tensor_tensor(out=resT[par][:, m, :], in0=t2[:], in1=resT[par][:, m, :], op=ALU.add),
                           r=[T2, RES[par][m]], w=[RES[par][m]])

            out_state = {}

            def out_pe(c, i):
                par = c % 2
                b0, b1 = gbank(), gbank()
                for ft in range(8):
                    bnk = b0 if ft < 4 else b1
                    op("pe", lambda e: e.transpose(ps_t[:, bnk, (ft % 4) * 128:(ft % 4 + 1) * 128],
                                                   resT[par][:, ft, i * 128:(i + 1) * 128], ident_f[:]),
                       r=[RES[par][ft], CONSTS], w=[PS[bnk]])
                out_state[(c, i)] = (b0, b1)

            def out_ev(c, i):
                b0, b1 = out_state.pop((c, i))
                ti = 4 * c + i
                op("act", lambda e: e.activation(out=ntost[:, 0:512], in_=psf(b0), func=AF.Copy), r=[PS[b0]], w=[NT[0]])
                op("dve", lambda e: e.tensor_copy(out=ntost[:, 512:1024], in_=psf(b1)), r=[PS[b1]], w=[NT[1]])
                dma("pool", out_d[ti * 128:(ti + 1) * 128, :], ntost[:], s_out, r=[NT[0], NT[1]])

            for i in range(4):
                a1_s(0, i) if i < 2 else None
            a1_t(0, 0)
            a1_s(0, 2)
            a1_t(0, 1)
            a1_s(0, 3)
            a1_t(0, 2)
            a1_t(0, 3)
            a1_fin(0)
            for m in range(8):
                a2_pe(0, m)
                a2_ev(0, m)
            for c in range(NCH):
                par = c % 2
                nxt = c + 1 < NCH
                b3(c)
                if stop_after == "B2" and c == 0:
                    pass
                if c > 0:
                    ln_full(par, "ln1_g", "ln1_b", False,
                            [(lambda i=i: out_pe(c - 1, i)) for i in range(4)],
                            [(lambda i=i: out_ev(c - 1, i)) for i in range(4)])
                else:
                    ln_full(par, "ln1_g", "ln1_b", False, [], [])
                ple_prep(c)
                ffn(c)
                if nxt:
                    ln_full(par, "ln2_g", "ln2_b", False,
                            [(lambda m=m: a2_pe(c + 1, m)) for m in range(4)],
                            [(lambda m=m: a2_ev(c + 1, m)) for m in range(4)])
                else:
                    ln_full(par, "ln2_g", "ln2_b", False, [], [])
                ple(c)
                if nxt:
                    ln_full(par, "ln3_g", "ln3_b", True,
                            [(lambda m=m: a2_pe(c + 1, m)) for m in range(4, 8)],
                            [(lambda m=m: a2_ev(c + 1, m)) for m in range(4, 8)])
                else:
                    ln_full(par, "ln3_g", "ln3_b", True, [], [])
            for i in range(4):
                out_pe(NCH - 1, i)
                out_ev(NCH - 1, i)
            kb.barrier()
        fb.close()
        return finish(nc, kb, out_d, dbg_outs, s_dbg)


def finish(nc, kb, out_d, dbg_outs, s_dbg):
    kb.barrier()
    nc._dbg_outs = dbg_outs
    nc._kb = kb
    return nc


def make_in_maps(inputs, cores):
    consts = host_consts()
    maps = []
    for b in cores:
        m = {"x": np.ascontiguousarray(inputs["x"][b]), "p": np.ascontiguousarray(inputs["p"][0, b]),
             "positions": np.ascontiguousarray(inputs["positions"][b]).reshape(1, S).astype(np.int32)}
        for k in W_SHAPES:
            a = np.asarray(inputs[k])
            if k not in ("ln_in_g", "ln_in_b"):
                a = a[0]
            m[k] = np.ascontiguousarray(a, dtype=np.float32)
        m.update(consts)
        m["c_pcols"] = pack_pcols(m)
        maps.append(m)
    return maps


def kernel(**inputs):
    nc = build_program()
    in_maps = make_in_maps(inputs, list(range(8)))
    res = run_bass_kernel_spmd(nc, in_maps, core_ids=list(range(8)))
    out = np.stack([np.asarray(r["out"]) for r in res.results], axis=0)
    return out.astype(np.float32)
```

```python
import contextlib
import math

import numpy as np

import concourse.bass as bass
import concourse.mybir as mybir
from concourse.bass_utils import run_bass_kernel_spmd

F32 = mybir.dt.float32
BF16 = mybir.dt.bfloat16
I32 = mybir.dt.int32
AF = mybir.ActivationFunctionType
ALU = mybir.AluOpType

S = 4096
D = 1024
NCH = 8
CH = 512
TC = 256
ALPHA = 2.0 ** 0.25
TWO_PI = 2.0 * math.pi
CW1 = 6.28125
CW2 = float(TWO_PI - 6.28125)
DUP_S = 1


class Tok:
    __slots__ = ("eng", "inst", "ms")

    def __init__(self, eng, inst):
        self.eng = eng
        self.inst = inst
        self.ms = None


class Buf:
    __slots__ = ("name", "w", "r")

    def __init__(self, name=""):
        self.name = name
        self.w = None
        self.r = {}


class DSem:
    def __init__(self, sem, name):
        self.sem = sem
        self.name = name
        self.val = 0


class KB:
    def __init__(self, nc, es):
        self.nc = nc
        self.es = es
        self.E = {"pe": nc.tensor, "act": nc.scalar, "dve": nc.vector, "pool": nc.gpsimd, "sp": nc.sync}
        self.sem = {e: es.enter_context(nc.semaphore("sem_" + e)) for e in ("pe", "act", "dve", "pool")}
        self.cnt = {e: 0 for e in self.sem}
        self.unflushed = {e: [] for e in self.sem}
        self.lasttok = {e: None for e in self.sem}
        self.seen = {e: {} for e in self.E}
        self.semname = {}
        self.dsems = []
        for e, s in self.sem.items():
            self.semname[id(s)] = "sem_" + e
        self.n_inst = 0
        self.n_wait = 0

    def dsem(self, name):
        s = self.es.enter_context(self.nc.semaphore(name))
        d = DSem(s, name)
        self.semname[id(s)] = name
        self.dsems.append(d)
        return d

    def _resolve(self, tok):
        if isinstance(tok, tuple):
            return tok
        if tok.ms is None:
            e = tok.eng
            self.cnt[e] += 1
            tok.inst.then_inc(self.sem[e], 1)
            lst = self.unflushed[e]
            i = lst.index(tok)
            for t in lst[: i + 1]:
                t.ms = self.cnt[e]
            self.unflushed[e] = lst[i + 1:]
        return (self.sem[tok.eng], tok.ms)

    def _need(self, eng, tok, waits, raw):
        if tok is None:
            return
        if not isinstance(tok, tuple) and tok.eng == eng and not raw:
            return
        if not isinstance(tok, tuple) and tok.eng == eng and eng == "pe":
            return
        sem, val = self._resolve(tok)
        key = self.semname[id(sem)]
        if self.seen[eng].get(key, 0) >= val:
            return
        self.seen[eng][key] = val
        waits.append((sem, val))

    def _deps(self, eng, r, w):
        waits = []
        for b in r:
            self._need(eng, b.w, waits, True)
        for b in w:
            self._need(eng, b.w, waits, False)
            for t in b.r.values():
                self._need(eng, t, waits, False)
        for (s, v) in waits:
            self.E[eng].wait_ge(s, v)
            self.n_wait += 1

    def op(self, eng, fn, r=(), w=()):
        self._deps(eng, r, w)
        inst = fn(self.E[eng])
        self.n_inst += 1
        tok = Tok(eng, inst)
        self.unflushed[eng].append(tok)
        self.lasttok[eng] = tok
        for b in r:
            if b not in w:
                b.r[eng] = tok
        for b in w:
            b.w = tok
            b.r = {}
        return tok

    def dma(self, q, out_ap, in_ap, dsem, r=(), w=(), **kw):
        self._deps(q, r, w)
        inst = self.E[q].dma_start(out=out_ap, in_=in_ap, **kw)
        dsem.val += 16
        inst.then_inc(dsem.sem, 16)
        tok = (dsem.sem, dsem.val)
        for b in r:
            b.r["dma:" + dsem.name] = tok
        for b in w:
            b.w = tok
            b.r = {}
        return tok

    def barrier(self):
        toks = []
        for e in self.sem:
            t = self.lasttok[e]
            if t is not None:
                toks.append(self._resolve(t))
        for d in self.dsems:
            if d.val > 0 and not d.name.startswith("d_cast"):
                toks.append((d.sem, d.val))
        for e in self.E:
            for (s, v) in toks:
                key = self.semname[id(s)]
                if self.seen[e].get(key, 0) >= v:
                    continue
                self.seen[e][key] = v
                self.E[e].wait_ge(s, v)


W_SHAPES = {
    "ln_in_g": [1024], "ln_in_b": [1024], "w_in": [1024, 3232], "b_gate": [2048], "q_norm_g": [384],
    "w_uq": [384, 768], "kv_norm_g": [256], "w_ukv": [256, 1024], "w_attn_br": [512, 1024],
    "a_re": [32, 64], "a_im": [32, 64], "log_dt": [32], "b_re": [32, 64, 16], "b_im": [32, 64, 16],
    "c_re": [32, 16, 64], "c_im": [32, 16, 64], "d_skip": [512], "w_glu": [512, 512], "b_glu": [512],
    "w_ssm_br": [512, 1024], "w_o": [1024, 1024], "ln1_g": [1024], "ln1_b": [1024],
    "w_up": [1024, 4096], "w_down": [4096, 1024], "ln2_g": [1024], "ln2_b": [1024],
    "w_ple_gate": [1024, 1024], "b_ple_gate": [1024], "w_ple": [256, 1024], "ln3_g": [1024], "ln3_b": [1024],
}


PCOL_LAYOUT = [("ln_in_g", 8), ("ln_in_b", 8), ("ln1_g", 8), ("ln1_b", 8), ("ln2_g", 8), ("ln2_b", 8),
               ("ln3_g", 8), ("ln3_b", 8), ("b_ple_gate", 8), ("b_glu", 4), ("d_skip", 4),
               ("q_norm_g", 3), ("kv_norm_g", 2), ("b_gate_a", 8), ("b_gate_b", 8)]


def pack_pcols(m):
    cols = []
    for name, n in PCOL_LAYOUT:
        if name == "b_gate_a":
            v = m["b_gate"][:1024]
        elif name == "b_gate_b":
            v = m["b_gate"][1024:]
        else:
            v = m[name]
        cols.append(np.asarray(v, np.float32).reshape(n, 128).T)
    return np.ascontiguousarray(np.concatenate(cols, axis=1))


def host_consts():
    ident = np.eye(128, dtype=np.float32)
    kk = np.arange(128)[:, None]
    qq = np.arange(128)[None, :]
    maskb = np.where(kk > qq, -30000.0, 0.0).astype(np.float32)
    iota = np.broadcast_to(np.arange(512, dtype=np.float32)[None, :], (128, 512)).copy()
    inv = (10000.0 ** (-np.arange(0, 32, 2, dtype=np.float32) / 32)).astype(np.float32)
    invf = np.zeros((128, 1), np.float32)
    for r in range(64, 96):
        invf[r, 0] = inv[(r - 64) % 16]
    return {"c_ident": ident, "c_maskb": maskb, "c_iota": iota, "c_invf": invf}


def build_program(stop_after=None, dbg=()):
    nc = bass.Bass("TRN2", target_bir_lowering=False)
    dbg = set(dbg)
    din = {}
    din["x"] = nc.dram_tensor("x", [S, D], F32, kind="ExternalInput").ap()
    din["p"] = nc.dram_tensor("p", [S, 256], F32, kind="ExternalInput").ap()
    din["positions"] = nc.dram_tensor("positions", [1, S], I32, kind="ExternalInput").ap()
    for k, shp in W_SHAPES.items():
        din[k] = nc.dram_tensor(k, shp, F32, kind="ExternalInput").ap()
    for k, v in host_consts().items():
        din[k] = nc.dram_tensor(k, list(v.shape), F32, kind="ExternalInput").ap()
    din["c_pcols"] = nc.dram_tensor("c_pcols", [128, 101], F32, kind="ExternalInput").ap()
    out_d = nc.dram_tensor("out", [S, D], F32, kind="ExternalOutput").ap()
    dbg_outs = {}

    mix_d = nc.dram_tensor("mix_d", [128, 8, 24, 128], BF16, kind="Internal").ap()
    wo_d = nc.dram_tensor("wo_d", [128, 8, 1024], BF16, kind="Internal").ap()
    wup_d = nc.dram_tensor("wup_d", [128, 4, 8, 1024], BF16, kind="Internal").ap()
    wdn_d = nc.dram_tensor("wdn_d", [128, 4, 8, 1024], BF16, kind="Internal").ap()
    wpg_d = nc.dram_tensor("wpg_d", [128, 8, 1024], BF16, kind="Internal").ap()
    wple_d = nc.dram_tensor("wple_d", [128, 2, 1024], BF16, kind="Internal").ap()

    with contextlib.ExitStack() as es:
        kb = KB(nc, es)
        op, dma = kb.op, kb.dma

        sb_ctr = [0]

        def sb(stack, name, shape, dt=F32):
            sb_ctr[0] += 1
            return stack.enter_context(nc.sbuf_tensor("%s_%d" % (name, sb_ctr[0]), shape, dt))

        ps_t = es.enter_context(nc.psum_tensor("ps", [128, 8, 512], F32))
        PS = [Buf("ps%d" % i) for i in range(8)]

        def psf(i):
            return ps_t[:, i, :]

        def psb(i):
            return ps_t[:, i, :].bitcast(BF16)

        ident_f = sb(es, "ident_f", [128, 128])
        ident_b = sb(es, "ident_b", [128, 128], BF16)
        maskb_f = sb(es, "maskb_f", [128, 128])
        maskb_b = sb(es, "maskb_b", [128, 128], BF16)
        ones_f = sb(es, "ones_f", [128, 128])
        ones_b = sb(es, "ones_b", [128, 128], BF16)
        onesd_b = sb(es, "onesd_b", [128, 128], BF16)
        iota_f = sb(es, "iota_f", [128, 512])
        invf = sb(es, "invf", [128, 1])
        NPC = 101
        pc = sb(es, "pcols", [128, NPC])
        pcs = sb(es, "pcols_s", [128, NPC])
        CONSTS = Buf("consts")
        s_setup = kb.dsem("d_setup")
        col = {}
        off = 0
        with nc.allow_non_contiguous_dma(reason="tiny parameter vectors"):
            dma("sp", ident_f[:], din["c_ident"][:, :], s_setup, w=[CONSTS])
            dma("sp", maskb_f[:], din["c_maskb"][:, :], s_setup, w=[CONSTS])
            dma("sp", iota_f[:], din["c_iota"][:, :], s_setup, w=[CONSTS])
            dma("sp", invf[:], din["c_invf"][:, :], s_setup, w=[CONSTS])
            dma("sp", pc[:], din["c_pcols"][:, :], s_setup, w=[CONSTS])
            for name, n in PCOL_LAYOUT:
                col[name] = off
                off += n
        assert off == NPC, (off, NPC)
        op("dve", lambda e: e.tensor_copy(out=ident_b[:], in_=ident_f[:]), r=[CONSTS], w=[CONSTS])
        op("dve", lambda e: e.tensor_copy(out=maskb_b[:], in_=maskb_f[:]), r=[CONSTS], w=[CONSTS])
        op("dve", lambda e: e.memset(ones_f[:], 1.0), w=[CONSTS])
        op("dve", lambda e: e.memset(ones_b[:], 1.0), w=[CONSTS])
        op("dve", lambda e: e.memset(onesd_b[:], 1.0 / 1024.0), w=[CONSTS])
        op("dve", lambda e: e.tensor_scalar(out=pcs[:], in0=pc[:], scalar1=ALPHA, scalar2=None, op0=ALU.mult),
           r=[CONSTS], w=[CONSTS])
        qg = col["q_norm_g"]
        op("dve", lambda e: e.tensor_scalar(out=pc[:, qg:qg + 3], in0=pc[:, qg:qg + 3], scalar1=96.0 ** -0.5,
                                            scalar2=None, op0=ALU.mult), r=[CONSTS], w=[CONSTS])

        def pcol(name, m=0, scaled=False):
            t = pcs if scaled else pc
            c = col[name] + m
            return t[:, c:c + 1]

        s_cast = kb.dsem("d_cast")
        CAST = Buf("cast")
        win_v = din["w_in"].rearrange("(kt p) c -> p kt c", p=128)

        def emit_casts():
            wat_v = din["w_attn_br"].rearrange("(kt p) c -> p kt c", p=128)
            wss_v = din["w_ssm_br"].rearrange("(kt p) c -> p kt c", p=128)
            for m in range(8):
                for half in range(2):
                    base = 1184 + half * 1024 + m * 128
                    dma("pool", mix_d[:, m, 8 * half:8 * half + 8, :], win_v[:, :, base:base + 128], s_cast, w=[CAST])
                dma("pool", mix_d[:, m, 16:20, :], wat_v[:, :, m * 128:(m + 1) * 128], s_cast, w=[CAST])
                dma("pool", mix_d[:, m, 20:24, :], wss_v[:, :, m * 128:(m + 1) * 128], s_cast, w=[CAST])
            dma("pool", wo_d[:, :, :], din["w_o"].rearrange("(kt p) c -> p kt c", p=128), s_cast, w=[CAST])
            for g in range(4):
                dma("pool", wup_d[:, g, :, :],
                    din["w_up"].rearrange("(kt p) c -> p kt c", p=128)[:, :, g * 1024:(g + 1) * 1024], s_cast, w=[CAST])
                dma("pool", wdn_d[:, g, :, :],
                    din["w_down"][g * 1024:(g + 1) * 1024, :].rearrange("(kk p) c -> p kk c", p=128), s_cast, w=[CAST])
            dma("pool", wpg_d[:, :, :], din["w_ple_gate"].rearrange("(kt p) c -> p kt c", p=128), s_cast, w=[CAST])
            dma("pool", wple_d[:, :, :], din["w_ple"].rearrange("(kt p) c -> p kt c", p=128), s_cast, w=[CAST])

        def sincos(stack_name, ang_ap_fn, shape, bufs, out_sin, out_cos, tmp):
            ANG, TMP, OUTB = bufs
            for outap, shift in ((out_sin, 0.0), (out_cos, math.pi / 2)):
                if outap is None:
                    continue
                a, q, qi = tmp["a"], tmp["q"], tmp["qi"]
                op("dve", lambda e: e.tensor_scalar(out=a, in0=ang_ap_fn(), scalar1=float(shift), scalar2=None, op0=ALU.add),
                   r=[ANG], w=[TMP])
                op("dve", lambda e: e.tensor_scalar(out=q, in0=a, scalar1=1.0 / TWO_PI, scalar2=None, op0=ALU.mult),
                   r=[TMP], w=[TMP])
                op("dve", lambda e: e.tensor_copy(out=qi, in_=q), r=[TMP], w=[TMP])
                op("dve", lambda e: e.tensor_copy(out=q, in_=qi), r=[TMP], w=[TMP])
                op("dve", lambda e: e.scalar_tensor_tensor(out=a, in0=q, scalar=-CW1, in1=a, op0=ALU.mult, op1=ALU.add),
                   r=[TMP], w=[TMP])
                op("dve", lambda e: e.scalar_tensor_tensor(out=a, in0=q, scalar=-CW2, in1=a, op0=ALU.mult, op1=ALU.add),
                   r=[TMP], w=[TMP])
                op("dve", lambda e: e.tensor_scalar(out=a, in0=a, scalar1=math.pi, scalar2=-math.pi, op0=ALU.min, op1=ALU.max),
                   r=[TMP], w=[TMP])
                op("act", lambda e: e.activation(out=outap, in_=a, func=AF.Sin), r=[TMP], w=[OUTB])

        XS = [Buf("xs%d" % i) for i in range(4)]
        s_x = [kb.dsem("d_x%d" % i) for i in range(4)]
        LNS = [Buf("lns%d" % i) for i in range(4)]

        def ln_in_tile(ti, xs_t, st_t, mv_t, rs_t, slot, g_scaled):
            dma("sp", xs_t[slot][:], din["x"][ti * 128:(ti + 1) * 128, :], s_x[slot], w=[XS[slot]])
            L = LNS[slot]
            op("dve", lambda e: e.bn_stats(out=st_t[slot][:, 0, :], in_=xs_t[slot][:, 0:512]), r=[XS[slot]], w=[L])
            op("dve", lambda e: e.bn_stats(out=st_t[slot][:, 1, :], in_=xs_t[slot][:, 512:1024]), r=[XS[slot]], w=[L])
            op("dve", lambda e: e.bn_aggr(out=mv_t[slot][:], in_=st_t[slot][:].rearrange("p a b -> p (a b)")), r=[L], w=[L])
            op("act", lambda e: e.activation(out=rs_t[slot][:], in_=mv_t[slot][:, 1:2], func=AF.Ln, bias=1e-5, scale=1.0),
               r=[L], w=[L])
            op("act", lambda e: e.activation(out=rs_t[slot][:], in_=rs_t[slot][:], func=AF.Exp, scale=-0.5), r=[L], w=[L])
            op("dve", lambda e: e.tensor_scalar(out=xs_t[slot][:], in0=xs_t[slot][:], scalar1=mv_t[slot][:, 0:1],
                                                scalar2=rs_t[slot][:, 0:1], op0=ALU.subtract, op1=ALU.mult),
               r=[L, XS[slot]], w=[XS[slot]])

        def ln_transpose_chunk(c, xs_t, st_t, mv_t, rs_t, banksets, evac, as_steps=False):
            nsl = len(xs_t)

            def stage_s(i):
                ti = 4 * c + i
                ln_in_tile(ti, xs_t, st_t, mv_t, rs_t, ti % nsl, False)

            def stage_t(i):
                ti = 4 * c + i
                slot = ti % nsl
                pa = banksets[ti % len(banksets)]
                for ft in range(8):
                    bank = pa + ft // 4
                    op("pe", lambda e: e.transpose(ps_t[:, bank, (ft % 4) * 128:(ft % 4 + 1) * 128],
                                                   xs_t[slot][:, ft * 128:(ft + 1) * 128], ident_f[:]),
                       r=[XS[slot], CONSTS], w=[PS[bank]])
                for ft in range(8):
                    bank = pa + ft // 4
                    evac(i, ft, ps_t[:, bank, (ft % 4) * 128:(ft % 4 + 1) * 128], PS[bank])
            steps_ = [lambda: stage_s(0), lambda: stage_s(1), lambda: stage_t(0), lambda: stage_s(2),
                      lambda: stage_t(1), lambda: stage_s(3), lambda: stage_t(2), lambda: stage_t(3)]
            if as_steps:
                return steps_
            for st_ in steps_:
                st_()

        def dbg_dump(name, ap, buf):
            if name not in dbg:
                return
            shp = list(ap.shape)
            d = nc.dram_tensor("dbg_" + name, shp, ap.dtype, kind="ExternalOutput").ap()
            dbg_outs[name] = d
            idx = tuple(slice(None) for _ in shp)
            dma("sp", d[idx], ap, s_dbg, r=[buf])

        s_dbg = kb.dsem("d_dbg")

        attnT = sb(es, "attnT", [128, 4, S], BF16)
        ATT = [Buf("att%d" % c) for c in range(NCH)]


        f12 = contextlib.ExitStack()
        cqnT = sb(f12, "cqnT", [128, 3, S], BF16)
        ckvnT = sb(f12, "ckvnT", [128, 2, S], BF16)
        krT = sb(f12, "krT", [128, S], BF16)
        cosT = sb(f12, "cosT", [128, S], BF16)
        sinT = sb(f12, "sinT", [128, S], BF16)
        CQN = [Buf("cqn%d" % c) for c in range(NCH)]
        CKV = [Buf("ckv%d" % c) for c in range(NCH)]
        KR = [Buf("kr%d" % c) for c in range(NCH)]
        ROPE = Buf("rope")
        wuq = sb(f12, "wuq", [128, 3, 768], BF16)
        wuqr = sb(f12, "wuqr", [128, 3, 768], BF16)
        wukv = sb(f12, "wukv", [128, 2, 1024], BF16)
        WATT = Buf("watt")
        s_w2 = kb.dsem("d_w2")

        with contextlib.ExitStack() as f1:
            wina = sb(f1, "wina", [128, 8, 672], BF16)
            wkr = sb(f1, "wkr", [128, 8, 96], BF16)
            wkrr = sb(f1, "wkrr", [128, 8, 96], BF16)
            WINA = Buf("wina")
            s_w1 = kb.dsem("d_w1")
            dma("pool", wina[:], win_v[:, :, 0:672], s_w1, w=[WINA])
            dma("pool", wuq[:], din["w_uq"].rearrange("(kt p) c -> p kt c", p=128), s_w2, w=[WATT])
            dma("pool", wukv[:], din["w_ukv"].rearrange("(kt p) c -> p kt c", p=128), s_w2, w=[WATT])
            emit_casts()
            op("dve", lambda e: e.memset(wkr[:], 0.0), w=[WINA])
            op("dve", lambda e: e.memset(wkrr[:], 0.0), w=[WINA])
            op("dve", lambda e: e.tensor_copy(out=wkr[:, :, 64:96], in_=wina[:, :, 640:672]), r=[WINA], w=[WINA])
            op("dve", lambda e: e.tensor_scalar(out=wkrr[:, :, 64:80], in0=wina[:, :, 656:672], scalar1=-1.0, scalar2=None,
                                                op0=ALU.mult), r=[WINA], w=[WINA])
            op("dve", lambda e: e.tensor_copy(out=wkrr[:, :, 80:96], in_=wina[:, :, 640:656]), r=[WINA], w=[WINA])

            xs_t = [sb(f1, "xs%d" % i_, [128, 1024]) for i_ in range(4)]
            st_t = [sb(f1, "st%d" % i_, [128, 2, 6]) for i_ in range(4)]
            mv_t = [sb(f1, "mv%d" % i_, [128, 2]) for i_ in range(4)]
            rs_t = [sb(f1, "rs%d" % i_, [128, 1]) for i_ in range(4)]
            h0T = [sb(f1, "h0T0", [128, 8, CH], BF16), sb(f1, "h0T1", [128, 8, CH], BF16)]
            H0T = [Buf("h0T0"), Buf("h0T1")]
            cqc = sb(f1, "cqc", [128, 5, CH], BF16)
            sqc = sb(f1, "sqc", [128, 5, CH], BF16)
            CQC = [Buf("cqc%d" % m) for m in range(5)]
            SQC = [Buf("sqc%d" % m) for m in range(5)]
            rsq = [sb(f1, "rsq0", [128, CH]), sb(f1, "rsq1", [128, CH])]
            RSQ = [Buf("rsq0"), Buf("rsq1")]
            t1 = sb(f1, "f1t1", [128, CH])
            t2 = sb(f1, "f1t2", [128, CH])
            T12 = Buf("f1t12")
            psrot = [0]

            def next_ps():
                i = 4 + (psrot[0] % 4)
                psrot[0] += 1
                return i

            def f1_ln(c):
                hb = c % 2

                def evac_f1(i, ft, pap, PB):
                    op("act", lambda e: e.activation(out=h0T[hb][:, ft, i * 128:(i + 1) * 128], in_=pap,
                                                     func=AF.Identity, bias=pcol("ln_in_b", ft), scale=pcol("ln_in_g", ft)),
                       r=[PB, CONSTS], w=[H0T[hb]])
                ln_transpose_chunk(c, xs_t, st_t, mv_t, rs_t, [0, 2], evac_f1)

            def f1_proj(c):
                hb = c % 2
                for m in range(5):
                    b = next_ps()
                    for kt in range(8):
                        op("pe", lambda e: e.matmul(psf(b), wina[:, kt, m * 128:(m + 1) * 128], h0T[hb][:, kt, :],
                                                    start=(kt == 0), stop=(kt == 7)), r=[WINA, H0T[hb]], w=[PS[b]])
                    op("act", lambda e: e.activation(out=cqc[:, m, :], in_=psf(b), func=AF.Copy), r=[PS[b]], w=[CQC[m]])
                    op("dve", lambda e: e.tensor_tensor(out=sqc[:, m, :], in0=cqc[:, m, :], in1=cqc[:, m, :], op=ALU.mult),
                       r=[CQC[m]], w=[SQC[m]])
                ba = next_ps()
                for kt in range(8):
                    op("pe", lambda e: e.matmul(ps_t[0:96, ba, :], wkr[:, kt, :], h0T[hb][:, kt, :], start=(kt == 0), stop=(kt == 7)),
                       r=[WINA, H0T[hb]], w=[PS[ba]])
                bb = next_ps()
                for kt in range(8):
                    op("pe", lambda e: e.matmul(ps_t[0:96, bb, :], wkrr[:, kt, :], h0T[hb][:, kt, :], start=(kt == 0), stop=(kt == 7)),
                       r=[WINA, H0T[hb]], w=[PS[bb]])
                return ba, bb

            def f1_tail(c, ba, bb):
                csl = slice(c * CH, (c + 1) * CH)
                op("dve", lambda e: e.tensor_tensor(out=t1[64:96, :], in0=ps_t[64:96, ba, :], in1=cosT[64:96, csl], op=ALU.mult),
                   r=[PS[ba], ROPE], w=[T12])
                op("dve", lambda e: e.tensor_tensor(out=t2[64:96, :], in0=ps_t[64:96, bb, :], in1=sinT[64:96, csl], op=ALU.mult),
                   r=[PS[bb], ROPE], w=[T12])
                op("dve", lambda e: e.tensor_tensor(out=krT[64:96, csl], in0=t1[64:96, :], in1=t2[64:96, :], op=ALU.add),
                   r=[T12], w=[KR[c]])
                for which, (m0, m1, nfeat, eps, gname, dstT, DST) in enumerate(
                        [(0, 3, 384.0, 1e-6, "q_norm_g", cqnT, CQN), (3, 5, 256.0, 1e-6, "kv_norm_g", ckvnT, CKV)]):
                    b = next_ps()
                    for m in range(m0, m1):
                        op("pe", lambda e: e.matmul(psf(b), ones_b[:], sqc[:, m, :], start=(m == m0), stop=(m == m1 - 1)),
                           r=[SQC[m], CONSTS], w=[PS[b]])
                    op("act", lambda e: e.activation(out=rsq[which][:], in_=psf(b), func=AF.Ln, bias=float(eps),
                                                     scale=1.0 / nfeat), r=[PS[b]], w=[RSQ[which]])
                    op("act", lambda e: e.activation(out=rsq[which][:], in_=rsq[which][:], func=AF.Exp, scale=-0.5),
                       r=[RSQ[which]], w=[RSQ[which]])
                    for m in range(m0, m1):
                        op("dve", lambda e: e.scalar_tensor_tensor(out=dstT[:, m - m0, csl], in0=cqc[:, m, :],
                                                                   scalar=pcol(gname, m - m0), in1=rsq[which][:],
                                                                   op0=ALU.mult, op1=ALU.mult),
                           r=[CQC[m], RSQ[which], CONSTS], w=[DST[c]])

            f1_ln(0)
            if True:
                rt = f1
                posi = sb(rt, "posi", [128, 1024], I32)
                angf = sb(rt, "angf", [128, 1024])
                ta = sb(rt, "rp_a", [128, 1024])
                tq = sb(rt, "rp_q", [128, 1024])
                tqi = sb(rt, "rp_qi", [128, 1024], I32)
                POS, ANG, TMP = Buf("pos"), Buf("ang"), Buf("rptmp")
                s_pos = kb.dsem("d_pos")
                for cc in range(4):
                    dma("sp", posi[:], din["positions"][:, cc * 1024:(cc + 1) * 1024].partition_broadcast(128), s_pos, w=[POS])
                    op("dve", lambda e: e.tensor_copy(out=angf[:], in_=posi[:]), r=[POS], w=[ANG])
                    op("dve", lambda e: e.tensor_scalar(out=angf[:], in0=angf[:], scalar1=invf[:, 0:1], scalar2=None,
                                                        op0=ALU.mult), r=[ANG, CONSTS], w=[ANG])
                    sincos("rope", lambda: angf[:], None, (ANG, TMP, ROPE),
                           sinT[:, cc * 1024:(cc + 1) * 1024], cosT[:, cc * 1024:(cc + 1) * 1024],
                           {"a": ta[:], "q": tq[:], "qi": tqi[:]})

            for c in range(NCH):
                ba, bb = f1_proj(c)
                if c + 1 < NCH:
                    f1_ln(c + 1)
                f1_tail(c, ba, bb)
            if stop_after == "F1":
                dbg_dump("cqnT", cqnT[:], CQN[NCH - 1])
                dbg_dump("ckvnT", ckvnT[:], CKV[NCH - 1])
                dbg_dump("krT", krT[64:96, :], KR[NCH - 1])
            kb.barrier()

        if stop_after == "F1":
            f12.close()
            return finish(nc, kb, out_d, dbg_outs, s_dbg)

        with contextlib.ExitStack() as f2:
            op("dve", lambda e: e.memset(wuqr[:], 0.0), w=[WATT])
            wuq4 = wuq[:].rearrange("p k (h d) -> p k h d", d=96)
            wuqr4 = wuqr[:].rearrange("p k (h d) -> p k h d", d=96)
            for kt in range(3):
                op("dve", lambda e: e.tensor_scalar(out=wuqr4[:, kt, :, 64:80], in0=wuq4[:, kt, :, 80:96], scalar1=-1.0,
                                                    scalar2=None, op0=ALU.mult), r=[WATT], w=[WATT])
                op("dve", lambda e: e.tensor_copy(out=wuqr4[:, kt, :, 80:96], in_=wuq4[:, kt, :, 64:80]), r=[WATT], w=[WATT])
            qT = [sb(f2, "qT0", [128, S], BF16), sb(f2, "qT1", [128, S], BF16)]
            kT = [sb(f2, "kT0", [128, S], BF16), sb(f2, "kT1", [128, S], BF16)]
            Vb = [sb(f2, "Vb0", [128, 32, 128], BF16), sb(f2, "Vb1", [128, 32, 128], BF16)]
            NPT = 4
            PT = [sb(f2, "PT%d" % i, [128, CH], BF16) for i in range(NPT)]
            rec = sb(f2, "rec", [128, CH])
            bcs = sb(f2, "bcs", [128, CH])
            t1 = sb(f2, "f2t1", [128, CH])
            t2 = sb(f2, "f2t2", [128, CH])
            QTn = [[Buf() for _ in range(NCH)] for _ in range(2)]
            QTr = [[Buf() for _ in range(NCH)] for _ in range(2)]
            KTn = [[Buf() for _ in range(NCH)] for _ in range(2)]
            KTr = [[Buf() for _ in range(NCH)] for _ in range(2)]
            VB = [[Buf() for _ in range(4)] for _ in range(2)]
            PTB = [Buf() for _ in range(NPT)]
            REC, BCS, T1B, T2B = Buf(), Buf(), Buf(), Buf()
            VONES = [Buf(), Buf()]
            op("dve", lambda e: e.memset(Vb[0][:, :, 64:128], 1.0), w=[VONES[0]])
            op("dve", lambda e: e.memset(Vb[1][:, :, 0:64], 1.0), w=[VONES[1]])
            prep_rot = [0]

            def prep_bank():
                b = 6 + (prep_rot[0] % 2)
                prep_rot[0] += 1
                return b

            def head_prep_pieces(h):
                hb = h % 2
                voff = 0 if hb == 0 else 64
                pieces_ = []

                def q_a(tc):
                    csl = slice(tc * CH, (tc + 1) * CH)
                    ba = prep_bank()
                    for kt in range(3):
                        op("pe", lambda e: e.matmul(ps_t[0:96, ba, :], wuq[:, kt, 96 * h:96 * h + 96], cqnT[:, kt, csl],
                                                    start=(kt == 0), stop=(kt == 2)), r=[WATT, CQN[tc]], w=[PS[ba]])
                    op("dve", lambda e: e.tensor_copy(out=qT[hb][0:64, csl], in_=ps_t[0:64, ba, :]), r=[PS[ba]], w=[QTn[hb][tc]])
                    op("dve", lambda e: e.tensor_tensor(out=t1[64:96, :], in0=ps_t[64:96, ba, :], in1=cosT[64:96, csl], op=ALU.mult),
                       r=[PS[ba], ROPE], w=[T1B])

                def q_b(tc):
                    csl = slice(tc * CH, (tc + 1) * CH)
                    bb = prep_bank()
                    for kt in range(3):
                        op("pe", lambda e: e.matmul(ps_t[0:96, bb, :], wuqr[:, kt, 96 * h:96 * h + 96], cqnT[:, kt, csl],
                                                    start=(kt == 0), stop=(kt == 2)), r=[WATT, CQN[tc]], w=[PS[bb]])
                    op("dve", lambda e: e.tensor_tensor(out=t2[64:96, :], in0=ps_t[64:96, bb, :], in1=sinT[64:96, csl], op=ALU.mult),
                       r=[PS[bb], ROPE], w=[T2B])
                    op("dve", lambda e: e.tensor_tensor(out=qT[hb][64:96, csl], in0=t1[64:96, :], in1=t2[64:96, :], op=ALU.add),
                       r=[T1B, T2B], w=[QTr[hb][tc]])

                def k_c(tc):
                    csl = slice(tc * CH, (tc + 1) * CH)
                    bk = prep_bank()
                    for kt in range(2):
                        op("pe", lambda e: e.matmul(ps_t[0:64, bk, :], wukv[:, kt, 128 * h:128 * h + 64], ckvnT[:, kt, csl],
                                                    start=(kt == 0), stop=(kt == 1)), r=[WATT, CKV[tc]], w=[PS[bk]])
                    op("dve", lambda e: e.tensor_copy(out=kT[hb][0:64, csl], in_=ps_t[0:64, bk, :]), r=[PS[bk]], w=[KTn[hb][tc]])
                    op("dve", lambda e: e.tensor_copy(out=kT[hb][64:96, csl], in_=krT[64:96, csl]), r=[KR[tc]], w=[KTr[hb][tc]])

                def v_half(tg, hf):
                    bv = prep_bank()
                    for t4 in range(4):
                        ti = tg * 8 + hf * 4 + t4
                        for kt in range(2):
                            op("pe", lambda e: e.matmul(ps_t[:, bv, t4 * 64:(t4 + 1) * 64], ckvnT[:, kt, ti * 128:(ti + 1) * 128],
                                                        wukv[:, kt, 128 * h + 64:128 * h + 128], start=(kt == 0), stop=(kt == 1)),
                               r=[WATT, CKV[ti // 4]], w=[PS[bv]])
                    t0_ = tg * 8 + hf * 4
                    op("dve", lambda e: e.tensor_copy(out=Vb[hb][:, t0_:t0_ + 4, voff:voff + 64],
                                                      in_=ps_t[:, bv, 0:256].rearrange("p (a b) -> p a b", b=64)),
                       r=[PS[bv]], w=[VB[hb][tg]])
                for tc in range(NCH):
                    pieces_.append(lambda tc=tc: q_a(tc))
                    pieces_.append(lambda tc=tc: q_b(tc))
                    pieces_.append(lambda tc=tc: k_c(tc))
                    if tc % 2 == 1:
                        tg = tc // 2
                        pieces_.append(lambda tg=tg: v_half(tg, 0))
                        pieces_.append(lambda tg=tg: v_half(tg, 1))
                return pieces_

            sc_rot = [0]
            pt_rot = [0]

            def head_flash(h, nxt_pieces):
                hb = h % 2
                prow = 64 if hb == 0 else 0
                orow = 0 if hb == 0 else 64
                items = [(qc, kt) for qc in range(NCH) for kt in range(4 * qc + 4)]
                state = {}

                def emit_S(idx):
                    qc, kt = items[idx]
                    j = kt - 4 * qc
                    col0 = 128 * j if j > 0 else 0
                    b = sc_rot[0] % 3
                    sc_rot[0] += 1
                    state[idx] = (b, col0)
                    for rep in range(DUP_S if j < 0 else 1):
                        op("pe", lambda e: e.matmul(ps_t[:, b, col0:CH], kT[hb][0:96, kt * 128:(kt + 1) * 128],
                                                    qT[hb][0:96, qc * CH + col0:(qc + 1) * CH], start=True, stop=(j < 0)),
                           r=[KTn[hb][kt // 4], KTr[hb][kt // 4], QTn[hb][qc], QTr[hb][qc]], w=[PS[b]])
                    if j >= 0:
                        op("pe", lambda e: e.matmul(ps_t[:, b, col0:col0 + 128], ident_b[:], maskb_b[:], start=False, stop=True),
                           r=[CONSTS], w=[PS[b]])

                deferred = []

                def emit_norm_a(qc, acc):
                    op("dve", lambda e: e.reciprocal(out=rec[prow:prow + 1, :], in_=ps_t[prow:prow + 1, acc, :]), r=[PS[acc]], w=[REC])

                def emit_norm_b(qc, acc):
                    op("pe", lambda e: e.matmul(psf(5), ones_f[prow:prow + 1, :], rec[prow:prow + 1, :], start=True, stop=True),
                       r=[REC, CONSTS], w=[PS[5]])
                    op("dve", lambda e: e.tensor_copy(out=bcs[orow:orow + 64, :], in_=ps_t[orow:orow + 64, 5, :]),
                       r=[PS[5]], w=[BCS])
                    op("dve", lambda e: e.tensor_tensor(out=attnT[orow:orow + 64, h // 2, qc * CH:(qc + 1) * CH],
                                                        in0=ps_t[orow:orow + 64, acc, :], in1=bcs[orow:orow + 64, :], op=ALU.mult),
                       r=[PS[acc], BCS], w=[ATT[qc]])

                n = len(items)
                emit_S(0)
                if n > 1:
                    emit_S(1)
                for idx in range(n):
                    qc, kt = items[idx]
                    nk = 4 * qc + 4
                    acc = 3 + (qc % 2)
                    b, col0 = state.pop(idx)
                    pi = pt_rot[0] % NPT
                    pt_rot[0] += 1
                    op("act", lambda e: e.activation(out=PT[pi][:, col0:CH], in_=ps_t[:, b, col0:CH], func=AF.Exp),
                       r=[PS[b]], w=[PTB[pi]])
                    if idx + 2 < n:
                        emit_S(idx + 2)
                    op("pe", lambda e: e.matmul(ps_t[:, acc, col0:CH], Vb[hb][:, kt, :], PT[pi][:, col0:CH],
                                                start=(kt == 0), stop=(kt == nk - 1)),
                       r=[VB[hb][kt // 8], VONES[hb], PTB[pi]], w=[PS[acc]])
                    for d in list(deferred):
                        d[0] -= 1
                        if d[0] <= 0:
                            emit_norm_b(d[1], d[2])
                            deferred.remove(d)
                    if kt == nk - 1:
                        emit_norm_a(qc, acc)
                        deferred.append([8, qc, acc])
                    if nxt_pieces and idx % 9 in (2, 6):
                        nxt_pieces.pop(0)()
                for d in deferred:
                    emit_norm_b(d[1], d[2])
                while nxt_pieces:
                    nxt_pieces.pop(0)()

            for p_ in head_prep_pieces(0):
                p_()
            for h in range(8):
                head_flash(h, head_prep_pieces(h + 1) if h + 1 < 8 else [])
            if stop_after == "F2":
                dbg_dump("attnT", attnT[:], ATT[NCH - 1])
            kb.barrier()
        f12.close()
        if stop_after == "F2":
            return finish(nc, kb, out_d, dbg_outs, s_dbg)

        fb = es.enter_context(contextlib.ExitStack())
        ygT = sb(fb, "ygT", [128, 4, S], BF16)
        YG = [Buf("yg%d" % c) for c in range(NCH)]
        with contextlib.ExitStack() as f3:
            s_w3 = kb.dsem("d_w3")
            s_s5 = kb.dsem("d_s5")
            winu = sb(f3, "winu", [128, 8, 512], BF16)
            wglu = sb(f3, "wglu", [128, 4, 512], BF16)
            W3 = Buf("w3")
            dma("pool", winu[:], win_v[:, :, 672:1184], s_w3, w=[W3])
            dma("pool", wglu[:], din["w_glu"].rearrange("(kt p) c -> p kt c", p=128), s_w3, w=[W3])
            cosTab = sb(f3, "cosTab", [128, 16, TC], BF16)
            sinTab = sb(f3, "sinTab", [128, 16, TC], BF16)
            WB = sb(f3, "WB", [128, 16, 2, 128], BF16)
            WA = sb(f3, "WA", [128, 16, 2, 128], BF16)
            LC = sb(f3, "LC", [128, 16, 2, 128], BF16)
            LCa = sb(f3, "LCa", [128, 16, 2, 128], BF16)
            Dsk = sb(f3, "Dsk", [128, 4, 128], BF16)
            K0D = sb(f3, "K0D", [128, 4, 128], BF16)
            rdec = sb(f3, "rdec", [128, 16])
            Er = sb(f3, "Er", [128, 16])
            Ei = sb(f3, "Ei", [128, 16])
            S5C = Buf("s5c")
            with contextlib.ExitStack() as sp_:
                Are = sb(sp_, "Are", [128, 16])
                Aim = sb(sp_, "Aim", [128, 16])
                Ldt = sb(sp_, "Ldt", [128, 16])
                Bre = sb(sp_, "Bre", [128, 16, 16])
                Bim = sb(sp_, "Bim", [128, 16, 16])
                Cin = [sb(sp_, "Cin_re", [128, 2, 2, 64]), sb(sp_, "Cin_im", [128, 2, 2, 64])]
                Csm = [sb(sp_, "Csm_re", [128, 16, 16]), sb(sp_, "Csm_im", [128, 16, 16])]
                BP = sb(sp_, "BP", [128, 16, 2, 128])
                PRM = Buf("s5prm")
                with nc.allow_non_contiguous_dma(reason="small S5 parameter loads"):
                    qs = ["sp", "act"]
                    qi_ = [0]

                    def pdma(dst, src):
                        q_ = qs[qi_[0] % 2]
                        qi_[0] += 1
                        dma(q_, dst, src, s_s5, w=[PRM])
                    Ain = [sb(sp_, "Ain_re", [16, 2, 64]), sb(sp_, "Ain_im", [16, 2, 64])]
                    Bin = [sb(sp_, "Bin_re", [16, 2, 64, 16]), sb(sp_, "Bin_im", [16, 2, 64, 16])]
                    PRM2 = Buf("s5prm2")
                    for ri, (na, nb_) in enumerate([("a_re", "b_re"), ("a_im", "b_im")]):
                        dma(qs[ri], Ain[ri][:], din[na].rearrange("(j two) p -> j two p", two=2), s_s5, w=[PRM2])
                        dma(qs[1 - ri], Bin[ri][:], din[nb_].rearrange("(j two) p n -> j two p n", two=2), s_s5, w=[PRM2])
                    for two in range(2):
                        psl = slice(64 * two, 64 * two + 64)
                        pdma(Ldt[psl, :], din["log_dt"].rearrange("(j two) -> two j", two=2)[two:two + 1, :].partition_broadcast(64))
                    for ri, nm in enumerate(["c_re", "c_im"]):
                        for blk in range(2):
                            for two in range(2):
                                pdma(Cin[ri][:, blk, two, :], din[nm][16 * blk + two:16 * blk + 16:2, :, :])
                for ri, dstA, dstB in ((0, Are, Bre), (1, Aim, Bim)):
                    op("pe", lambda e: e.transpose(ps_t[:, 6, 0:16], Ain[ri][:].rearrange("j a b -> j (a b)"), ident_f[0:16, 0:16]),
                       r=[PRM2, CONSTS], w=[PS[6]])
                    op("dve", lambda e: e.tensor_copy(out=dstA[:], in_=ps_t[:, 6, 0:16]), r=[PS[6]], w=[PRM])
                    for n_ in range(16):
                        op("pe", lambda e: e.transpose(ps_t[:, 7, n_ * 16:(n_ + 1) * 16],
                                                       Bin[ri][:, :, :, n_].rearrange("j a b -> j (a b)"), ident_f[0:16, 0:16]),
                           r=[PRM2, CONSTS], w=[PS[7]])
                    op("dve", lambda e: e.tensor_copy(out=dstB[:].rearrange("p j n -> p n j"),
                                                      in_=ps_t[:, 7, 0:256].rearrange("p (n j) -> p n j", j=16)),
                       r=[PS[7]], w=[PRM])
                sm = {}
                for nm in ["dt", "lr", "ldt", "ang", "mag", "sa", "ca", "abr", "abi", "den", "nr", "fre", "fim", "u1", "u2",
                           "thr", "ta", "tq"]:
                    sm[nm] = sb(sp_, "s5_" + nm, [128, 16])
                tqi = sb(sp_, "s5_tqi", [128, 16], I32)
                SM = Buf("s5sm")
                TMPB = Buf("s5tmp")

                def dv(fn, r=(PRM,), w=None):
                    op("dve", fn, r=list(r) + [SM], w=[SM] if w is None else w)

                def tt(o, a, b, o_):
                    dv(lambda e: e.tensor_tensor(out=o, in0=a, in1=b, op=o_))
                op("act", lambda e: e.activation(out=sm["dt"][:], in_=Ldt[:], func=AF.Exp), r=[PRM], w=[SM])
                dv(lambda e: e.tensor_scalar(out=sm["lr"][:], in0=Are[:], scalar1=-1e-4, scalar2=None, op0=ALU.min))
                tt(sm["ldt"][:], sm["lr"][:], sm["dt"][:], ALU.mult)
                tt(sm["ang"][:], Aim[:], sm["dt"][:], ALU.mult)
                op("act", lambda e: e.activation(out=sm["mag"][:], in_=sm["ldt"][:], func=AF.Exp), r=[SM], w=[SM])
                sincos("s5a", lambda: sm["ang"][:], None, (SM, TMPB, SM), sm["sa"][:], sm["ca"][:],
                       {"a": sm["ta"][:], "q": sm["tq"][:], "qi": tqi[:]})
                tt(sm["abr"][:], sm["mag"][:], sm["ca"][:], ALU.mult)
                tt(sm["abi"][:], sm["mag"][:], sm["sa"][:], ALU.mult)
                tt(sm["u1"][:], sm["lr"][:], sm["lr"][:], ALU.mult)
                tt(sm["u2"][:], Aim[:], Aim[:], ALU.mult)
                tt(sm["den"][:], sm["u1"][:], sm["u2"][:], ALU.add)
                dv(lambda e: e.reciprocal(out=sm["den"][:], in_=sm["den"][:]))
                dv(lambda e: e.tensor_scalar(out=sm["nr"][:], in0=sm["abr"][:], scalar1=-1.0, scalar2=None, op0=ALU.add))
                tt(sm["u1"][:], sm["nr"][:], sm["lr"][:], ALU.mult)
                tt(sm["u2"][:], sm["abi"][:], Aim[:], ALU.mult)
                tt(sm["fre"][:], sm["u1"][:], sm["u2"][:], ALU.add)
                tt(sm["fre"][:], sm["fre"][:], sm["den"][:], ALU.mult)
                tt(sm["u1"][:], sm["abi"][:], sm["lr"][:], ALU.mult)
                tt(sm["u2"][:], sm["nr"][:], Aim[:], ALU.mult)
                tt(sm["fim"][:], sm["u1"][:], sm["u2"][:], ALU.subtract)
                tt(sm["fim"][:], sm["fim"][:], sm["den"][:], ALU.mult)
                dv(lambda e: e.tensor_tensor(out=rdec[:], in0=sm["mag"][:], in1=sm["mag"][:], op=ALU.mult), w=[SM, S5C])
                Bb = [sb(sp_, "Bb_re", [128, 16, 16]), sb(sp_, "Bb_im", [128, 16, 16])]
                bt1 = sb(sp_, "bt1", [128, 16, 16])
                bt2 = sb(sp_, "bt2", [128, 16, 16])
                fre_b = sm["fre"][:].unsqueeze(2).to_broadcast([128, 16, 16])
                fim_b = sm["fim"][:].unsqueeze(2).to_broadcast([128, 16, 16])
                tt(bt1[:], Bre[:], fre_b, ALU.mult)
                tt(bt2[:], Bim[:], fim_b, ALU.mult)
                tt(Bb[0][:], bt1[:], bt2[:], ALU.subtract)
                tt(bt1[:], Bim[:], fre_b, ALU.mult)
                tt(bt2[:], Bre[:], fim_b, ALU.mult)
                tt(Bb[1][:], bt1[:], bt2[:], ALU.add)
                dv(lambda e: e.memset(BP[:], 0.0))
                for two in range(2):
                    psl = slice(64 * two, 64 * two + 64)
                    for r_ in range(4):
                        c0 = 32 * r_ + 16 * two
                        for ri in range(2):
                            dv(lambda e: e.tensor_copy(out=BP[psl, r_::4, ri, c0:c0 + 16], in_=Bb[ri][psl, r_::4, :]))
                for jg in range(8):
                    bnk = 6 + jg % 2
                    for q_ in range(4):
                        j, ri = 2 * jg + q_ // 2, q_ % 2
                        op("pe", lambda e: e.transpose(ps_t[:, bnk, q_ * 128:(q_ + 1) * 128], BP[:, j, ri, :], ident_f[:]),
                           r=[SM, CONSTS], w=[PS[bnk]])
                    op("dve", lambda e: e.tensor_copy(out=WB[:, 2 * jg:2 * jg + 2, :, :].rearrange("p a b c -> p (a b c)"),
                                                      in_=ps_t[:, bnk, :]), r=[PS[bnk]], w=[S5C])
                BPh = sb(sp_, "BPh", [128, 16, 2, 128], BF16)
                dv(lambda e: e.tensor_copy(out=BPh[:], in_=BP[:]))
                AB = [sb(sp_, "AB_re", [128, 16, 16]), sb(sp_, "AB_im", [128, 16, 16])]
                abr_b = sm["abr"][:].unsqueeze(2).to_broadcast([128, 16, 16])
                abi_b = sm["abi"][:].unsqueeze(2).to_broadcast([128, 16, 16])
                tt(bt1[:], Bb[0][:], abr_b, ALU.mult)
                tt(bt2[:], Bb[1][:], abi_b, ALU.mult)
                tt(AB[0][:], bt1[:], bt2[:], ALU.subtract)
                tt(bt1[:], Bb[1][:], abr_b, ALU.mult)
                tt(bt2[:], Bb[0][:], abi_b, ALU.mult)
                tt(AB[1][:], bt1[:], bt2[:], ALU.add)
                for two in range(2):
                    psl = slice(64 * two, 64 * two + 64)
                    for r_ in range(4):
                        c0 = 32 * r_ + 16 * two
                        for ri in range(2):
                            dv(lambda e: e.tensor_copy(out=BP[psl, r_::4, ri, c0:c0 + 16], in_=AB[ri][psl, r_::4, :]))
                for jg in range(8):
                    bnk = 6 + jg % 2
                    for q_ in range(4):
                        j, ri = 2 * jg + q_ // 2, q_ % 2
                        op("pe", lambda e: e.transpose(ps_t[:, bnk, q_ * 128:(q_ + 1) * 128], BP[:, j, ri, :], ident_f[:]),
                           r=[SM, CONSTS], w=[PS[bnk]])
                    op("dve", lambda e: e.tensor_copy(out=WA[:, 2 * jg:2 * jg + 2, :, :].rearrange("p a b c -> p (a b c)"),
                                                      in_=ps_t[:, bnk, :]), r=[PS[bnk]], w=[S5C])
                for ri in range(2):
                    for blk in range(2):
                        bnk = 6 + blk
                        op("pe", lambda e: e.transpose(ps_t[:, bnk, 0:128], Cin[ri][:, blk, :, :].rearrange("p a b -> p (a b)"),
                                                       ident_f[:]), r=[PRM, CONSTS], w=[PS[bnk]])
                        op("dve", lambda e: e.tensor_copy(out=Csm[ri][:, 8 * blk:8 * blk + 8, :].rearrange("p a b -> p (a b)"),
                                                          in_=ps_t[:, bnk, 0:128]), r=[PS[bnk]], w=[SM])
                op("dve", lambda e: e.memset(LC[:], 0.0), w=[S5C])
                for two in range(2):
                    psl = slice(64 * two, 64 * two + 64)
                    for r_ in range(4):
                        c0 = 32 * r_ + 16 * two
                        op("dve", lambda e: e.tensor_copy(out=LC[psl, r_::4, 0, c0:c0 + 16], in_=Csm[0][psl, r_::4, :]),
                           r=[SM], w=[S5C])
                        op("dve", lambda e: e.tensor_scalar(out=LC[psl, r_::4, 1, c0:c0 + 16], in0=Csm[1][psl, r_::4, :],
                                                            scalar1=-1.0, scalar2=None, op0=ALU.mult), r=[SM], w=[S5C])
                for m in range(4):
                    op("dve", lambda e: e.tensor_scalar(out=Dsk[:, m, :], in0=ident_f[:], scalar1=pcol("d_skip", m), scalar2=None,
                                                        op0=ALU.mult), r=[CONSTS], w=[S5C])
                CA = [sb(sp_, "CA_re", [128, 16, 16]), sb(sp_, "CA_im", [128, 16, 16])]
                tt(bt1[:], Csm[0][:], abr_b, ALU.mult)
                tt(bt2[:], Csm[1][:], abi_b, ALU.mult)
                tt(CA[0][:], bt1[:], bt2[:], ALU.subtract)
                tt(bt1[:], Csm[0][:], abi_b, ALU.mult)
                tt(bt2[:], Csm[1][:], abr_b, ALU.mult)
                tt(CA[1][:], bt1[:], bt2[:], ALU.add)
                op("dve", lambda e: e.memset(LCa[:], 0.0), w=[S5C])
                for two in range(2):
                    psl = slice(64 * two, 64 * two + 64)
                    for r_ in range(4):
                        c0 = 32 * r_ + 16 * two
                        op("dve", lambda e: e.tensor_copy(out=LCa[psl, r_::4, 0, c0:c0 + 16], in_=CA[0][psl, r_::4, :]),
                           r=[SM], w=[S5C])
                        op("dve", lambda e: e.tensor_scalar(out=LCa[psl, r_::4, 1, c0:c0 + 16], in0=CA[1][psl, r_::4, :],
                                                            scalar1=-1.0, scalar2=None, op0=ALU.mult), r=[SM], w=[S5C])
                for m in range(4):
                    bnk = 6 + m % 2
                    for jj in range(4):
                        for ri in range(2):
                            op("pe", lambda e: e.matmul(ps_t[:, bnk, 0:128], BPh[:, 4 * m + jj, ri, :], LC[:, 4 * m + jj, ri, :],
                                                        start=(jj == 0 and ri == 0), stop=(jj == 3 and ri == 1)),
                               r=[SM, S5C], w=[PS[bnk]])
                    op("dve", lambda e: e.scalar_tensor_tensor(out=K0D[:, m, :], in0=ident_f[:], scalar=pcol("d_skip", m),
                                                               in1=ps_t[:, bnk, 0:128], op0=ALU.mult, op1=ALU.add),
                       r=[PS[bnk], CONSTS], w=[S5C])
                dv(lambda e: e.tensor_scalar(out=sm["tq"][:], in0=sm["ang"][:], scalar1=1.0 / TWO_PI, scalar2=None, op0=ALU.mult))
                dv(lambda e: e.tensor_copy(out=tqi[:], in_=sm["tq"][:]))
                dv(lambda e: e.tensor_copy(out=sm["tq"][:], in_=tqi[:]))
                dv(lambda e: e.scalar_tensor_tensor(out=sm["thr"][:], in0=sm["tq"][:], scalar=-CW1, in1=sm["ang"][:],
                                                    op0=ALU.mult, op1=ALU.add))
                dv(lambda e: e.scalar_tensor_tensor(out=sm["thr"][:], in0=sm["tq"][:], scalar=-CW2, in1=sm["thr"][:],
                                                    op0=ALU.mult, op1=ALU.add))
                dv(lambda e: e.tensor_scalar(out=sm["u2"][:], in0=sm["thr"][:], scalar1=2.0, scalar2=None, op0=ALU.mult))
                dv(lambda e: e.tensor_scalar(out=sm["tq"][:], in0=sm["u2"][:], scalar1=1.0 / TWO_PI, scalar2=None, op0=ALU.mult))
                dv(lambda e: e.tensor_copy(out=tqi[:], in_=sm["tq"][:]))
                dv(lambda e: e.tensor_copy(out=sm["tq"][:], in_=tqi[:]))
                dv(lambda e: e.scalar_tensor_tensor(out=sm["thr"][:], in0=sm["tq"][:], scalar=-CW1, in1=sm["u2"][:],
                                                    op0=ALU.mult, op1=ALU.add))
                dv(lambda e: e.scalar_tensor_tensor(out=sm["thr"][:], in0=sm["tq"][:], scalar=-CW2, in1=sm["thr"][:],
                                                    op0=ALU.mult, op1=ALU.add))
                dv(lambda e: e.tensor_scalar(out=sm["u1"][:], in0=sm["thr"][:], scalar1=float(TC), scalar2=None, op0=ALU.mult))
                sincos("s5e", lambda: sm["u1"][:], None, (SM, TMPB, S5C), Ei[:], Er[:],
                       {"a": sm["ta"][:], "q": sm["tq"][:], "qi": tqi[:]})
                tgA = sb(sp_, "tgA", [128, 4, TC])
                tga = sb(sp_, "tga", [128, 4, TC])
                tgq = sb(sp_, "tgq", [128, 4, TC])
                tgqi = sb(sp_, "tgqi", [128, 4, TC], I32)
                TGA, TGT = Buf("tga"), Buf("tgt")
                for tg in range(4):
                    op("dve", lambda e: e.tensor_tensor(out=tgA[:], in0=sm["thr"][:, 4 * tg:4 * tg + 4].unsqueeze(2).to_broadcast([128, 4, TC]),
                                                        in1=iota_f[:, 0:TC].unsqueeze(1).to_broadcast([128, 4, TC]), op=ALU.mult),
                       r=[SM, CONSTS], w=[TGA])
                    sincos("s5t", lambda: tgA[:], None, (TGA, TGT, S5C), sinTab[:, 4 * tg:4 * tg + 4, :], cosTab[:, 4 * tg:4 * tg + 4, :],
                           {"a": tga[:], "q": tgq[:], "qi": tgqi[:]})
                kb.barrier()

            xs_t = [sb(f3, "xs%d" % i_, [128, 1024]) for i_ in range(2)]
            st_t = [sb(f3, "st%d" % i_, [128, 2, 6]) for i_ in range(2)]
            mv_t = [sb(f3, "mv%d" % i_, [128, 2]) for i_ in range(2)]
            rs_t = [sb(f3, "rs%d" % i_, [128, 1]) for i_ in range(2)]
            h0T = [sb(f3, "h0T0", [128, 8, CH], BF16), sb(f3, "h0T1", [128, 8, CH], BF16)]
            H0T = [Buf("h0T0"), Buf("h0T1")]
            uc = [sb(f3, "uc0", [128, 4, CH], BF16), sb(f3, "uc1", [128, 4, CH], BF16)]
            UC = [[Buf() for _ in range(4)] for _ in range(2)]
            ygc = sb(f3, "ygc", [128, 4, CH], BF16)
            YGC = [Buf() for _ in range(4)]
            sg = [sb(f3, "sg0", [128, CH], BF16), sb(f3, "sg1", [128, CH], BF16)]
            SG = [Buf(), Buf()]
            bre = [sb(f3, "bre0", [128, TC], BF16), sb(f3, "bre1", [128, TC], BF16)]
            bim = [sb(f3, "bim0", [128, TC], BF16), sb(f3, "bim1", [128, TC], BF16)]
            BRE = [Buf(), Buf()]
            BIM = [Buf(), Buf()]
            tmps = [{n_: sb(f3, "s5w%d_" % l_ + n_, [128, TC], BF16) for n_ in ["ta", "tb", "tc", "td", "vre", "vim", "zre", "zim"]}
                    for l_ in range(2)]
            TBs = [{n_: Buf() for n_ in tmps[0]} for l_ in range(2)]
            ZLs = [Buf("zl0"), Buf("zl1")]
            xr = [sb(f3, "xr%d" % i, [128, TC + 2], BF16) for i in range(8)]
            xi = [sb(f3, "xi%d" % i, [128, TC + 2], BF16) for i in range(8)]
            xlast = [sb(f3, "xlast_re", [128, 16], BF16), sb(f3, "xlast_im", [128, 16], BF16)]
            XL = [Buf("xl0"), Buf("xl1")]
            op("dve", lambda e: e.memset(xlast[0][:], 0.0), w=[XL[0], XL[1]])
            op("dve", lambda e: e.memset(xlast[1][:], 0.0), w=[XL[0], XL[1]])
            XR = [Buf() for _ in range(8)]
            XI = [Buf() for _ in range(8)]
            zin = [sb(f3, "zin_re", [128, 16]), sb(f3, "zin_im", [128, 16])]
            zl = [sb(f3, "zl_re", [128, 16]), sb(f3, "zl_im", [128, 16])]
            cw = [sb(f3, "cw%d" % i, [128, 16]) for i in range(4)]
            ZIN, ZL, CWB = Buf("zin"), Buf("zl"), Buf("cw")
            op("dve", lambda e: e.memset(zin[0][:], 0.0), w=[ZIN])
            op("dve", lambda e: e.memset(zin[1][:], 0.0), w=[ZIN])
            rot3 = [0]

            def gen_bank():
                b = 6 + rot3[0] % 2
                rot3[0] += 1
                return b

            def f3_front_steps(c):
                hb_ = c % 2

                def evac_f3(i, ft, pap, PB):
                    op("act", lambda e: e.activation(out=h0T[hb_][:, ft, i * 128:(i + 1) * 128], in_=pap,
                                                     func=AF.Identity, bias=pcol("ln_in_b", ft), scale=pcol("ln_in_g", ft)),
                       r=[PB, CONSTS], w=[H0T[hb_]])
                steps_ = ln_transpose_chunk(c, xs_t, st_t, mv_t, rs_t, [0], evac_f3, as_steps=True)

                def up(m):
                    b = gen_bank()
                    for kt in range(8):
                        op("pe", lambda e: e.matmul(psf(b), winu[:, kt, m * 128:(m + 1) * 128], h0T[hb_][:, kt, :],
                                                    start=(kt == 0), stop=(kt == 7)), r=[W3, H0T[hb_]], w=[PS[b]])
                    op("act", lambda e: e.activation(out=uc[hb_][:, m, :], in_=psf(b), func=AF.Copy), r=[PS[b]], w=[UC[hb_][m]])
                for m in range(4):
                    steps_.append(lambda m=m: up(m))
                return steps_

            for st_ in f3_front_steps(0):
                st_()
            for c in range(NCH):
                hb = c % 2
                csl = slice(c * CH, (c + 1) * CH)
                nxt_steps = f3_front_steps(c + 1) if c + 1 < NCH else []
                def tile_steps(j, lane):
                    m = j // 4
                    bs = lane
                    T = tmps[lane]
                    TBl = TBs[lane]
                    pr_, pi_ = 2 + 2 * bs, 3 + 2 * bs
                    ueo = uc[hb][:, m, :].rearrange("p (c two) -> p two c", two=2)
                    for ri, pb_ in ((0, pr_), (1, pi_)):
                        op("pe", lambda e: e.matmul(ps_t[:, pb_, 0:TC], WA[:, j, ri, :], ueo[:, 0, :], start=True, stop=False),
                           r=[S5C, UC[hb][m]], w=[PS[pb_]])
                        op("pe", lambda e: e.matmul(ps_t[:, pb_, 0:TC], WB[:, j, ri, :], ueo[:, 1, :], start=False, stop=True),
                           r=[S5C, UC[hb][m]], w=[PS[pb_]])
                    op("act", lambda e: e.activation(out=bre[bs][:], in_=ps_t[:, pr_, 0:TC], func=AF.Copy), r=[PS[pr_]], w=[BRE[bs]])
                    op("act", lambda e: e.activation(out=bim[bs][:], in_=ps_t[:, pi_, 0:TC], func=AF.Copy), r=[PS[pi_]], w=[BIM[bs]])
                    yield
                    cs_, sn_ = cosTab[:, j, :], sinTab[:, j, :]

                    def d2(o, a, b_, o_, rb, wb):
                        op("dve", lambda e: e.tensor_tensor(out=o, in0=a, in1=b_, op=o_), r=rb, w=wb)
                    d2(T["ta"][:], bre[bs][:], cs_, ALU.mult, [BRE[bs], S5C], [TBl["ta"]])
                    yield
                    d2(T["tb"][:], bim[bs][:], sn_, ALU.mult, [BIM[bs], S5C], [TBl["tb"]])
                    yield
                    d2(T["tc"][:], bim[bs][:], cs_, ALU.mult, [BIM[bs], S5C], [TBl["tc"]])
                    yield
                    d2(T["td"][:], bre[bs][:], sn_, ALU.mult, [BRE[bs], S5C], [TBl["td"]])
                    yield
                    d2(T["vre"][:], T["ta"][:], T["tb"][:], ALU.add, [TBl["ta"], TBl["tb"]], [TBl["vre"]])
                    yield
                    d2(T["vim"][:], T["tc"][:], T["td"][:], ALU.subtract, [TBl["tc"], TBl["td"]], [TBl["vim"]])
                    yield
                    op("dve", lambda e: e.tensor_tensor_scan(out=T["zre"][:], data0=rdec[:, j:j + 1].to_broadcast([128, TC]),
                                                             data1=T["vre"][:], initial=zin[0][:, j:j + 1], op0=ALU.mult, op1=ALU.add),
                       r=[TBl["vre"], ZIN, S5C], w=[TBl["zre"]])
                    yield
                    op("dve", lambda e: e.tensor_tensor_scan(out=T["zim"][:], data0=rdec[:, j:j + 1].to_broadcast([128, TC]),
                                                             data1=T["vim"][:], initial=zin[1][:, j:j + 1], op0=ALU.mult, op1=ALU.add),
                       r=[TBl["vim"], ZIN, S5C], w=[TBl["zim"]])
                    yield
                    op("dve", lambda e: e.tensor_copy(out=zl[0][:, j:j + 1], in_=T["zre"][:, TC - 1:TC]), r=[TBl["zre"]], w=[ZLs[lane]])
                    op("dve", lambda e: e.tensor_copy(out=zl[1][:, j:j + 1], in_=T["zim"][:, TC - 1:TC]), r=[TBl["zim"]], w=[ZLs[lane]])
                    yield
                    xs_ = (m % 2) * 4 + j % 4
                    op("dve", lambda e: e.tensor_copy(out=xr[xs_][:, 1:2], in_=xlast[0][:, j:j + 1]), r=[XL[lane]], w=[XR[xs_]])
                    op("dve", lambda e: e.tensor_copy(out=xi[xs_][:, 1:2], in_=xlast[1][:, j:j + 1]), r=[XL[lane]], w=[XI[xs_]])
                    d2(T["ta"][:], T["zre"][:], cs_, ALU.mult, [TBl["zre"], S5C], [TBl["ta"]])
                    yield
                    d2(T["tb"][:], T["zim"][:], sn_, ALU.mult, [TBl["zim"], S5C], [TBl["tb"]])
                    yield
                    d2(T["tc"][:], T["zim"][:], cs_, ALU.mult, [TBl["zim"], S5C], [TBl["tc"]])
                    yield
                    d2(T["td"][:], T["zre"][:], sn_, ALU.mult, [TBl["zre"], S5C], [TBl["td"]])
                    yield
                    d2(xr[xs_][:, 2:TC + 2], T["ta"][:], T["tb"][:], ALU.subtract, [TBl["ta"], TBl["tb"], XR[xs_]], [XR[xs_]])
                    yield
                    d2(xi[xs_][:, 2:TC + 2], T["tc"][:], T["td"][:], ALU.add, [TBl["tc"], TBl["td"], XI[xs_]], [XI[xs_]])
                    yield
                    op("dve", lambda e: e.tensor_copy(out=xlast[0][:, j:j + 1], in_=xr[xs_][:, TC + 1:TC + 2]), r=[XR[xs_]], w=[XL[lane]])
                    op("dve", lambda e: e.tensor_copy(out=xlast[1][:, j:j + 1], in_=xi[xs_][:, TC + 1:TC + 2]), r=[XI[xs_]], w=[XL[lane]])
                    yield

                def c_matmuls(m):
                    b = gen_bank()
                    ueo = uc[hb][:, m, :].rearrange("p (c two) -> p two c", two=2)
                    for jj in range(4):
                        j2 = 4 * m + jj
                        x2 = (m % 2) * 4 + jj
                        op("pe", lambda e: e.matmul(ps_t[:, b, 0:TC], LC[:, j2, 0, :], xr[x2][:, 2:TC + 2], start=(jj == 0), stop=False),
                           r=[S5C, XR[x2]], w=[PS[b]])
                        op("pe", lambda e: e.matmul(ps_t[:, b, 0:TC], LC[:, j2, 1, :], xi[x2][:, 2:TC + 2], start=False, stop=False),
                           r=[S5C, XI[x2]], w=[PS[b]])
                    op("pe", lambda e: e.matmul(ps_t[:, b, 0:TC], Dsk[:, m, :], ueo[:, 1, :], start=False, stop=True),
                       r=[S5C, UC[hb][m]], w=[PS[b]])
                    for jj in range(4):
                        j2 = 4 * m + jj
                        x2 = (m % 2) * 4 + jj
                        op("pe", lambda e: e.matmul(ps_t[:, b, TC:2 * TC], LCa[:, j2, 0, :], xr[x2][:, 1:TC + 1], start=(jj == 0), stop=False),
                           r=[S5C, XR[x2]], w=[PS[b]])
                        op("pe", lambda e: e.matmul(ps_t[:, b, TC:2 * TC], LCa[:, j2, 1, :], xi[x2][:, 1:TC + 1], start=False, stop=False),
                           r=[S5C, XI[x2]], w=[PS[b]])
                    op("pe", lambda e: e.matmul(ps_t[:, b, TC:2 * TC], K0D[:, m, :], ueo[:, 0, :], start=False, stop=True),
                       r=[S5C, UC[hb][m]], w=[PS[b]])
                    yeo = ygc[:, m, :].rearrange("p (c two) -> p two c", two=2)
                    op("act", lambda e: e.activation(out=yeo[:, 1, :], in_=ps_t[:, b, 0:TC], func=AF.Gelu_apprx_tanh), r=[PS[b]], w=[YGC[m]])
                    op("act", lambda e: e.activation(out=yeo[:, 0, :], in_=ps_t[:, b, TC:2 * TC], func=AF.Gelu_apprx_tanh), r=[PS[b]], w=[YGC[m]])

                def start_pair(jp):
                    gs = [tile_steps(2 * jp, 0), tile_steps(2 * jp + 1, 1)]
                    for g_ in gs:
                        next(g_)
                    return gs
                pair = start_pair(0)
                for jp in range(8):
                    alive = list(pair)
                    while alive:
                        for g_ in list(alive):
                            try:
                                next(g_)
                            except StopIteration:
                                alive.remove(g_)
                    if jp + 1 < 8:
                        pair = start_pair(jp + 1)
                    if jp % 2 == 1:
                        c_matmuls(jp // 2)
                    if jp >= 1:
                        for _ in range(2):
                            if nxt_steps:
                                nxt_steps.pop(0)()
                while nxt_steps:
                    nxt_steps.pop(0)()
                for m2 in range(4):
                    b = gen_bank()
                    for m in range(4):
                        op("pe", lambda e: e.matmul(psf(b), wglu[:, m, m2 * 128:(m2 + 1) * 128], ygc[:, m, :], start=(m == 0), stop=(m == 3)),
                           r=[W3, YGC[m]], w=[PS[b]])
                    op("act", lambda e: e.activation(out=sg[m2 % 2][:], in_=psf(b), func=AF.Sigmoid, bias=pcol("b_glu", m2), scale=1.0),
                       r=[PS[b], CONSTS], w=[SG[m2 % 2]])
                    op("dve", lambda e: e.tensor_tensor(out=ygT[:, m2, csl], in0=ygc[:, m2, :], in1=sg[m2 % 2][:], op=ALU.mult),
                       r=[YGC[m2], SG[m2 % 2]], w=[YG[c]])
                def c2(o, a, b_, o_, rb, wb):
                    op("dve", lambda e: e.tensor_tensor(out=o, in0=a, in1=b_, op=o_), r=rb, w=wb)
                c2(cw[0][:], Er[:], zl[0][:], ALU.mult, [S5C, ZLs[0], ZLs[1]], [CWB])
                c2(cw[1][:], Ei[:], zl[1][:], ALU.mult, [S5C, ZLs[0], ZLs[1]], [CWB])
                c2(cw[2][:], Er[:], zl[1][:], ALU.mult, [S5C, ZLs[0], ZLs[1]], [CWB])
                c2(cw[3][:], Ei[:], zl[0][:], ALU.mult, [S5C, ZLs[0], ZLs[1]], [CWB])
                c2(zin[0][:], cw[0][:], cw[1][:], ALU.subtract, [CWB], [ZIN])
                c2(zin[1][:], cw[2][:], cw[3][:], ALU.add, [CWB], [ZIN])
            if stop_after == "F3":
                dbg_dump("ygT", ygT[:], YG[NCH - 1])
            kb.barrier()
        if stop_after == "F3":
            fb.close()
            return finish(nc, kb, out_d, dbg_outs, s_dbg)

        with contextlib.ExitStack() as bk:
            NSLOT = 4
            RSZ = 4096
            ring = [sb(bk, "ring%d" % i, [128, RSZ], BF16) for i in range(NSLOT)]
            RING = [Buf("ring%d" % i) for i in range(NSLOT)]
            s_ring = [kb.dsem("d_ring%d" % i) for i in range(NSLOT)]
            xs_t = [sb(bk, "xs0", [128, 1024]), sb(bk, "xs1", [128, 1024])]
            st_t = [sb(bk, "st0", [128, 2, 6]), sb(bk, "st1", [128, 2, 6])]
            mv_t = [sb(bk, "mv0", [128, 2]), sb(bk, "mv1", [128, 2])]
            rs_t = [sb(bk, "rs0", [128, 1]), sb(bk, "rs1", [128, 1])]
            resT = [sb(bk, "resT%d" % i, [128, 8, CH]) for i in range(2)]
            hbT = [sb(bk, "hbT%d" % i, [128, 8, CH], BF16) for i in range(2)]
            mgT = sb(bk, "mgT", [128, 8, CH], BF16)
            RES = [[Buf("res%d_%d" % (i, m)) for m in range(8)] for i in range(2)]
            HB = [[Buf("hb%d_%d" % (i, m)) for m in range(8)] for i in range(2)]
            MG = [Buf("mg%d" % m) for m in range(8)]
            sga = [sb(bk, "sga%d" % i, [128, CH], BF16) for i in range(2)]
            sgb = [sb(bk, "sgb%d" % i, [128, CH], BF16) for i in range(2)]
            SGA = [Buf(), Buf()]
            SGB = [Buf(), Buf()]
            t1 = sb(bk, "bt1", [128, CH])
            t2 = sb(bk, "bt2", [128, CH])
            T1, T2 = Buf("t1"), Buf("t2")
            reluT = [sb(bk, "relu%d" % i, [128, 8, CH], BF16) for i in range(2)]
            RL = [[Buf() for _ in range(8)] for _ in range(2)]
            rtmp = [sb(bk, "rtmp%d" % i, [128, CH], BF16) for i in range(2)]
            RT = [Buf(), Buf()]
            mean_s = sb(bk, "mean_s", [128, CH])
            var_s = sb(bk, "var_s", [128, CH])
            rstd_s = sb(bk, "rstd_s", [128, CH])
            nmr_s = sb(bk, "nmr_s", [128, CH])
            LNB = Buf("lnb")
            ntost = sb(bk, "ntost", [128, 1024])
            NT = [Buf(), Buf()]
            pst = sb(bk, "pst", [128, 256])
            pbf = sb(bk, "pbf", [128, 256], BF16)
            PST, PBF = Buf(), Buf()
            s_p = kb.dsem("d_p0")
            pT = sb(bk, "pT", [128, 2, CH], BF16)
            PTT = Buf("pT")
            s_out = kb.dsem("d_out")
            rotb = [0]

            def gbank():
                b = 2 + rotb[0] % 6
                rotb[0] += 1
                return b

            def chunk_pieces():
                lst = []
                lst += [("wo", 0), ("wo", 1)]
                for g in range(4):
                    lst += [("up", g, 0), ("up", g, 1), ("dn", g, 0), ("dn", g, 1)]
                lst += [("mix", m) for m in range(4)]
                lst += [("pg", 0), ("ple", 0), ("pg", 1)]
                lst += [("mix", m) for m in range(4, 8)]
                return lst
            pieces = [("mix", m) for m in range(8)]
            for c in range(NCH):
                cp = chunk_pieces()
                if c == NCH - 1:
                    cp = [p_ for p_ in cp if p_[0] != "mix"]
                pieces += cp
            ring_state = {"issued": 0, "cur": -1}

            def ring_issue():
                k = ring_state["issued"]
                kind = pieces[k][0]
                sl = k % NSLOT
                if kind == "mix":
                    m = pieces[k][1]
                    dst, src = ring[sl][:, 0:3072], mix_d[:, m, :, :].rearrange("p k c -> p (k c)")
                elif kind == "wo":
                    hf = pieces[k][1]
                    dst = ring[sl][:, :].rearrange("p (k c) -> p k c", k=8)
                    src = wo_d[:, :, hf * 512:(hf + 1) * 512]
                elif kind == "up":
                    g, hf = pieces[k][1], pieces[k][2]
                    dst = ring[sl][:, :].rearrange("p (k c) -> p k c", k=8)
                    src = wup_d[:, g, :, hf * 512:(hf + 1) * 512]
                elif kind == "dn":
                    g, hf = pieces[k][1], pieces[k][2]
                    dst = ring[sl][:, :].rearrange("p (k c) -> p k c", k=8)
                    src = wdn_d[:, g, :, hf * 512:(hf + 1) * 512]
                elif kind == "pg":
                    hf = pieces[k][1]
                    dst = ring[sl][:, :].rearrange("p (k c) -> p k c", k=8)
                    src = wpg_d[:, :, hf * 512:(hf + 1) * 512]
                else:
                    dst, src = ring[sl][:, 0:2048], wple_d.rearrange("p k c -> p (k c)")
                dma("sp", dst, src, s_ring[sl], r=[CAST], w=[RING[sl]])
                ring_state["issued"] += 1

            def ring_next(kind, live_prev=0):
                ring_state["cur"] += 1
                k = ring_state["cur"]
                assert pieces[k][0] == kind, (k, pieces[k], kind)
                while ring_state["issued"] < min(len(pieces), k + NSLOT - live_prev):
                    ring_issue()
                return ring[k % NSLOT], RING[k % NSLOT]

            def a1_s(c, i):
                ln_in_tile(4 * c + i, xs_t, st_t, mv_t, rs_t, (4 * c + i) % 2, False)

            def a1_t(c, i):
                par = c % 2
                slot = (4 * c + i) % 2
                for ft in range(8):
                    bank = ft // 4
                    op("pe", lambda e: e.transpose(ps_t[:, bank, (ft % 4) * 128:(ft % 4 + 1) * 128],
                                                   xs_t[slot][:, ft * 128:(ft + 1) * 128], ident_f[:]),
                       r=[XS[slot], CONSTS], w=[PS[bank]])
                for ft in range(8):
                    bank = ft // 4
                    op("act", lambda e: e.activation(out=resT[par][:, ft, i * 128:(i + 1) * 128],
                                                     in_=ps_t[:, bank, (ft % 4) * 128:(ft % 4 + 1) * 128], func=AF.Identity,
                                                     bias=pcol("ln_in_b", ft, True), scale=pcol("ln_in_g", ft, True)),
                       r=[PS[bank], CONSTS], w=[RES[par][ft]])

            def a1_fin(c):
                par = c % 2
                for ft in range(8):
                    op("dve", lambda e: e.tensor_scalar(out=hbT[par][:, ft, :], in0=resT[par][:, ft, :], scalar1=1.0 / ALPHA,
                                                        scalar2=None, op0=ALU.mult), r=[RES[par][ft]], w=[HB[par][ft]])

            a2_state = {}

            def a2_pe(c, m):
                par = c % 2
                csl = slice(c * CH, (c + 1) * CH)
                rg, RG = ring_next("mix")
                wv = rg[:, 0:3072].rearrange("p (k c) -> p k c", k=24)
                ba, bb_, bc_, bd_ = gbank(), gbank(), gbank(), gbank()
                for kt in range(8):
                    op("pe", lambda e: e.matmul(psf(ba), wv[:, kt, :], hbT[par][:, kt, :], start=(kt == 0), stop=(kt == 7)),
                       r=[RG, HB[par][kt]], w=[PS[ba]])
                for kt in range(8):
                    op("pe", lambda e: e.matmul(psf(bb_), wv[:, 8 + kt, :], hbT[par][:, kt, :], start=(kt == 0), stop=(kt == 7)),
                       r=[RG, HB[par][kt]], w=[PS[bb_]])
                for kt in range(4):
                    op("pe", lambda e: e.matmul(psf(bc_), wv[:, 16 + kt, :], attnT[:, kt, csl], start=(kt == 0), stop=(kt == 3)),
                       r=[RG, ATT[c]], w=[PS[bc_]])
                for kt in range(4):
                    op("pe", lambda e: e.matmul(psf(bd_), wv[:, 20 + kt, :], ygT[:, kt, csl], start=(kt == 0), stop=(kt == 3)),
                       r=[RG, YG[c]], w=[PS[bd_]])
                a2_state[(c, m)] = (ba, bb_, bc_, bd_)

            def a2_ev(c, m):
                ba, bb_, bc_, bd_ = a2_state.pop((c, m))
                mi = m % 2
                op("act", lambda e: e.activation(out=sga[mi][:], in_=psf(ba), func=AF.Sigmoid, bias=pcol("b_gate_a", m), scale=1.0),
                   r=[PS[ba], CONSTS], w=[SGA[mi]])
                op("act", lambda e: e.activation(out=sgb[mi][:], in_=psf(bb_), func=AF.Sigmoid, bias=pcol("b_gate_b", m), scale=1.0),
                   r=[PS[bb_], CONSTS], w=[SGB[mi]])
                op("dve", lambda e: e.tensor_tensor(out=t1[:], in0=psf(bc_), in1=sga[mi][:], op=ALU.mult), r=[PS[bc_], SGA[mi]], w=[T1])
                op("dve", lambda e: e.tensor_tensor(out=t2[:], in0=psf(bd_), in1=sgb[mi][:], op=ALU.mult), r=[PS[bd_], SGB[mi]], w=[T2])
                op("dve", lambda e: e.tensor_tensor(out=mgT[:, m, :], in0=t1[:], in1=t2[:], op=ALU.add), r=[T1, T2], w=[MG[m]])

            def acc_res(par, m, b):
                op("dve", lambda e: e.tensor_tensor(out=resT[par][:, m, :], in0=psf(b), in1=resT[par][:, m, :], op=ALU.add),
                   r=[PS[b], RES[par][m]], w=[RES[par][m]])

            def b3(c):
                par = c % 2
                for hf in range(2):
                    rg, RG = ring_next("wo")
                    wv = rg[:, :].rearrange("p (k c) -> p k c", k=8)
                    for mm in range(4):
                        m = 4 * hf + mm
                        b = gbank()
                        for kt in range(8):
                            op("pe", lambda e: e.matmul(psf(b), wv[:, kt, mm * 128:(mm + 1) * 128], mgT[:, kt, :],
                                                        start=(kt == 0), stop=(kt == 7)), r=[RG, MG[kt]], w=[PS[b]])
                        acc_res(par, m, b)

            sq = reluT[0]
            SQ = RL[0]
            ln_state = {}

            def ln_a1(par):
                for m in range(8):
                    op("act", lambda e: e.activation(out=hbT[par][:, m, :], in_=resT[par][:, m, :], func=AF.Copy),
                       r=[RES[par][m]], w=[HB[par][m]])
                    op("act", lambda e: e.activation(out=sq[:, m, :], in_=resT[par][:, m, :], func=AF.Square),
                       r=[RES[par][m]], w=[SQ[m]])

            def ln_a2(par):
                bm, bq = gbank(), gbank()
                for m in range(8):
                    op("pe", lambda e: e.matmul(psf(bm), onesd_b[:], hbT[par][:, m, :], start=(m == 0), stop=(m == 7)),
                       r=[CONSTS, HB[par][m]], w=[PS[bm]])
                for m in range(8):
                    op("pe", lambda e: e.matmul(psf(bq), onesd_b[:], sq[:, m, :], start=(m == 0), stop=(m == 7)),
                       r=[CONSTS, SQ[m]], w=[PS[bq]])
                op("act", lambda e: e.activation(out=mean_s[:], in_=psf(bm), func=AF.Copy), r=[PS[bm]], w=[LNB])
                op("dve", lambda e: e.tensor_tensor(out=var_s[:], in0=mean_s[:], in1=mean_s[:], op=ALU.mult), r=[LNB], w=[LNB])
                op("dve", lambda e: e.tensor_tensor(out=var_s[:], in0=psf(bq), in1=var_s[:], op=ALU.subtract), r=[PS[bq], LNB], w=[LNB])
                op("act", lambda e: e.activation(out=rstd_s[:], in_=var_s[:], func=AF.Ln, bias=1e-5, scale=1.0), r=[LNB], w=[LNB])
                op("act", lambda e: e.activation(out=rstd_s[:], in_=rstd_s[:], func=AF.Exp, scale=-0.5), r=[LNB], w=[LNB])
                op("dve", lambda e: e.scalar_tensor_tensor(out=nmr_s[:], in0=mean_s[:], scalar=-1.0, in1=rstd_s[:],
                                                           op0=ALU.mult, op1=ALU.mult), r=[LNB], w=[LNB])

            def ln_b(par, m, gname, bname, final):
                pp = m % 2
                nt = ntost[:, pp * 512:(pp + 1) * 512]
                op("dve", lambda e: e.tensor_tensor(out=nt, in0=resT[par][:, m, :], in1=rstd_s[:], op=ALU.mult),
                   r=[RES[par][m], LNB], w=[NT[pp]])
                op("dve", lambda e: e.tensor_tensor(out=nt, in0=nt, in1=nmr_s[:], op=ALU.add), r=[NT[pp], LNB], w=[NT[pp]])
                if final:
                    op("act", lambda e: e.activation(out=resT[par][:, m, :], in_=nt, func=AF.Identity,
                                                     bias=pcol(bname, m), scale=pcol(gname, m)),
                       r=[NT[pp], CONSTS], w=[RES[par][m]])
                else:
                    op("act", lambda e: e.activation(out=resT[par][:, m, :], in_=nt, func=AF.Identity,
                                                     bias=pcol(bname, m, True), scale=pcol(gname, m, True)),
                       r=[NT[pp], CONSTS], w=[RES[par][m]])
                    op("act", lambda e: e.activation(out=hbT[par][:, m, :], in_=nt, func=AF.Identity,
                                                     bias=pcol(bname, m), scale=pcol(gname, m)),
                       r=[NT[pp], CONSTS], w=[HB[par][m]])

            def ln_full(par, gname, bname, final, fill_pe, fill_ev):
                ln_a1(par)
                if len(fill_pe) > 0:
                    fill_pe[0]()
                ln_a2(par)
                for q_ in range(4):
                    ln_b(par, 2 * q_, gname, bname, final)
                    ln_b(par, 2 * q_ + 1, gname, bname, final)
                    if q_ < len(fill_ev):
                        fill_ev[q_]()
                    if q_ + 1 < len(fill_pe):
                        fill_pe[q_ + 1]()

            def ffn(c):
                par = c % 2
                for g in range(4):
                    rp_ = g % 2
                    if c + 1 < NCH:
                        if g == 0:
                            a1_s(c + 1, 0)
                            a1_s(c + 1, 1)
                        else:
                            a1_t(c + 1, g - 1)
                            if g + 1 < 4:
                                a1_s(c + 1, g + 1)
                    for hf in range(2):
                        rg, RG = ring_next("up")
                        wv = rg[:, :].rearrange("p (k c) -> p k c", k=8)
                        for m4 in range(4):
                            mm = 4 * hf + m4
                            b = gbank()
                            for kt in range(8):
                                op("pe", lambda e: e.matmul(psf(b), wv[:, kt, m4 * 128:(m4 + 1) * 128], hbT[par][:, kt, :],
                                                            start=(kt == 0), stop=(kt == 7)), r=[RG, HB[par][kt]], w=[PS[b]])
                            rq = mm % 2
                            op("act", lambda e: e.activation(out=rtmp[rq][:], in_=psf(b), func=AF.Relu), r=[PS[b]], w=[RT[rq]])
                            op("dve", lambda e: e.tensor_tensor(out=reluT[rp_][:, mm, :], in0=rtmp[rq][:], in1=rtmp[rq][:], op=ALU.mult),
                               r=[RT[rq]], w=[RL[rp_][mm]])
                    for hf in range(2):
                        rg, RG = ring_next("dn")
                        wv = rg[:, :].rearrange("p (k c) -> p k c", k=8)
                        for m4 in range(4):
                            m = 4 * hf + m4
                            b = gbank()
                            for mm in range(8):
                                op("pe", lambda e: e.matmul(psf(b), wv[:, mm, m4 * 128:(m4 + 1) * 128], reluT[rp_][:, mm, :],
                                                            start=(mm == 0), stop=(mm == 7)), r=[RG, RL[rp_][mm]], w=[PS[b]])
                            acc_res(par, m, b)
                if c + 1 < NCH:
                    a1_t(c + 1, 3)
                    a1_fin(c + 1)

            def ple_prep(c):
                bpt = gbank()
                for i in range(4):
                    ti = 4 * c + i
                    dma("sp", pst[:], din["p"][ti * 128:(ti + 1) * 128, :], s_p, w=[PST])
                    op("dve", lambda e: e.tensor_copy(out=pbf[:], in_=pst[:]), r=[PST], w=[PBF])
                    for kt in range(2):
                        op("pe", lambda e: e.transpose(psb(bpt)[:, kt * CH + i * 128:kt * CH + (i + 1) * 128],
                                                       pbf[:, kt * 128:(kt + 1) * 128], ident_b[:]),
                           r=[PBF, CONSTS], w=[PS[bpt]])
                op("dve", lambda e: e.tensor_copy(out=pT[:].rearrange("p k c -> p (k c)"), in_=psb(bpt)), r=[PS[bpt]], w=[PTT])

            def ple(c):
                par = c % 2
                rg2 = RG2 = None
                for hf in range(2):
                    rg, RG = ring_next("pg", live_prev=(1 if hf == 1 else 0))
                    wv = rg[:, :].rearrange("p (k c) -> p k c", k=8)
                    if hf == 0:
                        rg2, RG2 = ring_next("ple", live_prev=1)
                        wv2 = rg2[:, 0:2048].rearrange("p (k c) -> p k c", k=2)
                    for m4 in range(4):
                        m = 4 * hf + m4
                        bg_, bp_ = gbank(), gbank()
                        for kt in range(8):
                            op("pe", lambda e: e.matmul(psf(bg_), wv[:, kt, m4 * 128:(m4 + 1) * 128], hbT[par][:, kt, :],
                                                        start=(kt == 0), stop=(kt == 7)), r=[RG, HB[par][kt]], w=[PS[bg_]])
                        for kt in range(2):
                            op("pe", lambda e: e.matmul(psf(bp_), wv2[:, kt, m * 128:(m + 1) * 128], pT[:, kt, :],
                                                        start=(kt == 0), stop=(kt == 1)), r=[RG2, PTT], w=[PS[bp_]])
                        op("act", lambda e: e.activation(out=t1[:], in_=psf(bg_), func=AF.Sigmoid, bias=pcol("b_ple_gate", m), scale=1.0),
                           r=[PS[bg_], CONSTS], w=[T1])
                        op("dve", lambda e: e.tensor_tensor(out=t2[:], in0=psf(bp_), in1=t1[:], op=ALU.mult), r=[PS[bp_], T1], w=[T2])
                        op("dve", lambda e: e.tensor_tensor(out=resT[par][:, m, :], in0=t2[:], in1=resT[par][:, m, :], op=ALU.add),
                           r=[T2, RES[par][m]], w=[RES[par][m]])

            out_state = {}

            def out_pe(c, i):
                par = c % 2
                b0, b1 = gbank(), gbank()
                for ft in range(8):
                    bnk = b0 if ft < 4 else b1
                    op("pe", lambda e: e.transpose(ps_t[:, bnk, (ft % 4) * 128:(ft % 4 + 1) * 128],
                                                   resT[par][:, ft, i * 128:(i + 1) * 128], ident_f[:]),
                       r=[RES[par][ft], CONSTS], w=[PS[bnk]])
                out_state[(c, i)] = (b0, b1)

            def out_ev(c, i):
                b0, b1 = out_state.pop((c, i))
                ti = 4 * c + i
                op("act", lambda e: e.activation(out=ntost[:, 0:512], in_=psf(b0), func=AF.Copy), r=[PS[b0]], w=[NT[0]])
                op("dve", lambda e: e.tensor_copy(out=ntost[:, 512:1024], in_=psf(b1)), r=[PS[b1]], w=[NT[1]])
                dma("pool", out_d[ti * 128:(ti + 1) * 128, :], ntost[:], s_out, r=[NT[0], NT[1]])

            for i in range(4):
                a1_s(0, i) if i < 2 else None
            a1_t(0, 0)
            a1_s(0, 2)
            a1_t(0, 1)
            a1_s(0, 3)
            a1_t(0, 2)
            a1_t(0, 3)
            a1_fin(0)
            for m in range(8):
                a2_pe(0, m)
                a2_ev(0, m)
            for c in range(NCH):
                par = c % 2
                nxt = c + 1 < NCH
                b3(c)
                if stop_after == "B2" and c == 0:
                    pass
                if c > 0:
                    ln_full(par, "ln1_g", "ln1_b", False,
                            [(lambda i=i: out_pe(c - 1, i)) for i in range(4)],
                            [(lambda i=i: out_ev(c - 1, i)) for i in range(4)])
                else:
                    ln_full(par, "ln1_g", "ln1_b", False, [], [])
                ple_prep(c)
                ffn(c)
                if nxt:
                    ln_full(par, "ln2_g", "ln2_b", False,
                            [(lambda m=m: a2_pe(c + 1, m)) for m in range(4)],
                            [(lambda m=m: a2_ev(c + 1, m)) for m in range(4)])
                else:
                    ln_full(par, "ln2_g", "ln2_b", False, [], [])
                ple(c)
                if nxt:
                    ln_full(par, "ln3_g", "ln3_b", True,
                            [(lambda m=m: a2_pe(c + 1, m)) for m in range(4, 8)],
                            [(lambda m=m: a2_ev(c + 1, m)) for m in range(4, 8)])
                else:
                    ln_full(par, "ln3_g", "ln3_b", True, [], [])
            for i in range(4):
                out_pe(NCH - 1, i)
                out_ev(NCH - 1, i)
            kb.barrier()
        fb.close()
        return finish(nc, kb, out_d, dbg_outs, s_dbg)


def finish(nc, kb, out_d, dbg_outs, s_dbg):
    kb.barrier()
    nc._dbg_outs = dbg_outs
    nc._kb = kb
    return nc


def make_in_maps(inputs, cores):
    consts = host_consts()
    maps = []
    for b in cores:
        m = {"x": np.ascontiguousarray(inputs["x"][b]), "p": np.ascontiguousarray(inputs["p"][0, b]),
             "positions": np.ascontiguousarray(inputs["positions"][b]).reshape(1, S).astype(np.int32)}
        for k in W_SHAPES:
            a = np.asarray(inputs[k])
            if k not in ("ln_in_g", "ln_in_b"):
                a = a[0]
            m[k] = np.ascontiguousarray(a, dtype=np.float32)
        m.update(consts)
        m["c_pcols"] = pack_pcols(m)
        maps.append(m)
    return maps


def kernel(**inputs):
    nc = build_program()
    in_maps = make_in_maps(inputs, list(range(8)))
    res = run_bass_kernel_spmd(nc, in_maps, core_ids=list(range(8)))
    out = np.stack([np.asarray(r["out"]) for r in res.results], axis=0)
    return out.astype(np.float32)
```
